# Optimizing a Trainium2 kernel written in Bass

```python
import math
import jax, jax.numpy as jnp
from jax import lax
import numpy as np

D_MODEL = 4096
BATCH = 1
SEQ = 8192
DEPTH = 1

MIX_WIDTH = D_MODEL
CONV_WIDTH_CH = MIX_WIDTH // 2
ATTN_HEADS = 16
HEAD_DIM = 128
ATTN_WIDTH = ATTN_HEADS * HEAD_DIM
CONV_KERNEL = 31
DILATED_PATTERNS = ((128, 1), (512, 4), (2048, 16))
ATTN_BLOCK = 128
REL_BUCKETS = 32
REL_MAX_DISTANCE = 1024
NORM_EPS = 1e-6
LN_EPS = 1e-5
NEG_INF = -1e30
IN_SPLITS = (CONV_WIDTH_CH, CONV_WIDTH_CH, CONV_WIDTH_CH,
             ATTN_WIDTH, ATTN_WIDTH, ATTN_WIDTH, ATTN_WIDTH)
IN_WIDTH = sum(IN_SPLITS)

kernel_name = "hybrid_conformer_conv_dilated_window_attn_encoder"


def rms_norm(x, g, eps=NORM_EPS):
    xf = x.astype(jnp.float32)
    y = xf * lax.rsqrt(jnp.mean(xf * xf, axis=-1, keepdims=True) + eps) * g.astype(jnp.float32)
    return y.astype(x.dtype)


def t5_relative_bucket(rel):
    half = REL_BUCKETS // 2
    exact = half // 2
    n = jnp.abs(rel)
    nf = jnp.maximum(n, 1).astype(jnp.float32)
    large = exact + (jnp.log(nf / exact) / math.log(REL_MAX_DISTANCE / exact)
                     * (half - exact)).astype(jnp.int32)
    large = jnp.minimum(large, half - 1)
    return jnp.where(rel > 0, half, 0) + jnp.where(n < exact, n, large)


def dilated_window_attention(q, k, v, rel_bias, dilation, radius):
    B, S, H, Dh = q.shape
    L = S // dilation
    N = B * dilation

    def to_sub(t):
        return t.reshape(B, L, dilation, H, Dh).transpose(0, 2, 1, 3, 4).reshape(N, L, H, Dh)

    qs, ks, vs = to_sub(q), to_sub(k), to_sub(v)
    bq = min(ATTN_BLOCK, L)
    nb = -(-L // bq)
    Lp = nb * bq
    bk = bq + 2 * radius
    qs = jnp.pad(qs, ((0, 0), (0, Lp - L), (0, 0), (0, 0))).reshape(N, nb, bq, H, Dh)
    kpad = ((0, 0), (radius, radius + Lp - L), (0, 0), (0, 0))
    key_idx = (jnp.arange(nb) * bq)[:, None] + jnp.arange(bk)[None, :]
    kb = jnp.pad(ks, kpad)[:, key_idx]
    vb = jnp.pad(vs, kpad)[:, key_idx]

    rel = jnp.arange(bk)[None, :] - radius - jnp.arange(bq)[:, None]
    bias = rel_bias[t5_relative_bucket(rel * dilation)]
    bias = bias.transpose(2, 0, 1).astype(jnp.float32)
    key_pos = key_idx - radius
    valid = (jnp.abs(rel) <= radius)[None] & ((key_pos >= 0) & (key_pos < L))[:, None, :]

    s = jnp.einsum('nbqhd,nbkhd->nbhqk', qs.astype(jnp.float32), kb.astype(jnp.float32))
    s = s * (Dh ** -0.5) + bias
    s = jnp.where(valid[None, :, None], s, NEG_INF)
    lse = jax.nn.logsumexp(s, axis=-1)
    p = jnp.exp(s - lse[..., None])
    o = jnp.einsum('nbhqk,nbkhd->nbqhd', p, vb.astype(jnp.float32))
    o = o.reshape(N, Lp, H, Dh)[:, :L]
    lse = lse.transpose(0, 1, 3, 2).reshape(N, Lp, H)[:, :L]
    o = o.reshape(B, dilation, L, H, Dh).transpose(0, 2, 1, 3, 4).reshape(B, S, H, Dh)
    lse = lse.reshape(B, dilation, L, H).transpose(0, 2, 1, 3).reshape(B, S, H)
    return o, lse


def longnet_mixture(q, k, v, rel_bias):
    outs, lses = [], []
    for window, dilation in DILATED_PATTERNS:
        o, l = dilated_window_attention(q, k, v, rel_bias, dilation, window // (2 * dilation))
        outs.append(o)
        lses.append(l)
    w = jax.nn.softmax(jnp.stack(lses, axis=0), axis=0)
    return jnp.sum(w[..., None] * jnp.stack(outs, axis=0), axis=0)


def conformer_conv(u, glu_gate, w_dw, b_dw, ln_g, ln_b):
    a = u * jax.nn.sigmoid(glu_gate)
    c = lax.conv_general_dilated(
        a, w_dw[:, None, :].astype(a.dtype), window_strides=(1,),
        padding=[(CONV_KERNEL // 2, CONV_KERNEL // 2)],
        dimension_numbers=('NWC', 'WIO', 'NWC'), feature_group_count=a.shape[-1]) + b_dw
    cf = c.astype(jnp.float32)
    mu = jnp.mean(cf, axis=-1, keepdims=True)
    var = jnp.mean(jnp.square(cf - mu), axis=-1, keepdims=True)
    cn = (cf - mu) * lax.rsqrt(var + LN_EPS) * ln_g.astype(jnp.float32) + ln_b.astype(jnp.float32)
    return jax.nn.silu(cn).astype(u.dtype)


def setup_inputs(seed: int = 0) -> dict:
    key = jax.random.key(seed)
    ks = jax.random.split(key, 12)
    f32 = jnp.float32
    x = jax.random.normal(ks[0], (BATCH, SEQ, D_MODEL), f32)
    norm_g = 1.0 + 0.02 * jax.random.normal(ks[1], (DEPTH, D_MODEL), f32)
    w_in = jax.random.normal(ks[2], (DEPTH, D_MODEL, IN_WIDTH), f32) * D_MODEL ** -0.5
    q_norm_g = 1.0 + 0.02 * jax.random.normal(ks[3], (DEPTH, HEAD_DIM), f32)
    k_norm_g = 1.0 + 0.02 * jax.random.normal(ks[4], (DEPTH, HEAD_DIM), f32)
    rel_bias = 0.5 * jax.random.normal(ks[5], (REL_BUCKETS, ATTN_HEADS), f32)
    conv_w = jax.random.normal(ks[6], (DEPTH, CONV_KERNEL, CONV_WIDTH_CH), f32) * CONV_KERNEL ** -0.5
    conv_b = 0.02 * jax.random.normal(ks[7], (DEPTH, CONV_WIDTH_CH), f32)
    conv_ln_g = 1.0 + 0.02 * jax.random.normal(ks[8], (DEPTH, CONV_WIDTH_CH), f32)
    conv_ln_b = 0.02 * jax.random.normal(ks[9], (DEPTH, CONV_WIDTH_CH), f32)
    w_out = jax.random.normal(ks[10], (DEPTH, MIX_WIDTH, D_MODEL), f32) * MIX_WIDTH ** -0.5
    return {"x": x, "norm_g": norm_g, "w_in": w_in, "q_norm_g": q_norm_g, "k_norm_g": k_norm_g,
            "rel_bias": rel_bias, "conv_w": conv_w, "conv_b": conv_b, "conv_ln_g": conv_ln_g,
            "conv_ln_b": conv_ln_b, "w_out": w_out}


def reference(x, norm_g, w_in, q_norm_g, k_norm_g, rel_bias, conv_w, conv_b, conv_ln_g,
              conv_ln_b, w_out):
    B, S, _ = x.shape
    split_pts = list(np.cumsum(IN_SPLITS)[:-1])
    for l in range(DEPTH):
        xn = rms_norm(x, norm_g[l])
        h = jnp.einsum('bsd,de->bse', xn, w_in[l])
        c_val, c_glu, c_gate, q, k, v, a_gate = jnp.split(h, split_pts, axis=-1)

        conv_out = conformer_conv(c_val, c_glu, conv_w[l], conv_b[l], conv_ln_g[l], conv_ln_b[l])
        conv_out = conv_out * jax.nn.silu(c_gate)

        q = rms_norm(q.reshape(B, S, ATTN_HEADS, HEAD_DIM), q_norm_g[l])
        k = rms_norm(k.reshape(B, S, ATTN_HEADS, HEAD_DIM), k_norm_g[l])
        v = v.reshape(B, S, ATTN_HEADS, HEAD_DIM)
        attn = longnet_mixture(q, k, v, rel_bias).astype(x.dtype).reshape(B, S, ATTN_WIDTH)
        attn_out = attn * jax.nn.silu(a_gate)

        mixed = jnp.concatenate([conv_out, attn_out], axis=-1)
        x = x + jnp.einsum('bse,ed->bsd', mixed, w_out[l])
    return x
```

```python
import numpy as np
import ml_dtypes
from contextlib import ExitStack

import concourse.bass as bass
import concourse.mybir as mybir
from concourse.bass_utils import run_bass_kernel_spmd

F32 = mybir.dt.float32
BF16 = mybir.dt.bfloat16
AF = mybir.ActivationFunctionType
ALU = mybir.AluOpType

D = 4096
S = 8192
NCORE = 8
TOK = 1024
WIN = 3072
NCH = 32
INW = 14336
NHEAD = 16
NBLK = 53
NP = 640
P_G, P_QG, P_KG, P_CW, P_CB, P_LG, P_LB, P_KV = 0, 32, 33, 34, 530, 546, 562, 578
NORM_EPS = 1e-6
LN_EPS = 1e-5
PATTERNS = ((128, 1), (512, 4), (2048, 16))


class Sem:
    def __init__(s, h):
        s.h = h
        s.v = 0


class Res:
    def __init__(s):
        s.w = None
        s.r = {}


class Eng:
    def __init__(s, nc, h, name, es):
        s.h = h
        s.sem = Sem(es.enter_context(nc.semaphore("e_" + name)))
        s.seen = {}

    def wait(s, ev):
        if ev is None:
            return
        sem, v = ev
        if v <= 0 or s.seen.get(sem, 0) >= v:
            return
        s.h.wait_ge(sem.h, v)
        s.seen[sem] = v

    def pre(s, reads=(), writes=()):
        for r in reads:
            s.wait(r.w)
        for w in writes:
            if w.w is not None and w.w[0] is not s.sem:
                s.wait(w.w)
            for sem, v in list(w.r.items()):
                if sem is not s.sem:
                    s.wait((sem, v))

    def post(s, ins, reads=(), writes=()):
        s.sem.v += 1
        ins.then_inc(s.sem.h, 1)
        ev = (s.sem, s.sem.v)
        for r in reads:
            r.r[ev[0]] = ev[1]
        for w in writes:
            w.w = ev
            w.r = {}
        return ev

    def op(s, fn, reads=(), writes=()):
        s.pre(reads, writes)
        return s.post(fn(), reads, writes)


class DmaPool:
    def __init__(s, nc, es, name, k):
        s.sems = [Sem(es.enter_context(nc.semaphore(f"d_{name}{i}"))) for i in range(k)]
        s.last = [None] * k
        s.i = 0

    def dma(s, q, out, in_, reads=(), writes=()):
        k = s.i % len(s.sems)
        s.i += 1
        sem = s.sems[k]
        q.wait(s.last[k])
        q.pre(reads, writes)
        ins = q.h.dma_start(out=out, in_=in_)
        sem.v += 16
        ins.then_inc(sem.h, 16)
        ev = (sem, sem.v)
        s.last[k] = ev
        for r in reads:
            r.r[sem] = sem.v
        for w in writes:
            w.w = ev
            w.r = {}
        return ev

    def events(s):
        return [e for e in s.last if e is not None]


def attn_blocks():
    blks = []
    for c in range(9):
        tlo, thi = max(0, 128 * (c - 1)), min(TOK, 128 * (c + 1))
        blks.append(dict(p=0, nk=128, kslice=slice(960 + 128 * c, 960 + 128 * c + 128, 1),
                         qslice=slice(tlo, thi, 1), q0=tlo - 128 * (c - 1), nq=thi - tlo,
                         v=("v1", c)))
    for r in range(4):
        for c in range(3):
            llo, lhi = max(256, 128 + 128 * c), min(512, 384 + 128 * c)
            k0 = 4 * (192 + 128 * c) + r
            blks.append(dict(p=1, nk=128, kslice=slice(k0, k0 + 512, 4),
                             qslice=slice(4 * (llo - 256) + r, 4 * (lhi - 256), 4),
                             q0=llo - (128 + 128 * c), nq=lhi - llo, v=("v2", r, c)))
    for r in range(16):
        blks.append(dict(p=2, nk=128, kslice=slice(r, 2048, 16), qslice=slice(r, TOK, 16),
                         q0=128, nq=64, v=("v3a", r)))
    for r in range(16):
        blks.append(dict(p=2, nk=64, kslice=slice(2048 + r, WIN, 16), qslice=slice(r, TOK, 16),
                         q0=0, nq=64, v=("v3b", r)))
    assert len(blks) == NBLK
    return blks


def key_window_index(b, kk):
    s = b["kslice"]
    return s.start + s.step * kk


def build_program(debug=False):
    nc = bass.Bass("TRN2", target_bir_lowering=False)
    dkind = "ExternalOutput" if debug else "Internal"
    xw = nc.dram_tensor("xw", [WIN, D], F32, kind="ExternalInput").ap()
    w_in = nc.dram_tensor("w_in", [D, INW], F32, kind="ExternalInput").ap()
    w_out = nc.dram_tensor("w_out", [D, D], F32, kind="ExternalInput").ap()
    params = nc.dram_tensor("params", [128, NP], F32, kind="ExternalInput").ap()
    biasT = nc.dram_tensor("biasT", [128, 48, 256], F32, kind="ExternalInput").ap()
    maskd = nc.dram_tensor("mask", [128, 256], F32, kind="ExternalInput").ap()
    y = nc.dram_tensor("y", [TOK, D], F32, kind="ExternalOutput").ap()
    kT_win = nc.dram_tensor("kT_win", [NHEAD, 128, WIN], BF16, kind=dkind).ap()
    v_win = nc.dram_tensor("v_win", [WIN, 2048], BF16, kind=dkind).ap()
    qT_d = nc.dram_tensor("qT_d", [NHEAD, 128, TOK], BF16, kind=dkind).ap()
    gT_d = nc.dram_tensor("gT_d", [NHEAD, 128, TOK], BF16, kind=dkind).ap()
    if debug:
        mix_d = nc.dram_tensor("mix_d", [32, 128, TOK], BF16, kind="ExternalOutput").ap()

    w_in_v = w_in.rearrange("(c p) n -> p c n", p=128)
    w_out_v = w_out.rearrange("(c p) n -> p c n", p=128)
    blks = attn_blocks()

    with ExitStack() as es:
        PE = Eng(nc, nc.tensor, "pe", es)
        ACT = Eng(nc, nc.scalar, "act", es)
        DVE = Eng(nc, nc.vector, "dve", es)
        POOL = Eng(nc, nc.gpsimd, "pool", es)
        SP = Eng(nc, nc.sync, "sp", es)
        engines = [PE, ACT, DVE, POOL, SP]
        d_w = DmaPool(nc, es, "w", 4)
        d_x = DmaPool(nc, es, "x", 4)
        d_s = DmaPool(nc, es, "s", 8)
        d_l = DmaPool(nc, es, "l", 8)
        d_o = DmaPool(nc, es, "o", 8)
        pools = [d_w, d_x, d_s, d_l, d_o]

        def barrier():
            evs = [(e.sem, e.sem.v) for e in engines]
            for p in pools:
                evs += p.events()
            for e in engines:
                for ev in evs:
                    if ev[0] is not e.sem:
                        e.wait(ev)

        uid = [0]

        def T(stack, name, shape, dt):
            uid[0] += 1
            return stack.enter_context(nc.sbuf_tensor(f"{name}_{uid[0]}", shape, dt))

        prm = T(es, "prm", [128, NP], F32)
        ident = T(es, "ident", [128, 128], BF16)
        ones1 = T(es, "ones1", [128, 128], BF16)
        onesq = T(es, "onesq", [128, 128], BF16)
        onesl = T(es, "onesl", [128, 128], BF16)
        mixC = T(es, "mixC", [128, 16, TOK], BF16)
        r_prm, r_const, r_mixC = Res(), Res(), Res()
        psb = [es.enter_context(nc.psum_tensor(f"psb{i}", [128, 512], F32)) for i in range(8)]
        r_ps = [Res() for _ in range(8)]

        d_s.dma(SP, prm[:], params[:, :], writes=[r_prm])
        POOL.op(lambda: nc.gpsimd.memset(ident[:], 0.0), writes=[r_const])
        POOL.op(lambda: nc.gpsimd.affine_select(out=ident[:], in_=ident[:], pattern=[[-1, 128]],
                                                compare_op=ALU.not_equal, fill=1.0, base=0,
                                                channel_multiplier=1), reads=[r_const], writes=[r_const])
        POOL.op(lambda: nc.gpsimd.memset(ones1[:], 1.0), writes=[r_const])
        POOL.op(lambda: nc.gpsimd.memset(onesq[:], 1.0 / 128), writes=[r_const])
        POOL.op(lambda: nc.gpsimd.memset(onesl[:], 1.0 / 2048), writes=[r_const])

        passes = [("L", 0), ("O", 1024), ("R", 2048)]
        for pname, w0 in passes:
            own = pname == "O"
            with ExitStack() as ps_:
                xnT = T(ps_, "xnT", [128, NCH, 1056], BF16)
                r_xnA, r_xnD = Res(), Res()
                wr = T(ps_, "wr", [128, 4, NCH, 128], BF16)
                r_w = [Res() for _ in range(4)]

                if own:
                    cols = []
                    for cb in range(16):
                        cols += [("glu", cb, 16 + cb), ("val", cb, cb)]
                    cols += [("gate", cb, 32 + cb) for cb in range(16)]
                    cols += [("ag", h, 96 + h) for h in range(16)]
                    cols += [("v", h, 80 + h) for h in range(16)]
                    cols += [("q", h, 48 + h) for h in range(16)]
                    cols += [("k", h, 64 + h) for h in range(16)]
                else:
                    cols = [("v", h, 80 + h) for h in range(16)] + [("k", h, 64 + h) for h in range(16)]

                def load_w(i):
                    if i < len(cols):
                        j = cols[i][2]
                        d_w.dma(POOL, wr[:, i % 4], w_in_v[:, :, 128 * j:128 * j + 128],
                                writes=[r_w[i % 4]])

                for i in range(3):
                    load_w(i)

                with ExitStack() as ns_:
                    xs = [T(ns_, f"xs{i}", [128, D], F32) for i in range(2)]
                    xb = [T(ns_, f"xb{i}", [128, D], BF16) for i in range(2)]
                    st = [T(ns_, f"st{i}", [128, 4], F32) for i in range(2)]
                    r_xs = [Res(), Res()]
                    r_xb = [Res(), Res()]
                    r_st = [Res(), Res()]
                    r_junk = Res()
                    tiles = [(w0 + 128 * i, 128, 128 * i) for i in range(8)]
                    if own:
                        tiles.append((None, 32, 1024))
                    bank_i = 0
                    for ti, (row, npart, col0) in enumerate(tiles):
                        sl = ti % 2
                        if row is not None:
                            d_x.dma(SP, xs[sl][:], xw[row:row + 128, :], writes=[r_xs[sl]])
                        else:
                            d_x.dma(SP, xs[sl][0:16, :], xw[1008:1024, :], writes=[r_xs[sl]])
                            d_x.dma(SP, xs[sl][16:32, :], xw[2048:2064, :], writes=[])
                            ev2 = d_x.last[(d_x.i - 1) % 4]
                            ACT.wait(ev2)
                            DVE.wait(ev2)
                        ACT.op(lambda sl=sl, n=npart: nc.scalar.activation(
                            out=xb[sl][0:n, :], in_=xs[sl][0:n, :], func=AF.Square,
                            accum_out=st[sl][0:n, 0:1]), reads=[r_xs[sl]], writes=[r_xb[sl], r_st[sl]])
                        ACT.op(lambda sl=sl, n=npart: nc.scalar.activation(
                            out=st[sl][0:n, 1:2], in_=st[sl][0:n, 0:1], func=AF.Sqrt,
                            scale=1.0 / D, bias=NORM_EPS), reads=[r_st[sl]], writes=[r_st[sl]])
                        DVE.op(lambda sl=sl, n=npart: nc.vector.reciprocal(
                            out=st[sl][0:n, 2:3], in_=st[sl][0:n, 1:2]), reads=[r_st[sl]], writes=[r_st[sl]])
                        DVE.op(lambda sl=sl, n=npart: nc.vector.tensor_scalar(
                            out=xb[sl][0:n, :], in0=xs[sl][0:n, :], scalar1=st[sl][0:n, 2:3],
                            scalar2=None, op0=ALU.mult), reads=[r_xs[sl], r_st[sl]], writes=[r_xb[sl]])
                        for grp in range(4):
                            bk = bank_i % 4
                            bank_i += 1
                            pt = psb[bk][:].bitcast(BF16)

                            def tr(sl=sl, n=npart, grp=grp, pt=pt):
                                ins = None
                                for k in range(8):
                                    c = grp * 8 + k
                                    ins = nc.tensor.transpose(pt[:, k * 128:k * 128 + n],
                                                              xb[sl][0:n, c * 128:(c + 1) * 128],
                                                              ident[0:n, 0:n])
                                return ins
                            PE.op(tr, reads=[r_xb[sl], r_const], writes=[r_ps[bk]])
                            useA = (grp % 2 == 0)
                            E = ACT if useA else DVE
                            rx = r_xnA if useA else r_xnD
                            for k in range(8):
                                c = grp * 8 + k
                                if useA:
                                    fn = (lambda c=c, k=k, n=npart, pt=pt, col0=col0: nc.scalar.mul(
                                        out=xnT[:, c, col0:col0 + n], in_=pt[:, k * 128:k * 128 + n],
                                        mul=prm[:, P_G + c:P_G + c + 1]))
                                else:
                                    fn = (lambda c=c, k=k, n=npart, pt=pt, col0=col0: nc.vector.tensor_scalar(
                                        out=xnT[:, c, col0:col0 + n], in0=pt[:, k * 128:k * 128 + n],
                                        scalar1=prm[:, P_G + c:P_G + c + 1], scalar2=None, op0=ALU.mult))
                                E.op(fn, reads=[r_prm], writes=[r_ps[bk], rx])
                    barrier()

                ntok_groups = [(0, 512), (512, 1024)]
                main_i = [0]
                main_nb = [3]

                def main_mm(i, halo_cols=None):
                    slot = i % 4
                    out = []
                    for (c0, c1) in ntok_groups:
                        bk = main_i[0] % main_nb[0]
                        main_i[0] += 1

                        def mm(bk=bk, c0=c0, c1=c1, slot=slot):
                            ins = None
                            for c in range(NCH):
                                ins = nc.tensor.matmul(psb[bk][:, 0:c1 - c0], lhsT=wr[:, slot, c, :],
                                                       rhs=xnT[:, c, c0:c1], start=(c == 0), stop=(c == NCH - 1))
                            return ins
                        PE.op(mm, reads=[r_w[slot], r_xnA, r_xnD], writes=[r_ps[bk]])
                        out.append(bk)
                    if halo_cols is not None:
                        def mmh(slot=slot, hc=halo_cols):
                            ins = None
                            for c in range(NCH):
                                ins = nc.tensor.matmul(psb[3][:, hc:hc + 32], lhsT=wr[:, slot, c, :],
                                                       rhs=xnT[:, c, 1024:1056], start=(c == 0), stop=(c == NCH - 1))
                            return ins
                        PE.op(mmh, reads=[r_w[slot], r_xnA, r_xnD], writes=[r_ps[3]])
                    return out

                if own:
                    with ExitStack() as cs_:
                        sig = T(cs_, "sig", [128, 1056], F32)
                        aext = T(cs_, "aext", [128, 1056], F32)
                        accA = T(cs_, "accA", [128, TOK], F32)
                        accB = T(cs_, "accB", [128, TOK], F32)
                        csq = T(cs_, "csq", [128, TOK], BF16)
                        musb = T(cs_, "musb", [128, TOK], F32)
                        rsb = T(cs_, "rsb", [128, TOK], F32)
                        tmp1 = T(cs_, "tmp1", [128, TOK], F32)
                        tmp2 = T(cs_, "tmp2", [128, TOK], F32)
                        gsb = T(cs_, "gsb", [128, TOK], BF16)
                        r_sig, r_aext, r_accA, r_accB, r_csq = Res(), Res(), Res(), Res(), Res()
                        r_mu, r_rs, r_t1, r_t2, r_gs = Res(), Res(), Res(), Res(), Res()
                        for i in range(32):
                            kind, cb, j = cols[i]
                            load_w(i + 3)
                            banks = main_mm(i, halo_cols=(0 if kind == "glu" else 32))
                            if kind == "glu":
                                for hf, bk in enumerate(banks):
                                    ACT.op(lambda bk=bk, hf=hf: nc.scalar.activation(
                                        out=sig[:, 16 + 512 * hf:16 + 512 * hf + 512], in_=psb[bk][:, :],
                                        func=AF.Sigmoid), writes=[r_ps[bk], r_sig])
                                ACT.op(lambda: nc.scalar.activation(out=sig[:, 0:16], in_=psb[3][:, 0:16],
                                                                    func=AF.Sigmoid), writes=[r_ps[3], r_sig])
                                ACT.op(lambda: nc.scalar.activation(out=sig[:, 1040:1056], in_=psb[3][:, 16:32],
                                                                    func=AF.Sigmoid), writes=[r_ps[3], r_sig])
                            else:
                                for hf, bk in enumerate(banks):
                                    DVE.op(lambda bk=bk, hf=hf: nc.vector.tensor_tensor(
                                        out=aext[:, 16 + 512 * hf:16 + 512 * hf + 512], in0=psb[bk][:, :],
                                        in1=sig[:, 16 + 512 * hf:16 + 512 * hf + 512], op=ALU.mult),
                                        reads=[r_sig], writes=[r_ps[bk], r_aext])
                                DVE.op(lambda: nc.vector.tensor_tensor(out=aext[:, 0:16], in0=psb[3][:, 32:48],
                                                                       in1=sig[:, 0:16], op=ALU.mult),
                                       reads=[r_sig], writes=[r_ps[3], r_aext])
                                DVE.op(lambda: nc.vector.tensor_tensor(out=aext[:, 1040:1056], in0=psb[3][:, 48:64],
                                                                       in1=sig[:, 1040:1056], op=ALU.mult),
                                       reads=[r_sig], writes=[r_ps[3], r_aext])
                                cwc = P_CW + cb * 31
                                DVE.op(lambda cwc=cwc, cb=cb: nc.vector.tensor_scalar(
                                    out=accA[:], in0=aext[:, 1:1 + TOK], scalar1=prm[:, cwc:cwc + 1],
                                    scalar2=prm[:, P_CB + cb:P_CB + cb + 1], op0=ALU.mult, op1=ALU.add),
                                    reads=[r_aext, r_prm], writes=[r_accA])
                                DVE.op(lambda cwc=cwc: nc.vector.tensor_scalar(
                                    out=accB[:], in0=aext[:, 2:2 + TOK], scalar1=prm[:, cwc + 1:cwc + 2],
                                    scalar2=None, op0=ALU.mult), reads=[r_aext, r_prm], writes=[r_accB])
                                for jt in range(2, 31):
                                    acc, racc = (accA, r_accA) if jt % 2 == 0 else (accB, r_accB)
                                    DVE.op(lambda jt=jt, acc=acc, cwc=cwc: nc.vector.scalar_tensor_tensor(
                                        out=acc[:], in0=aext[:, 1 + jt:1 + jt + TOK],
                                        scalar=prm[:, cwc + jt:cwc + jt + 1], in1=acc[:],
                                        op0=ALU.mult, op1=ALU.add), reads=[r_aext, racc], writes=[racc])
                                DVE.op(lambda cb=cb: nc.vector.tensor_tensor(
                                    out=mixC[:, cb, :], in0=accA[:], in1=accB[:], op=ALU.add),
                                    reads=[r_accA, r_accB], writes=[r_mixC])
                                ACT.op(lambda cb=cb: nc.scalar.activation(out=csq[:], in_=mixC[:, cb, :],
                                                                          func=AF.Square),
                                       reads=[r_mixC], writes=[r_csq])
                                for hf in range(2):
                                    def stat(hf=hf, cb=cb):
                                        nc.tensor.matmul(psb[4 + hf][:, :], lhsT=onesl[:], rhs=mixC[:, cb, 512 * hf:512 * hf + 512],
                                                         start=(cb == 0), stop=(cb == 15), skip_group_check=True)
                                        return nc.tensor.matmul(psb[6 + hf][:, :], lhsT=onesl[:], rhs=csq[:, 512 * hf:512 * hf + 512],
                                                                start=(cb == 0), stop=(cb == 15), skip_group_check=True)
                                    PE.op(stat, reads=[r_mixC, r_csq, r_const], writes=[r_ps[4 + hf], r_ps[6 + hf]])
                        for hf in range(2):
                            hs = slice(512 * hf, 512 * hf + 512)
                            DVE.op(lambda hf=hf, hs=hs: nc.vector.tensor_copy(out=musb[:, hs], in_=psb[4 + hf][:, :]),
                                   writes=[r_ps[4 + hf], r_mu])
                            DVE.op(lambda hs=hs: nc.vector.tensor_tensor(out=tmp1[:, hs], in0=musb[:, hs], in1=musb[:, hs],
                                                                         op=ALU.mult), reads=[r_mu], writes=[r_t1])
                            DVE.op(lambda hf=hf, hs=hs: nc.vector.tensor_tensor(out=tmp2[:, hs], in0=psb[6 + hf][:, :],
                                                                                in1=tmp1[:, hs], op=ALU.subtract),
                                   reads=[r_t1], writes=[r_ps[6 + hf], r_t2])
                            ACT.op(lambda hs=hs: nc.scalar.activation(out=tmp1[:, hs], in_=tmp2[:, hs], func=AF.Sqrt,
                                                                      bias=LN_EPS), reads=[r_t2], writes=[r_t1])
                            DVE.op(lambda hs=hs: nc.vector.reciprocal(out=rsb[:, hs], in_=tmp1[:, hs]),
                                   reads=[r_t1], writes=[r_rs])
                        for i in range(32, 48):
                            kind, cb, j = cols[i]
                            load_w(i + 3)
                            banks = main_mm(i)
                            for hf, bk in enumerate(banks):
                                ACT.op(lambda bk=bk, hf=hf: nc.scalar.activation(
                                    out=gsb[:, 512 * hf:512 * hf + 512], in_=psb[bk][:, :], func=AF.Silu),
                                    writes=[r_ps[bk], r_gs])
                            DVE.op(lambda cb=cb: nc.vector.tensor_tensor(out=tmp1[:], in0=mixC[:, cb, :], in1=musb[:],
                                                                         op=ALU.subtract),
                                   reads=[r_mixC, r_mu], writes=[r_t1])
                            DVE.op(lambda: nc.vector.tensor_tensor(out=tmp2[:], in0=tmp1[:], in1=rsb[:], op=ALU.mult),
                                   reads=[r_t1, r_rs], writes=[r_t2])
                            ACT.op(lambda cb=cb: nc.scalar.activation(
                                out=tmp1[:], in_=tmp2[:], func=AF.Silu, scale=prm[:, P_LG + cb:P_LG + cb + 1],
                                bias=prm[:, P_LB + cb:P_LB + cb + 1]), reads=[r_t2, r_prm], writes=[r_t1])
                            DVE.op(lambda cb=cb: nc.vector.tensor_tensor(out=mixC[:, cb, :], in0=tmp1[:], in1=gsb[:],
                                                                         op=ALU.mult),
                                   reads=[r_t1, r_gs], writes=[r_mixC])
                        barrier()
                    first_rest = 48
                else:
                    first_rest = 0

                with ExitStack() as qs_:
                    sq = [T(qs_, f"sq{i}", [128, 512], BF16) for i in range(2)]
                    sd = [T(qs_, f"sd{i}", [128, 512], F32) for i in range(2)]
                    rs = [T(qs_, f"rs{i}", [128, 512], F32) for i in range(2)]
                    kn = [T(qs_, f"kn{i}", [128, TOK], BF16) for i in range(2)]
                    vT = T(qs_, "vT", [128, TOK], BF16)
                    vst = T(qs_, "vst", [128, 8, 512], BF16)
                    r_sq, r_sd, r_rs2 = [Res(), Res()], [Res(), Res()], [Res(), Res()]
                    r_kn = [Res(), Res()]
                    r_vT, r_vst = Res(), Res()
                    kn_i = [0]
                    pending = []
                    main_nb[0] = 4

                    def stage2(kind, h, banks):
                        if kind in ("q", "k"):
                            gcol = P_QG if kind == "q" else P_KG
                            b = kn_i[0] % 2
                            kn_i[0] += 1
                            for hf, bk in enumerate(banks):
                                PE.op(lambda hf=hf: nc.tensor.matmul(psb[4 + hf][:, :], lhsT=onesq[:], rhs=sq[hf][:],
                                                                     start=True, stop=True),
                                      reads=[r_sq[hf], r_const], writes=[r_ps[4 + hf]])
                                ACT.op(lambda hf=hf: nc.scalar.activation(out=sd[hf][:], in_=psb[4 + hf][:, :],
                                                                          func=AF.Sqrt, bias=NORM_EPS),
                                       writes=[r_ps[4 + hf], r_sd[hf]])
                                DVE.op(lambda hf=hf: nc.vector.reciprocal(out=rs[hf][:], in_=sd[hf][:]),
                                       reads=[r_sd[hf]], writes=[r_rs2[hf]])
                                DVE.op(lambda hf=hf, bk=bk, b=b, gcol=gcol: nc.vector.scalar_tensor_tensor(
                                    out=kn[b][:, 512 * hf:512 * hf + 512], in0=psb[bk][:, :],
                                    scalar=prm[:, gcol:gcol + 1], in1=rs[hf][:], op0=ALU.mult, op1=ALU.mult),
                                    reads=[r_rs2[hf], r_prm], writes=[r_ps[bk], r_kn[b]])
                            if kind == "k":
                                d_s.dma(SP, kT_win[h, :, w0:w0 + TOK], kn[b][:], reads=[r_kn[b]])
                            else:
                                d_s.dma(SP, qT_d[h, :, :], kn[b][:], reads=[r_kn[b]])
                        elif kind == "v":
                            pt = psb[6][:].bitcast(BF16)

                            def tr(pt=pt):
                                ins = None
                                for t in range(8):
                                    ins = nc.tensor.transpose(pt[:, t * 128:(t + 1) * 128], vT[:, t * 128:(t + 1) * 128],
                                                              ident[:])
                                return ins
                            PE.op(tr, reads=[r_vT, r_const], writes=[r_ps[6]])
                            hh = h % 4
                            DVE.op(lambda pt=pt, hh=hh: nc.vector.tensor_copy(
                                out=vst[:, :, hh * 128:(hh + 1) * 128],
                                in_=pt[:, 0:1024].rearrange("p (t d) -> p t d", d=128)),
                                writes=[r_ps[6], r_vst])
                            if hh == 3:
                                g = h // 4
                                d_s.dma(SP, v_win[w0:w0 + TOK, g * 512:(g + 1) * 512].rearrange("(t p) n -> p t n", p=128),
                                        vst[:], reads=[r_vst])

                    for i in range(first_rest, len(cols)):
                        kind, h, j = cols[i]
                        load_w(i + 3)
                        banks = main_mm(i)
                        for fn in pending:
                            fn()
                        pending = []
                        if kind in ("q", "k"):
                            for hf, bk in enumerate(banks):
                                ACT.op(lambda bk=bk, hf=hf: nc.scalar.activation(out=sq[hf][:], in_=psb[bk][:, :],
                                                                                 func=AF.Square),
                                       writes=[r_ps[bk], r_sq[hf]])
                        elif kind == "v":
                            for hf, bk in enumerate(banks):
                                ACT.op(lambda bk=bk, hf=hf: nc.scalar.activation(out=vT[:, 512 * hf:512 * hf + 512],
                                                                                 in_=psb[bk][:, :], func=AF.Copy),
                                       writes=[r_ps[bk], r_vT])
                        elif kind == "ag":
                            b = kn_i[0] % 2
                            kn_i[0] += 1
                            for hf, bk in enumerate(banks):
                                ACT.op(lambda bk=bk, hf=hf, b=b: nc.scalar.activation(
                                    out=kn[b][:, 512 * hf:512 * hf + 512], in_=psb[bk][:, :], func=AF.Silu),
                                    writes=[r_ps[bk], r_kn[b]])
                            d_s.dma(SP, gT_d[h, :, :], kn[b][:], reads=[r_kn[b]])
                        if kind in ("q", "k", "v"):
                            pending.append(lambda kind=kind, h=h, banks=banks: stage2(kind, h, banks))
                    for fn in pending:
                        fn()
                    barrier()

        mixA = T(es, "mixA", [128, 16, TOK], BF16)
        r_mixA = Res()
        with ExitStack() as as_:
            EB = T(as_, "EB", [128, 48, 256], BF16)
            msk = T(as_, "msk", [128, 256], F32)
            r_EB, r_msk = Res(), Res()
            d_l.dma(SP, msk[:], maskd[:, :], writes=[r_msk])
            with ExitStack() as bs_:
                bst = T(bs_, "bst", [128, 16, 256], F32)
                r_bst = Res()
                for p in range(3):
                    d_l.dma(SP, bst[:], biasT[:, 16 * p:16 * p + 16, :], writes=[r_bst])
                    ACT.op(lambda: nc.scalar.activation(out=bst[:], in_=bst[:], func=AF.Exp),
                           reads=[r_bst], writes=[r_bst])
                    for h in range(16):
                        DVE.op(lambda p=p, h=h: nc.vector.tensor_tensor(out=EB[:, 16 * p + h, :], in0=bst[:, h, :],
                                                                        in1=msk[:], op=ALU.mult),
                               reads=[r_bst, r_msk], writes=[r_EB])
                barrier()
            kTs = T(as_, "kTs", [128, 2, WIN], BF16)
            qs = T(as_, "qs", [128, 2, TOK], BF16)
            gs = T(as_, "gs", [128, 2, TOK], BF16)
            v1 = T(as_, "v1", [128, 9, 256], BF16)
            v2 = T(as_, "v2", [128, 4, 3, 256], BF16)
            v3a = T(as_, "v3a", [128, 16, 256], BF16)
            v3b = T(as_, "v3b", [64, 16, 256], BF16)
            esb = [T(as_, f"esb{i}", [128, 256], BF16) for i in range(2)]
            psb_ = [T(as_, f"pp{i}", [128, 256], BF16) for i in range(2)]
            rz = T(as_, "rz", [128, TOK], F32)
            at = T(as_, "at", [128, TOK], F32)
            r_kTs, r_qs, r_gs2, r_v = Res(), Res(), Res(), Res()
            r_esb, r_pp = [Res(), Res()], [Res(), Res()]
            r_rz, r_at = Res(), Res()
            NB0, ZB0, SB0 = 0, 2, 4
            scale = 1.0 / np.sqrt(128.0)
            blk_i = 0
            for hp in range(8):
                d_l.dma(SP, kTs[:], kT_win[2 * hp:2 * hp + 2, :, :].rearrange("h d w -> d h w"), writes=[r_kTs])
                d_l.dma(SP, qs[:], qT_d[2 * hp:2 * hp + 2, :, :].rearrange("h d w -> d h w"), writes=[r_qs])
                d_l.dma(SP, gs[:], gT_d[2 * hp:2 * hp + 2, :, :].rearrange("h d w -> d h w"), writes=[r_gs2])
                vc = slice(256 * hp, 256 * hp + 256)
                e1 = d_l.dma(SP, v1[:], v_win[960:960 + 1152, vc].rearrange("(c p) n -> p c n", p=128), writes=[r_v])
                e2 = d_l.dma(SP, v2[:], v_win[768:2304, vc].rearrange("(c p r) n -> p r c n", p=128, r=4), writes=[])
                e3 = d_l.dma(SP, v3a[:], v_win[0:2048, vc].rearrange("(p r) n -> p r n", r=16), writes=[])
                e4 = d_l.dma(SP, v3b[:], v_win[2048:3072, vc].rearrange("(p r) n -> p r n", r=16), writes=[r_v])
                for e in (e1, e2, e3):
                    PE.wait(e)
                for hh in range(2):
                    h = 2 * hp + hh
                    started = [False, False]
                    for bi, b in enumerate(blks):
                        nk, nq, q0, p = b["nk"], b["nq"], b["q0"], b["p"]
                        sb = SB0 + blk_i % 2
                        eb = blk_i % 2
                        blk_i += 1
                        PE.op(lambda b=b, sb=sb, nk=nk, nq=nq, hh=hh: nc.tensor.matmul(
                            psb[sb][0:nk, 0:nq], lhsT=kTs[:, hh, b["kslice"]], rhs=qs[:, hh, b["qslice"]],
                            start=True, stop=True), reads=[r_kTs, r_qs], writes=[r_ps[sb]])
                        ACT.op(lambda sb=sb, eb=eb, nk=nk, nq=nq: nc.scalar.activation(
                            out=esb[eb][0:nk, 0:nq], in_=psb[sb][0:nk, 0:nq], func=AF.Exp, scale=float(scale)),
                            writes=[r_ps[sb], r_esb[eb]])
                        DVE.op(lambda eb=eb, nk=nk, nq=nq, q0=q0, p=p, h=h, bi=bi: nc.vector.scalar_tensor_tensor(
                            out=psb_[eb][0:nk, 0:nq], in0=esb[eb][0:nk, 0:nq],
                            scalar=prm[0:nk, P_KV + bi:P_KV + bi + 1], in1=EB[0:nk, 16 * p + h, q0:q0 + nq],
                            op0=ALU.mult, op1=ALU.mult), reads=[r_esb[eb], r_EB, r_prm], writes=[r_pp[eb]])
                        vk = b["v"]
                        if vk[0] == "v1":
                            vap = v1[:, vk[1], 128 * hh:128 * hh + 128]
                        elif vk[0] == "v2":
                            vap = v2[:, vk[1], vk[2], 128 * hh:128 * hh + 128]
                        elif vk[0] == "v3a":
                            vap = v3a[:, vk[1], 128 * hh:128 * hh + 128]
                        else:
                            vap = v3b[:, vk[1], 128 * hh:128 * hh + 128]
                        qsl = b["qslice"]
                        pieces = []
                        qi = 0
                        while qi < nq:
                            t0 = qsl.start + qsl.step * qi
                            bank = t0 // 512
                            n_in = min(nq - qi, (512 * (bank + 1) - t0 + qsl.step - 1) // qsl.step)
                            if b["p"] == 0:
                                n_in = min(n_in, 128)
                            pieces.append((qi, n_in, bank, t0 - 512 * bank))
                            qi += n_in

                        def pv(pieces=pieces, eb=eb, nk=nk, vap=vap, step=qsl.step, started=started):
                            ins = None
                            for (qi, n_in, bank, c0) in pieces:
                                st_ = not started[bank]
                                started[bank] = True
                                osl = slice(c0, c0 + step * (n_in - 1) + 1, step)
                                nc.tensor.matmul(psb[NB0 + bank][:, osl], lhsT=vap, rhs=psb_[eb][0:nk, qi:qi + n_in],
                                                 start=st_, stop=False, skip_group_check=True)
                                ins = nc.tensor.matmul(psb[ZB0 + bank][:, osl], lhsT=ones1[0:nk, :],
                                                       rhs=psb_[eb][0:nk, qi:qi + n_in],
                                                       start=st_, stop=False, skip_group_check=True)
                            return ins
                        PE.op(pv, reads=[r_pp[eb], r_v, r_const],
                              writes=[r_ps[NB0], r_ps[NB0 + 1], r_ps[ZB0], r_ps[ZB0 + 1]])
                    for bank in range(2):
                        hs = slice(512 * bank, 512 * bank + 512)
                        DVE.op(lambda bank=bank, hs=hs: nc.vector.reciprocal(out=rz[:, hs], in_=psb[ZB0 + bank][:, :]),
                               writes=[r_ps[ZB0 + bank], r_rz])
                        DVE.op(lambda bank=bank, hs=hs: nc.vector.tensor_tensor(out=at[:, hs], in0=psb[NB0 + bank][:, :],
                                                                                in1=rz[:, hs], op=ALU.mult),
                               reads=[r_rz], writes=[r_ps[NB0 + bank], r_at])
                        DVE.op(lambda hs=hs, h=h, hh=hh: nc.vector.tensor_tensor(out=mixA[:, h, hs], in0=at[:, hs],
                                                                                 in1=gs[:, hh, hs], op=ALU.mult),
                               reads=[r_at, r_gs2], writes=[r_mixA])
            barrier()

        if debug:
            d_s.dma(SP, mix_d[0:16, :, :].rearrange("c p t -> p c t"), mixC[:], reads=[r_mixC])
            d_s.dma(SP, mix_d[16:32, :, :].rearrange("c p t -> p c t"), mixA[:], reads=[r_mixA])

        with ExitStack() as os_:
            wo = T(os_, "wo", [128, 2, NCH, 512], BF16)
            r_wo = [Res(), Res()]
            xp = [T(os_, f"xp{i}", [128, 512], F32) for i in range(4)]
            yst = [T(os_, f"yst{i}", [128, 512], F32) for i in range(4)]
            r_xp = [Res() for _ in range(4)]
            r_y = [Res() for _ in range(4)]

            def load_wo(n):
                if n < 8:
                    d_w.dma(POOL, wo[:, n % 2], w_out_v[:, :, 512 * n:512 * n + 512], writes=[r_wo[n % 2]])
            load_wo(0)
            u = 0
            out_evs = []
            for n in range(8):
                load_wo(n + 1)
                for i in range(8):
                    bk = u % 4
                    sl = u % 4
                    u += 1
                    d_x.dma(SP, xp[sl][:], xw[1024 + 128 * i:1024 + 128 * i + 128, 512 * n:512 * n + 512],
                            writes=[r_xp[sl]])

                    def mm(bk=bk, i=i, n=n):
                        ins = None
                        for e in range(NCH):
                            src = mixC if e < 16 else mixA
                            ins = nc.tensor.matmul(psb[bk][:, :], lhsT=src[:, e % 16, 128 * i:128 * i + 128],
                                                   rhs=wo[:, n % 2, e, :], start=(e == 0), stop=(e == NCH - 1))
                        return ins
                    PE.op(mm, reads=[r_wo[n % 2], r_mixC, r_mixA], writes=[r_ps[bk]])
                    DVE.op(lambda bk=bk, sl=sl: nc.vector.tensor_tensor(out=yst[sl][:], in0=psb[bk][:, :], in1=xp[sl][:],
                                                                        op=ALU.add),
                           reads=[r_xp[sl]], writes=[r_ps[bk], r_y[sl]])
                    out_evs.append(d_o.dma(SP, y[128 * i:128 * i + 128, 512 * n:512 * n + 512], yst[sl][:],
                                           reads=[r_y[sl]]))
            for ev in d_o.events():
                SP.wait(ev)
            barrier()
    return nc


def _t5_bucket(rel):
    import math
    half, exact = 16, 8
    rel = np.asarray(rel, dtype=np.int32)
    n = np.abs(rel)
    nf = np.maximum(n, 1).astype(np.float32)
    large = exact + (np.log(nf / np.float32(exact)) / np.float32(math.log(1024 / exact))
                     * np.float32(half - exact)).astype(np.int32)
    large = np.minimum(large, half - 1)
    return np.where(rel > 0, half, 0) + np.where(n < exact, n, large)


_CACHE = {}


def _prep(inputs, debug=False):
    x = np.asarray(inputs["x"], dtype=np.float32)[0]
    f = lambda k: np.asarray(inputs[k], dtype=np.float32)
    xpad = np.zeros((S + 2048, D), np.float32)
    xpad[1024:1024 + S] = x
    w_in = np.ascontiguousarray(f("w_in")[0])
    w_out = np.ascontiguousarray(f("w_out")[0])
    prm = np.zeros((128, NP), np.float32)
    prm[:, P_G:P_G + 32] = f("norm_g")[0].reshape(32, 128).T
    prm[:, P_QG] = f("q_norm_g")[0]
    prm[:, P_KG] = f("k_norm_g")[0]
    cw = f("conv_w")[0]
    prm[:, P_CW:P_CW + 496] = cw.reshape(31, 16, 128).transpose(2, 1, 0).reshape(128, 496)
    prm[:, P_CB:P_CB + 16] = f("conv_b")[0].reshape(16, 128).T
    prm[:, P_LG:P_LG + 16] = f("conv_ln_g")[0].reshape(16, 128).T
    prm[:, P_LB:P_LB + 16] = f("conv_ln_b")[0].reshape(16, 128).T
    rb = f("rel_bias")
    kk = np.arange(128)[:, None]
    qq = np.arange(256)[None, :]
    rel = 64 + kk - qq
    biasT = np.zeros((128, 48, 256), np.float32)
    for p, (_, dil) in enumerate(PATTERNS):
        bucket = _t5_bucket(rel * dil)
        biasT[:, 16 * p:16 * p + 16, :] = rb[bucket].transpose(0, 2, 1)
    mask = ((qq - kk >= 0) & (qq - kk <= 128)).astype(np.float32)
    blks = attn_blocks()
    in_maps = []
    for c in range(NCORE):
        pc = prm.copy()
        for bi, b in enumerate(blks):
            wk = np.array([key_window_index(b, k) if k < b["nk"] else -10 ** 6 for k in range(128)])
            g = 1024 * c - 1024 + wk
            pc[:, P_KV + bi] = ((g >= 0) & (g < S) & (wk >= 0)).astype(np.float32)
        in_maps.append({"xw": np.ascontiguousarray(xpad[1024 * c:1024 * c + WIN]), "w_in": w_in, "w_out": w_out,
                        "params": pc, "biasT": biasT, "mask": mask})
    return in_maps


def kernel(**inputs):
    in_maps = _prep(inputs)
    if "nc" not in _CACHE:
        _CACHE["nc"] = build_program()
    res = run_bass_kernel_spmd(_CACHE["nc"], in_maps, core_ids=list(range(NCORE)))
    out = np.concatenate([np.asarray(r["y"], dtype=np.float32) for r in res.results], axis=0)
    return out[None]
```

```python
import numpy as np
import ml_dtypes
from contextlib import ExitStack

import concourse.bass as bass
import concourse.mybir as mybir
from concourse.bass_utils import run_bass_kernel_spmd

F32 = mybir.dt.float32
BF16 = mybir.dt.bfloat16
AF = mybir.ActivationFunctionType
ALU = mybir.AluOpType

D = 4096
S = 8192
NCORE = 8
TOK = 1024
WIN = 3072
NCH = 32
INW = 14336
NHEAD = 16
NBLK = 53
NP = 640
P_G, P_QG, P_KG, P_CW, P_CB, P_LG, P_LB, P_KV = 0, 32, 33, 34, 530, 546, 562, 578
NORM_EPS = 1e-6
LN_EPS = 1e-5
PATTERNS = ((128, 1), (512, 4), (2048, 16))


class Sem:
    def __init__(s, h):
        s.h = h
        s.v = 0


class Res:
    def __init__(s):
        s.w = None
        s.r = {}


class Eng:
    def __init__(s, nc, h, name, es):
        s.h = h
        s.sem = Sem(es.enter_context(nc.semaphore("e_" + name)))
        s.seen = {}

    def wait(s, ev):
        if ev is None:
            return
        sem, v = ev
        if v <= 0 or s.seen.get(sem, 0) >= v:
            return
        s.h.wait_ge(sem.h, v)
        s.seen[sem] = v

    def pre(s, reads=(), writes=()):
        for r in reads:
            s.wait(r.w)
        for w in writes:
            if w.w is not None and w.w[0] is not s.sem:
                s.wait(w.w)
            for sem, v in list(w.r.items()):
                if sem is not s.sem:
                    s.wait((sem, v))

    def post(s, ins, reads=(), writes=()):
        s.sem.v += 1
        ins.then_inc(s.sem.h, 1)
        ev = (s.sem, s.sem.v)
        for r in reads:
            r.r[ev[0]] = ev[1]
        for w in writes:
            w.w = ev
            w.r = {}
        return ev

    def op(s, fn, reads=(), writes=()):
        s.pre(reads, writes)
        return s.post(fn(), reads, writes)


class DmaPool:
    def __init__(s, nc, es, name, k):
        s.sems = [Sem(es.enter_context(nc.semaphore(f"d_{name}{i}"))) for i in range(k)]
        s.last = [None] * k
        s.i = 0

    def dma(s, q, out, in_, reads=(), writes=()):
        k = s.i % len(s.sems)
        s.i += 1
        sem = s.sems[k]
        q.wait(s.last[k])
        q.pre(reads, writes)
        ins = q.h.dma_start(out=out, in_=in_)
        sem.v += 16
        ins.then_inc(sem.h, 16)
        ev = (sem, sem.v)
        s.last[k] = ev
        for r in reads:
            r.r[sem] = sem.v
        for w in writes:
            w.w = ev
            w.r = {}
        return ev

    def events(s):
        return [e for e in s.last if e is not None]


def attn_blocks():
    blks = []
    for c in range(9):
        tlo, thi = max(0, 128 * (c - 1)), min(TOK, 128 * (c + 1))
        blks.append(dict(p=0, nk=128, kslice=slice(960 + 128 * c, 960 + 128 * c + 128, 1),
                         qslice=slice(tlo, thi, 1), q0=tlo - 128 * (c - 1), nq=thi - tlo,
                         v=("v1", c)))
    for r in range(4):
        for c in range(3):
            llo, lhi = max(256, 128 + 128 * c), min(512, 384 + 128 * c)
            k0 = 4 * (192 + 128 * c) + r
            blks.append(dict(p=1, nk=128, kslice=slice(k0, k0 + 512, 4),
                             qslice=slice(4 * (llo - 256) + r, 4 * (lhi - 256), 4),
                             q0=llo - (128 + 128 * c), nq=lhi - llo, v=("v2", r, c)))
    for r in range(16):
        blks.append(dict(p=2, nk=128, kslice=slice(r, 2048, 16), qslice=slice(r, TOK, 16),
                         q0=128, nq=64, v=("v3a", r)))
    for r in range(16):
        blks.append(dict(p=2, nk=64, kslice=slice(2048 + r, WIN, 16), qslice=slice(r, TOK, 16),
                         q0=0, nq=64, v=("v3b", r)))
    assert len(blks) == NBLK
    return blks


def key_window_index(b, kk):
    s = b["kslice"]
    return s.start + s.step * kk


def build_program(debug=False):
    nc = bass.Bass("TRN2", target_bir_lowering=False)
    dkind = "ExternalOutput" if debug else "Internal"
    xw = nc.dram_tensor("xw", [WIN, D], F32, kind="ExternalInput").ap()
    w_in = nc.dram_tensor("w_in", [D, INW], F32, kind="ExternalInput").ap()
    w_out = nc.dram_tensor("w_out", [D, D], F32, kind="ExternalInput").ap()
    params = nc.dram_tensor("params", [128, NP], F32, kind="ExternalInput").ap()
    biasT = nc.dram_tensor("biasT", [128, 48, 256], F32, kind="ExternalInput").ap()
    maskd = nc.dram_tensor("mask", [128, 256], F32, kind="ExternalInput").ap()
    y = nc.dram_tensor("y", [TOK, D], F32, kind="ExternalOutput").ap()
    kT_win = nc.dram_tensor("kT_win", [NHEAD, 128, WIN], BF16, kind=dkind).ap()
    v_win = nc.dram_tensor("v_win", [WIN, 2048], BF16, kind=dkind).ap()
    qT_d = nc.dram_tensor("qT_d", [NHEAD, 128, TOK], BF16, kind=dkind).ap()
    gT_d = nc.dram_tensor("gT_d", [NHEAD, 128, TOK], BF16, kind=dkind).ap()
    if debug:
        mix_d = nc.dram_tensor("mix_d", [32, 128, TOK], BF16, kind="ExternalOutput").ap()

    w_in_v = w_in.rearrange("(c p) n -> p c n", p=128)
    w_out_v = w_out.rearrange("(c p) n -> p c n", p=128)
    blks = attn_blocks()

    with ExitStack() as es:
        PE = Eng(nc, nc.tensor, "pe", es)
        ACT = Eng(nc, nc.scalar, "act", es)
        DVE = Eng(nc, nc.vector, "dve", es)
        POOL = Eng(nc, nc.gpsimd, "pool", es)
        SP = Eng(nc, nc.sync, "sp", es)
        engines = [PE, ACT, DVE, POOL, SP]
        d_w = DmaPool(nc, es, "w", 4)
        d_x = DmaPool(nc, es, "x", 4)
        d_s = DmaPool(nc, es, "s", 8)
        d_l = DmaPool(nc, es, "l", 8)
        d_o = DmaPool(nc, es, "o", 8)
        pools = [d_w, d_x, d_s, d_l, d_o]

        def barrier():
            evs = [(e.sem, e.sem.v) for e in engines]
            for p in pools:
                evs += p.events()
            for e in engines:
                for ev in evs:
                    if ev[0] is not e.sem:
                        e.wait(ev)

        uid = [0]

        def T(stack, name, shape, dt):
            uid[0] += 1
            return stack.enter_context(nc.sbuf_tensor(f"{name}_{uid[0]}", shape, dt))

        prm = T(es, "prm", [128, NP], F32)
        ident = T(es, "ident", [128, 128], BF16)
        ones1 = T(es, "ones1", [128, 128], BF16)
        onesq = T(es, "onesq", [128, 128], BF16)
        onesl = T(es, "onesl", [128, 128], BF16)
        mixC = T(es, "mixC", [128, 16, TOK], BF16)
        r_prm, r_const, r_mixC = Res(), Res(), Res()
        psb = [es.enter_context(nc.psum_tensor(f"psb{i}", [128, 512], F32)) for i in range(8)]
        r_ps = [Res() for _ in range(8)]

        d_s.dma(SP, prm[:], params[:, :], writes=[r_prm])
        POOL.op(lambda: nc.gpsimd.memset(ident[:], 0.0), writes=[r_const])
        POOL.op(lambda: nc.gpsimd.affine_select(out=ident[:], in_=ident[:], pattern=[[-1, 128]],
                                                compare_op=ALU.not_equal, fill=1.0, base=0,
                                                channel_multiplier=1), reads=[r_const], writes=[r_const])
        POOL.op(lambda: nc.gpsimd.memset(ones1[:], 1.0), writes=[r_const])
        POOL.op(lambda: nc.gpsimd.memset(onesq[:], 1.0 / 128), writes=[r_const])
        POOL.op(lambda: nc.gpsimd.memset(onesl[:], 1.0 / 2048), writes=[r_const])

        passes = [("L", 0), ("O", 1024), ("R", 2048)]
        for pname, w0 in passes:
            own = pname == "O"
            with ExitStack() as ps_:
                xnT = T(ps_, "xnT", [128, NCH, 1056], BF16)
                r_xnA, r_xnD = Res(), Res()
                wr = T(ps_, "wr", [128, 4, NCH, 128], BF16)
                r_w = [Res() for _ in range(4)]

                if own:
                    cols = []
                    for cb in range(16):
                        cols += [("glu", cb, 16 + cb), ("val", cb, cb)]
                    cols += [("gate", cb, 32 + cb) for cb in range(16)]
                    cols += [("ag", h, 96 + h) for h in range(16)]
                    cols += [("v", h, 80 + h) for h in range(16)]
                    cols += [("q", h, 48 + h) for h in range(16)]
                    cols += [("k", h, 64 + h) for h in range(16)]
                else:
                    cols = [("v", h, 80 + h) for h in range(16)] + [("k", h, 64 + h) for h in range(16)]

                def load_w(i):
                    if i < len(cols):
                        j = cols[i][2]
                        d_w.dma(POOL, wr[:, i % 4], w_in_v[:, :, 128 * j:128 * j + 128],
                                writes=[r_w[i % 4]])

                for i in range(3):
                    load_w(i)

                with ExitStack() as ns_:
                    xs = [T(ns_, f"xs{i}", [128, D], F32) for i in range(2)]
                    xb = [T(ns_, f"xb{i}", [128, D], BF16) for i in range(2)]
                    st = [T(ns_, f"st{i}", [128, 4], F32) for i in range(2)]
                    r_xs = [Res(), Res()]
                    r_xb = [Res(), Res()]
                    r_st = [Res(), Res()]
                    r_junk = Res()
                    tiles = [(w0 + 128 * i, 128, 128 * i) for i in range(8)]
                    if own:
                        tiles.append((None, 32, 1024))
                    bank_i = 0
                    for ti, (row, npart, col0) in enumerate(tiles):
                        sl = ti % 2
                        if row is not None:
                            d_x.dma(SP, xs[sl][:], xw[row:row + 128, :], writes=[r_xs[sl]])
                        else:
                            d_x.dma(SP, xs[sl][0:16, :], xw[1008:1024, :], writes=[r_xs[sl]])
                            d_x.dma(SP, xs[sl][16:32, :], xw[2048:2064, :], writes=[])
                            ev2 = d_x.last[(d_x.i - 1) % 4]
                            ACT.wait(ev2)
                            DVE.wait(ev2)
                        ACT.op(lambda sl=sl, n=npart: nc.scalar.activation(
                            out=xb[sl][0:n, :], in_=xs[sl][0:n, :], func=AF.Square,
                            accum_out=st[sl][0:n, 0:1]), reads=[r_xs[sl]], writes=[r_xb[sl], r_st[sl]])
                        ACT.op(lambda sl=sl, n=npart: nc.scalar.activation(
                            out=st[sl][0:n, 1:2], in_=st[sl][0:n, 0:1], func=AF.Sqrt,
                            scale=1.0 / D, bias=NORM_EPS), reads=[r_st[sl]], writes=[r_st[sl]])
                        DVE.op(lambda sl=sl, n=npart: nc.vector.reciprocal(
                            out=st[sl][0:n, 2:3], in_=st[sl][0:n, 1:2]), reads=[r_st[sl]], writes=[r_st[sl]])
                        DVE.op(lambda sl=sl, n=npart: nc.vector.tensor_scalar(
                            out=xb[sl][0:n, :], in0=xs[sl][0:n, :], scalar1=st[sl][0:n, 2:3],
                            scalar2=None, op0=ALU.mult), reads=[r_xs[sl], r_st[sl]], writes=[r_xb[sl]])
                        for grp in range(4):
                            bk = bank_i % 4
                            bank_i += 1
                            pt = psb[bk][:].bitcast(BF16)

                            def tr(sl=sl, n=npart, grp=grp, pt=pt):
                                ins = None
                                for k in range(8):
                                    c = grp * 8 + k
                                    ins = nc.tensor.transpose(pt[:, k * 128:k * 128 + n],
                                                              xb[sl][0:n, c * 128:(c + 1) * 128],
                                                              ident[0:n, 0:n])
                                return ins
                            PE.op(tr, reads=[r_xb[sl], r_const], writes=[r_ps[bk]])
                            useA = (grp % 2 == 0)
                            E = ACT if useA else DVE
                            rx = r_xnA if useA else r_xnD
                            for k in range(8):
                                c = grp * 8 + k
                                if useA:
                                    fn = (lambda c=c, k=k, n=npart, pt=pt, col0=col0: nc.scalar.mul(
                                        out=xnT[:, c, col0:col0 + n], in_=pt[:, k * 128:k * 128 + n],
                                        mul=prm[:, P_G + c:P_G + c + 1]))
                                else:
                                    fn = (lambda c=c, k=k, n=npart, pt=pt, col0=col0: nc.vector.tensor_scalar(
                                        out=xnT[:, c, col0:col0 + n], in0=pt[:, k * 128:k * 128 + n],
                                        scalar1=prm[:, P_G + c:P_G + c + 1], scalar2=None, op0=ALU.mult))
                                E.op(fn, reads=[r_prm], writes=[r_ps[bk], rx])
                    barrier()

                ntok_groups = [(0, 512), (512, 1024)]
                main_i = [0]
                main_nb = [3]

                def main_mm(i, halo_cols=None):
                    slot = i % 4
                    out = []
                    for (c0, c1) in ntok_groups:
                        bk = main_i[0] % main_nb[0]
                        main_i[0] += 1

                        def mm(bk=bk, c0=c0, c1=c1, slot=slot):
                            ins = None
                            for c in range(NCH):
                                ins = nc.tensor.matmul(psb[bk][:, 0:c1 - c0], lhsT=wr[:, slot, c, :],
                                                       rhs=xnT[:, c, c0:c1], start=(c == 0), stop=(c == NCH - 1))
                            return ins
                        PE.op(mm, reads=[r_w[slot], r_xnA, r_xnD], writes=[r_ps[bk]])
                        out.append(bk)
                    if halo_cols is not None:
                        def mmh(slot=slot, hc=halo_cols):
                            ins = None
                            for c in range(NCH):
                                ins = nc.tensor.matmul(psb[3][:, hc:hc + 32], lhsT=wr[:, slot, c, :],
                                                       rhs=xnT[:, c, 1024:1056], start=(c == 0), stop=(c == NCH - 1))
                            return ins
                        PE.op(mmh, reads=[r_w[slot], r_xnA, r_xnD], writes=[r_ps[3]])
                    return out

                if own:
                    with ExitStack() as cs_:
                        sig = T(cs_, "sig", [128, 1056], F32)
                        valsb = T(cs_, "valsb", [128, 1056], F32)
                        aext = [T(cs_, f"aext{i}", [128, 1056], F32) for i in range(2)]
                        accA = T(cs_, "accA", [128, TOK], F32)
                        accB = T(cs_, "accB", [128, TOK], F32)
                        accC = [T(cs_, f"accC{i}", [128, TOK], F32) for i in range(2)]
                        tmpP = T(cs_, "tmpP", [128, TOK], F32)
                        csq = [T(cs_, f"csq{i}", [128, TOK], BF16) for i in range(3)]
                        r_sig, r_valsb, r_accA, r_accB, r_tmpP = Res(), Res(), Res(), Res(), Res()
                        r_aext = [Res(), Res()]
                        r_accC = [Res(), Res()]
                        r_csq = [Res(), Res(), Res()]
                        NDV = 27
                        pend_stats = []

                        def emit_stats(cb):
                            for hf in range(2):
                                def stat(hf=hf, cb=cb):
                                    nc.tensor.matmul(psb[4 + hf][:, :], lhsT=onesl[:], rhs=mixC[:, cb, 512 * hf:512 * hf + 512],
                                                     start=(cb == 0), stop=(cb == 15), skip_group_check=True)
                                    return nc.tensor.matmul(psb[6 + hf][:, :], lhsT=onesl[:], rhs=csq[cb % 3][:, 512 * hf:512 * hf + 512],
                                                            start=(cb == 0), stop=(cb == 15), skip_group_check=True)
                                PE.op(stat, reads=[r_mixC, r_csq[cb % 3], r_const], writes=[r_ps[4 + hf], r_ps[6 + hf]])

                        for i in range(32):
                            kind, cb, j = cols[i]
                            load_w(i + 3)
                            banks = main_mm(i, halo_cols=(0 if kind == "glu" else 32))
                            if kind == "glu":
                                while pend_stats and pend_stats[0] <= cb - 2:
                                    emit_stats(pend_stats.pop(0))
                                for hf, bk in enumerate(banks):
                                    ACT.op(lambda bk=bk, hf=hf: nc.scalar.activation(
                                        out=sig[:, 16 + 512 * hf:16 + 512 * hf + 512], in_=psb[bk][:, :],
                                        func=AF.Sigmoid), writes=[r_ps[bk], r_sig])
                                ACT.op(lambda: nc.scalar.activation(out=sig[:, 0:16], in_=psb[3][:, 0:16],
                                                                    func=AF.Sigmoid), writes=[r_ps[3], r_sig])
                                ACT.op(lambda: nc.scalar.activation(out=sig[:, 1040:1056], in_=psb[3][:, 16:32],
                                                                    func=AF.Sigmoid), writes=[r_ps[3], r_sig])
                            else:
                                ae, rae = aext[cb % 2], r_aext[cb % 2]
                                aC, raC = accC[cb % 2], r_accC[cb % 2]
                                for hf, bk in enumerate(banks):
                                    ACT.op(lambda bk=bk, hf=hf: nc.scalar.activation(
                                        out=valsb[:, 16 + 512 * hf:16 + 512 * hf + 512], in_=psb[bk][:, :], func=AF.Copy),
                                        writes=[r_ps[bk], r_valsb])
                                ACT.op(lambda: nc.scalar.activation(out=valsb[:, 0:16], in_=psb[3][:, 32:48], func=AF.Copy),
                                       writes=[r_ps[3], r_valsb])
                                ACT.op(lambda: nc.scalar.activation(out=valsb[:, 1040:1056], in_=psb[3][:, 48:64], func=AF.Copy),
                                       writes=[r_ps[3], r_valsb])
                                POOL.op(lambda ae=ae: nc.gpsimd.tensor_tensor(out=ae[:], in0=valsb[:], in1=sig[:], op=ALU.mult),
                                        reads=[r_sig, r_valsb], writes=[rae])
                                cwc = P_CW + cb * 31
                                POOL.op(lambda ae=ae, aC=aC, cwc=cwc: nc.gpsimd.tensor_scalar(
                                    out=aC[:], in0=ae[:, 1 + NDV:1 + NDV + TOK], scalar1=prm[:, cwc + NDV:cwc + NDV + 1],
                                    scalar2=None, op0=ALU.mult), reads=[rae, r_prm], writes=[raC])
                                for jt in range(NDV + 1, 31):
                                    POOL.op(lambda ae=ae, jt=jt, cwc=cwc: nc.gpsimd.tensor_scalar(
                                        out=tmpP[:], in0=ae[:, 1 + jt:1 + jt + TOK], scalar1=prm[:, cwc + jt:cwc + jt + 1],
                                        scalar2=None, op0=ALU.mult), reads=[rae, r_prm], writes=[r_tmpP])
                                    POOL.op(lambda aC=aC: nc.gpsimd.tensor_tensor(out=aC[:], in0=aC[:], in1=tmpP[:], op=ALU.add),
                                            reads=[raC, r_tmpP], writes=[raC])
                                DVE.op(lambda ae=ae, cwc=cwc, cb=cb: nc.vector.tensor_scalar(
                                    out=accA[:], in0=ae[:, 1:1 + TOK], scalar1=prm[:, cwc:cwc + 1],
                                    scalar2=prm[:, P_CB + cb:P_CB + cb + 1], op0=ALU.mult, op1=ALU.add),
                                    reads=[rae, r_prm], writes=[r_accA])
                                DVE.op(lambda ae=ae, cwc=cwc: nc.vector.tensor_scalar(
                                    out=accB[:], in0=ae[:, 2:2 + TOK], scalar1=prm[:, cwc + 1:cwc + 2],
                                    scalar2=None, op0=ALU.mult), reads=[rae, r_prm], writes=[r_accB])
                                for jt in range(2, NDV):
                                    acc, racc = (accA, r_accA) if jt % 2 == 0 else (accB, r_accB)
                                    DVE.op(lambda ae=ae, jt=jt, acc=acc, cwc=cwc: nc.vector.scalar_tensor_tensor(
                                        out=acc[:], in0=ae[:, 1 + jt:1 + jt + TOK],
                                        scalar=prm[:, cwc + jt:cwc + jt + 1], in1=acc[:],
                                        op0=ALU.mult, op1=ALU.add), reads=[rae, racc], writes=[racc])
                                DVE.op(lambda aC=aC: nc.vector.tensor_tensor(out=accB[:], in0=accB[:], in1=aC[:], op=ALU.add),
                                       reads=[r_accB, raC], writes=[r_accB])
                                DVE.op(lambda cb=cb: nc.vector.tensor_tensor(
                                    out=mixC[:, cb, :], in0=accA[:], in1=accB[:], op=ALU.add),
                                    reads=[r_accA, r_accB], writes=[r_mixC])
                                ACT.op(lambda cb=cb: nc.scalar.activation(out=csq[cb % 3][:], in_=mixC[:, cb, :],
                                                                          func=AF.Square),
                                       reads=[r_mixC], writes=[r_csq[cb % 3]])
                                pend_stats.append(cb)
                        while pend_stats:
                            emit_stats(pend_stats.pop(0))
                        barrier()
                    with ExitStack() as cs_:
                        musb = T(cs_, "musb", [128, TOK], F32)
                        rsb = T(cs_, "rsb", [128, TOK], F32)
                        tmp1 = T(cs_, "tmp1", [128, TOK], F32)
                        tmp2 = T(cs_, "tmp2", [128, TOK], F32)
                        gsb = T(cs_, "gsb", [128, TOK], BF16)
                        r_mu, r_rs, r_t1, r_t2, r_gs = Res(), Res(), Res(), Res(), Res()
                        for hf in range(2):
                            hs = slice(512 * hf, 512 * hf + 512)
                            DVE.op(lambda hf=hf, hs=hs: nc.vector.tensor_copy(out=musb[:, hs], in_=psb[4 + hf][:, :]),
                                   writes=[r_ps[4 + hf], r_mu])
                            DVE.op(lambda hs=hs: nc.vector.tensor_tensor(out=tmp1[:, hs], in0=musb[:, hs], in1=musb[:, hs],
                                                                         op=ALU.mult), reads=[r_mu], writes=[r_t1])
                            DVE.op(lambda hf=hf, hs=hs: nc.vector.tensor_tensor(out=tmp2[:, hs], in0=psb[6 + hf][:, :],
                                                                                in1=tmp1[:, hs], op=ALU.subtract),
                                   reads=[r_t1], writes=[r_ps[6 + hf], r_t2])
                            ACT.op(lambda hs=hs: nc.scalar.activation(out=tmp1[:, hs], in_=tmp2[:, hs], func=AF.Sqrt,
                                                                      bias=LN_EPS), reads=[r_t2], writes=[r_t1])
                            DVE.op(lambda hs=hs: nc.vector.reciprocal(out=rsb[:, hs], in_=tmp1[:, hs]),
                                   reads=[r_t1], writes=[r_rs])
                        for i in range(32, 48):
                            kind, cb, j = cols[i]
                            load_w(i + 3)
                            banks = main_mm(i)
                            for hf, bk in enumerate(banks):
                                ACT.op(lambda bk=bk, hf=hf: nc.scalar.activation(
                                    out=gsb[:, 512 * hf:512 * hf + 512], in_=psb[bk][:, :], func=AF.Silu),
                                    writes=[r_ps[bk], r_gs])
                            DVE.op(lambda cb=cb: nc.vector.tensor_tensor(out=tmp1[:], in0=mixC[:, cb, :], in1=musb[:],
                                                                         op=ALU.subtract),
                                   reads=[r_mixC, r_mu], writes=[r_t1])
                            DVE.op(lambda: nc.vector.tensor_tensor(out=tmp2[:], in0=tmp1[:], in1=rsb[:], op=ALU.mult),
                                   reads=[r_t1, r_rs], writes=[r_t2])
                            ACT.op(lambda cb=cb: nc.scalar.activation(
                                out=tmp1[:], in_=tmp2[:], func=AF.Silu, scale=prm[:, P_LG + cb:P_LG + cb + 1],
                                bias=prm[:, P_LB + cb:P_LB + cb + 1]), reads=[r_t2, r_prm], writes=[r_t1])
                            DVE.op(lambda cb=cb: nc.vector.tensor_tensor(out=mixC[:, cb, :], in0=tmp1[:], in1=gsb[:],
                                                                         op=ALU.mult),
                                   reads=[r_t1, r_gs], writes=[r_mixC])
                        barrier()
                    first_rest = 48
                else:
                    first_rest = 0

                with ExitStack() as qs_:
                    sq = [T(qs_, f"sq{i}", [128, 512], BF16) for i in range(2)]
                    sd = [T(qs_, f"sd{i}", [128, 512], F32) for i in range(2)]
                    rs = [T(qs_, f"rs{i}", [128, 512], F32) for i in range(2)]
                    kn = [T(qs_, f"kn{i}", [128, TOK], BF16) for i in range(2)]
                    raw = [[T(qs_, f"raw{i}{j}", [128, 512], F32) for j in range(2)] for i in range(2)]
                    r_raw = [[Res(), Res()], [Res(), Res()]]
                    raw_i = [0]
                    vT = T(qs_, "vT", [128, TOK], BF16)
                    vst = T(qs_, "vst", [128, 8, 512], BF16)
                    r_sq, r_sd, r_rs2 = [Res(), Res()], [Res(), Res()], [Res(), Res()]
                    r_kn = [Res(), Res()]
                    r_vT, r_vst = Res(), Res()
                    kn_i = [0]
                    pending = []
                    main_nb[0] = 4

                    def stage2(kind, h, banks, rw=0):
                        if kind in ("q", "k"):
                            gcol = P_QG if kind == "q" else P_KG
                            b = kn_i[0] % 2
                            kn_i[0] += 1
                            for hf, bk in enumerate(banks):
                                PE.op(lambda hf=hf: nc.tensor.matmul(psb[4 + hf][:, :], lhsT=onesq[:], rhs=sq[hf][:],
                                                                     start=True, stop=True),
                                      reads=[r_sq[hf], r_const], writes=[r_ps[4 + hf]])
                                ACT.op(lambda hf=hf: nc.scalar.activation(out=sd[hf][:], in_=psb[4 + hf][:, :],
                                                                          func=AF.Sqrt, bias=NORM_EPS),
                                       writes=[r_ps[4 + hf], r_sd[hf]])
                                DVE.op(lambda hf=hf: nc.vector.reciprocal(out=rs[hf][:], in_=sd[hf][:]),
                                       reads=[r_sd[hf]], writes=[r_rs2[hf]])
                                DVE.op(lambda hf=hf, rw=rw, b=b, gcol=gcol: nc.vector.scalar_tensor_tensor(
                                    out=kn[b][:, 512 * hf:512 * hf + 512], in0=raw[rw][hf][:],
                                    scalar=prm[:, gcol:gcol + 1], in1=rs[hf][:], op0=ALU.mult, op1=ALU.mult),
                                    reads=[r_rs2[hf], r_prm, r_raw[rw][hf]], writes=[r_kn[b]])
                            if kind == "k":
                                d_s.dma(SP, kT_win[h, :, w0:w0 + TOK], kn[b][:], reads=[r_kn[b]])
                            else:
                                d_s.dma(SP, qT_d[h, :, :], kn[b][:], reads=[r_kn[b]])
                        elif kind == "v":
                            pt = psb[6][:].bitcast(BF16)

                            def tr(pt=pt):
                                ins = None
                                for t in range(8):
                                    ins = nc.tensor.transpose(pt[:, t * 128:(t + 1) * 128], vT[:, t * 128:(t + 1) * 128],
                                                              ident[:])
                                return ins
                            PE.op(tr, reads=[r_vT, r_const], writes=[r_ps[6]])
                            hh = h % 4
                            DVE.op(lambda pt=pt, hh=hh: nc.vector.tensor_copy(
                                out=vst[:, :, hh * 128:(hh + 1) * 128],
                                in_=pt[:, 0:1024].rearrange("p (t d) -> p t d", d=128)),
                                writes=[r_ps[6], r_vst])
                            if hh == 3:
                                g = h // 4
                                d_s.dma(SP, v_win[w0:w0 + TOK, g * 512:(g + 1) * 512].rearrange("(t p) n -> p t n", p=128),
                                        vst[:], reads=[r_vst])

                    for i in range(first_rest, len(cols)):
                        kind, h, j = cols[i]
                        load_w(i + 3)
                        banks = main_mm(i)
                        for fn in pending:
                            fn()
                        pending = []
                        rw = raw_i[0] % 2
                        if kind in ("q", "k"):
                            raw_i[0] += 1
                            for hf, bk in enumerate(banks):
                                ACT.op(lambda bk=bk, hf=hf: nc.scalar.activation(out=sq[hf][:], in_=psb[bk][:, :],
                                                                                 func=AF.Square),
                                       writes=[r_ps[bk], r_sq[hf]])
                                ACT.op(lambda bk=bk, hf=hf, rw=rw: nc.scalar.activation(out=raw[rw][hf][:], in_=psb[bk][:, :],
                                                                                        func=AF.Copy),
                                       writes=[r_ps[bk], r_raw[rw][hf]])
                        elif kind == "v":
                            for hf, bk in enumerate(banks):
                                ACT.op(lambda bk=bk, hf=hf: nc.scalar.activation(out=vT[:, 512 * hf:512 * hf + 512],
                                                                                 in_=psb[bk][:, :], func=AF.Copy),
                                       writes=[r_ps[bk], r_vT])
                        elif kind == "ag":
                            b = kn_i[0] % 2
                            kn_i[0] += 1
                            for hf, bk in enumerate(banks):
                                ACT.op(lambda bk=bk, hf=hf, b=b: nc.scalar.activation(
                                    out=kn[b][:, 512 * hf:512 * hf + 512], in_=psb[bk][:, :], func=AF.Silu),
                                    writes=[r_ps[bk], r_kn[b]])
                            d_s.dma(SP, gT_d[h, :, :], kn[b][:], reads=[r_kn[b]])
                        if kind in ("q", "k", "v"):
                            pending.append(lambda kind=kind, h=h, banks=banks, rw=rw: stage2(kind, h, banks, rw))
                    for fn in pending:
                        fn()
                    barrier()

        mixA = T(es, "mixA", [128, 16, TOK], BF16)
        r_mixA = Res()
        with ExitStack() as as_:
            EB = T(as_, "EB", [128, 48, 256], BF16)
            msk = T(as_, "msk", [128, 256], F32)
            r_EB, r_msk = Res(), Res()
            d_l.dma(SP, msk[:], maskd[:, :], writes=[r_msk])
            with ExitStack() as bs_:
                bst = T(bs_, "bst", [128, 16, 256], F32)
                r_bst = Res()
                for p in range(3):
                    d_l.dma(SP, bst[:], biasT[:, 16 * p:16 * p + 16, :], writes=[r_bst])
                    ACT.op(lambda: nc.scalar.activation(out=bst[:], in_=bst[:], func=AF.Exp),
                           reads=[r_bst], writes=[r_bst])
                    for h in range(16):
                        DVE.op(lambda p=p, h=h: nc.vector.tensor_tensor(out=EB[:, 16 * p + h, :], in0=bst[:, h, :],
                                                                        in1=msk[:], op=ALU.mult),
                               reads=[r_bst, r_msk], writes=[r_EB])
                barrier()
            DEPTH = 3
            NR = 4
            kTs = [T(as_, f"kTs{i}", [128, WIN], BF16) for i in range(2)]
            qs = [T(as_, f"qs{i}", [128, TOK], BF16) for i in range(2)]
            gs = [T(as_, f"gs{i}", [128, TOK], BF16) for i in range(2)]
            v1 = [T(as_, f"v1{i}", [128, 9, 128], BF16) for i in range(2)]
            v2 = [T(as_, f"v2{i}", [128, 4, 3, 128], BF16) for i in range(2)]
            v3a = [T(as_, f"v3a{i}", [128, 16, 128], BF16) for i in range(2)]
            v3b = [T(as_, f"v3b{i}", [64, 16, 128], BF16) for i in range(2)]
            esb = [T(as_, f"esb{i}", [128, 256], BF16) for i in range(NR)]
            psb_ = [T(as_, f"pp{i}", [128, 256], BF16) for i in range(NR)]
            rz = T(as_, "rz", [128, TOK], F32)
            at = T(as_, "at", [128, TOK], F32)
            r_kTs, r_qs, r_gs2 = [Res(), Res()], [Res(), Res()], [Res(), Res()]
            r_v = [[Res() for _ in range(4)] for _ in range(2)]
            r_esb, r_pp = [Res() for _ in range(NR)], [Res() for _ in range(NR)]
            r_rz, r_at = Res(), Res()
            NB0, ZB0, SB0 = 0, 2, 4
            scale = 1.0 / np.sqrt(128.0)

            def load_head(h):
                if h >= NHEAD:
                    return
                sl = h % 2
                vc = slice(128 * h, 128 * h + 128)
                d_l.dma(SP, kTs[sl][:], kT_win[h, :, :], writes=[r_kTs[sl]])
                d_l.dma(SP, qs[sl][:], qT_d[h, :, :], writes=[r_qs[sl]])
                d_l.dma(SP, v1[sl][:], v_win[960:960 + 1152, vc].rearrange("(c p) n -> p c n", p=128),
                        writes=[r_v[sl][0]])
                d_l.dma(SP, v2[sl][:], v_win[768:2304, vc].rearrange("(c p r) n -> p r c n", p=128, r=4),
                        writes=[r_v[sl][1]])
                d_l.dma(SP, v3a[sl][:], v_win[0:2048, vc].rearrange("(p r) n -> p r n", r=16),
                        writes=[r_v[sl][2]])
                d_l.dma(SP, v3b[sl][:], v_win[2048:3072, vc].rearrange("(p r) n -> p r n", r=16),
                        writes=[r_v[sl][3]])
                d_l.dma(SP, gs[sl][:], gT_d[h, :, :], writes=[r_gs2[sl]])

            units = [(h, bi) for h in range(NHEAD) for bi in range(NBLK)]
            started = {}

            def stage_ab(u):
                h, bi = units[u]
                b = blks[bi]
                sl = h % 2
                nk, nq, q0, p = b["nk"], b["nq"], b["q0"], b["p"]
                sb = SB0 + u % NR
                eb = u % NR
                PE.op(lambda: nc.tensor.matmul(
                    psb[sb][0:nk, 0:nq], lhsT=kTs[sl][:, b["kslice"]], rhs=qs[sl][:, b["qslice"]],
                    start=True, stop=True), reads=[r_kTs[sl], r_qs[sl]], writes=[r_ps[sb]])
                ACT.op(lambda: nc.scalar.activation(
                    out=esb[eb][0:nk, 0:nq], in_=psb[sb][0:nk, 0:nq], func=AF.Exp, scale=float(scale)),
                    writes=[r_ps[sb], r_esb[eb]])
                DVE.op(lambda: nc.vector.scalar_tensor_tensor(
                    out=psb_[eb][0:nk, 0:nq], in0=esb[eb][0:nk, 0:nq],
                    scalar=prm[0:nk, P_KV + bi:P_KV + bi + 1], in1=EB[0:nk, 16 * p + h, q0:q0 + nq],
                    op0=ALU.mult, op1=ALU.mult), reads=[r_esb[eb], r_EB, r_prm], writes=[r_pp[eb]])

            def stage_c(u):
                h, bi = units[u]
                b = blks[bi]
                sl = h % 2
                nk, nq = b["nk"], b["nq"]
                eb = u % NR
                vk = b["v"]
                if vk[0] == "v1":
                    vap, rv = v1[sl][:, vk[1], :], r_v[sl][0]
                elif vk[0] == "v2":
                    vap, rv = v2[sl][:, vk[1], vk[2], :], r_v[sl][1]
                elif vk[0] == "v3a":
                    vap, rv = v3a[sl][:, vk[1], :], r_v[sl][2]
                else:
                    vap, rv = v3b[sl][:, vk[1], :], r_v[sl][3]
                qsl = b["qslice"]
                pieces = []
                qi = 0
                while qi < nq:
                    t0 = qsl.start + qsl.step * qi
                    bank = t0 // 512
                    n_in = min(nq - qi, (512 * (bank + 1) - t0 + qsl.step - 1) // qsl.step)
                    if b["p"] == 0:
                        n_in = min(n_in, 128)
                    pieces.append((qi, n_in, bank, t0 - 512 * bank))
                    qi += n_in

                def pv():
                    ins = None
                    for (qi, n_in, bank, c0) in pieces:
                        st_ = not started.get((h, bank), False)
                        started[(h, bank)] = True
                        osl = slice(c0, c0 + qsl.step * (n_in - 1) + 1, qsl.step)
                        nc.tensor.matmul(psb[NB0 + bank][:, osl], lhsT=vap, rhs=psb_[eb][0:nk, qi:qi + n_in],
                                         start=st_, stop=False, skip_group_check=True)
                        ins = nc.tensor.matmul(psb[ZB0 + bank][:, osl], lhsT=ones1[0:nk, :],
                                               rhs=psb_[eb][0:nk, qi:qi + n_in],
                                               start=st_, stop=False, skip_group_check=True)
                    return ins
                banks = sorted(set(pc[2] for pc in pieces))
                PE.op(pv, reads=[r_pp[eb], rv, r_const],
                      writes=[r_ps[NB0 + bk] for bk in banks] + [r_ps[ZB0 + bk] for bk in banks])
                if bi == NBLK - 1:
                    for bank in range(2):
                        hs = slice(512 * bank, 512 * bank + 512)
                        DVE.op(lambda bank=bank, hs=hs: nc.vector.reciprocal(out=rz[:, hs], in_=psb[ZB0 + bank][:, :]),
                               writes=[r_ps[ZB0 + bank], r_rz])
                        DVE.op(lambda bank=bank, hs=hs: nc.vector.tensor_tensor(out=at[:, hs], in0=psb[NB0 + bank][:, :],
                                                                                in1=rz[:, hs], op=ALU.mult),
                               reads=[r_rz], writes=[r_ps[NB0 + bank], r_at])
                        DVE.op(lambda hs=hs: nc.vector.tensor_tensor(out=mixA[:, h, hs], in0=at[:, hs],
                                                                     in1=gs[sl][:, hs], op=ALU.mult),
                               reads=[r_at, r_gs2[sl]], writes=[r_mixA])
                    load_head(h + 2)

            load_head(0)
            load_head(1)
            for u in range(len(units) + DEPTH):
                if u < len(units):
                    stage_ab(u)
                if u - DEPTH >= 0:
                    stage_c(u - DEPTH)
            barrier()

        if debug:
            d_s.dma(SP, mix_d[0:16, :, :].rearrange("c p t -> p c t"), mixC[:], reads=[r_mixC])
            d_s.dma(SP, mix_d[16:32, :, :].rearrange("c p t -> p c t"), mixA[:], reads=[r_mixA])

        with ExitStack() as os_:
            wo = T(os_, "wo", [128, 2, NCH, 512], BF16)
            r_wo = [Res(), Res()]
            xp = [T(os_, f"xp{i}", [128, 512], F32) for i in range(4)]
            yst = [T(os_, f"yst{i}", [128, 512], F32) for i in range(4)]
            r_xp = [Res() for _ in range(4)]
            r_y = [Res() for _ in range(4)]

            def load_wo(n):
                if n < 8:
                    d_w.dma(POOL, wo[:, n % 2], w_out_v[:, :, 512 * n:512 * n + 512], writes=[r_wo[n % 2]])
            load_wo(0)
            u = 0
            out_evs = []
            for n in range(8):
                load_wo(n + 1)
                for i in range(8):
                    bk = u % 4
                    sl = u % 4
                    u += 1
                    d_x.dma(SP, xp[sl][:], xw[1024 + 128 * i:1024 + 128 * i + 128, 512 * n:512 * n + 512],
                            writes=[r_xp[sl]])

                    def mm(bk=bk, i=i, n=n):
                        ins = None
                        for e in range(NCH):
                            src = mixC if e < 16 else mixA
                            ins = nc.tensor.matmul(psb[bk][:, :], lhsT=src[:, e % 16, 128 * i:128 * i + 128],
                                                   rhs=wo[:, n % 2, e, :], start=(e == 0), stop=(e == NCH - 1))
                        return ins
                    PE.op(mm, reads=[r_wo[n % 2], r_mixC, r_mixA], writes=[r_ps[bk]])
                    DVE.op(lambda bk=bk, sl=sl: nc.vector.tensor_tensor(out=yst[sl][:], in0=psb[bk][:, :], in1=xp[sl][:],
                                                                        op=ALU.add),
                           reads=[r_xp[sl]], writes=[r_ps[bk], r_y[sl]])
                    out_evs.append(d_o.dma(SP, y[128 * i:128 * i + 128, 512 * n:512 * n + 512], yst[sl][:],
                                           reads=[r_y[sl]]))
            for ev in d_o.events():
                SP.wait(ev)
            barrier()
    return nc


def _t5_bucket(rel):
    import math
    half, exact = 16, 8
    rel = np.asarray(rel, dtype=np.int32)
    n = np.abs(rel)
    nf = np.maximum(n, 1).astype(np.float32)
    large = exact + (np.log(nf / np.float32(exact)) / np.float32(math.log(1024 / exact))
                     * np.float32(half - exact)).astype(np.int32)
    large = np.minimum(large, half - 1)
    return np.where(rel > 0, half, 0) + np.where(n < exact, n, large)


_CACHE = {}


def _prep(inputs, debug=False):
    x = np.asarray(inputs["x"], dtype=np.float32)[0]
    f = lambda k: np.asarray(inputs[k], dtype=np.float32)
    xpad = np.zeros((S + 2048, D), np.float32)
    xpad[1024:1024 + S] = x
    w_in = np.ascontiguousarray(f("w_in")[0])
    w_out = np.ascontiguousarray(f("w_out")[0])
    prm = np.zeros((128, NP), np.float32)
    prm[:, P_G:P_G + 32] = f("norm_g")[0].reshape(32, 128).T
    prm[:, P_QG] = f("q_norm_g")[0]
    prm[:, P_KG] = f("k_norm_g")[0]
    cw = f("conv_w")[0]
    prm[:, P_CW:P_CW + 496] = cw.reshape(31, 16, 128).transpose(2, 1, 0).reshape(128, 496)
    prm[:, P_CB:P_CB + 16] = f("conv_b")[0].reshape(16, 128).T
    prm[:, P_LG:P_LG + 16] = f("conv_ln_g")[0].reshape(16, 128).T
    prm[:, P_LB:P_LB + 16] = f("conv_ln_b")[0].reshape(16, 128).T
    rb = f("rel_bias")
    kk = np.arange(128)[:, None]
    qq = np.arange(256)[None, :]
    rel = 64 + kk - qq
    biasT = np.zeros((128, 48, 256), np.float32)
    for p, (_, dil) in enumerate(PATTERNS):
        bucket = _t5_bucket(rel * dil)
        biasT[:, 16 * p:16 * p + 16, :] = rb[bucket].transpose(0, 2, 1)
    mask = ((qq - kk >= 0) & (qq - kk <= 128)).astype(np.float32)
    blks = attn_blocks()
    in_maps = []
    for c in range(NCORE):
        pc = prm.copy()
        for bi, b in enumerate(blks):
            wk = np.array([key_window_index(b, k) if k < b["nk"] else -10 ** 6 for k in range(128)])
            g = 1024 * c - 1024 + wk
            pc[:, P_KV + bi] = ((g >= 0) & (g < S) & (wk >= 0)).astype(np.float32)
        in_maps.append({"xw": np.ascontiguousarray(xpad[1024 * c:1024 * c + WIN]), "w_in": w_in, "w_out": w_out,
                        "params": pc, "biasT": biasT, "mask": mask})
    return in_maps


def kernel(**inputs):
    in_maps = _prep(inputs)
    if "nc" not in _CACHE:
        _CACHE["nc"] = build_program()
    res = run_bass_kernel_spmd(_CACHE["nc"], in_maps, core_ids=list(range(NCORE)))
    out = np.concatenate([np.asarray(r["y"], dtype=np.float32) for r in res.results], axis=0)
    return out[None]
```

```python
import numpy as np
import ml_dtypes
from contextlib import ExitStack

import concourse.bass as bass
import concourse.mybir as mybir
from concourse.bass_utils import run_bass_kernel_spmd

F32 = mybir.dt.float32
BF16 = mybir.dt.bfloat16
AF = mybir.ActivationFunctionType
ALU = mybir.AluOpType

D = 4096
S = 8192
NCORE = 8
TOK = 1024
WIN = 3072
NCH = 32
INW = 14336
NHEAD = 16
NBLK = 53
NP = 640
P_G, P_QG, P_KG, P_CW, P_CB, P_LG, P_LB, P_KV = 0, 32, 33, 34, 530, 546, 562, 578
NORM_EPS = 1e-6
LN_EPS = 1e-5
PATTERNS = ((128, 1), (512, 4), (2048, 16))


class Sem:
    def __init__(s, h):
        s.h = h
        s.v = 0


class Res:
    def __init__(s):
        s.w = None
        s.r = {}


class Eng:
    def __init__(s, nc, h, name, es):
        s.h = h
        s.sem = Sem(es.enter_context(nc.semaphore("e_" + name)))
        s.seen = {}

    def wait(s, ev):
        if ev is None:
            return
        sem, v = ev
        if v <= 0 or s.seen.get(sem, 0) >= v:
            return
        s.h.wait_ge(sem.h, v)
        s.seen[sem] = v

    def pre(s, reads=(), writes=(), own_ok=()):
        for r in reads:
            if r in own_ok and r.w is not None and r.w[0] is s.sem:
                continue
            s.wait(r.w)
        for w in writes:
            if w.w is not None and w.w[0] is not s.sem:
                s.wait(w.w)
            for sem, v in list(w.r.items()):
                if sem is not s.sem:
                    s.wait((sem, v))

    def post(s, ins, reads=(), writes=()):
        s.sem.v += 1
        ins.then_inc(s.sem.h, 1)
        ev = (s.sem, s.sem.v)
        for r in reads:
            r.r[ev[0]] = ev[1]
        for w in writes:
            w.w = ev
            w.r = {}
        return ev

    def op(s, fn, reads=(), writes=(), own_ok=()):
        s.pre(reads, writes, own_ok)
        return s.post(fn(), reads, writes)


class DmaPool:
    def __init__(s, nc, es, name, k):
        s.sems = [Sem(es.enter_context(nc.semaphore(f"d_{name}{i}"))) for i in range(k)]
        s.last = [None] * k
        s.i = 0

    def dma(s, q, out, in_, reads=(), writes=()):
        k = s.i % len(s.sems)
        s.i += 1
        sem = s.sems[k]
        q.wait(s.last[k])
        q.pre(reads, writes)
        ins = q.h.dma_start(out=out, in_=in_)
        sem.v += 16
        ins.then_inc(sem.h, 16)
        ev = (sem, sem.v)
        s.last[k] = ev
        for r in reads:
            r.r[sem] = sem.v
        for w in writes:
            w.w = ev
            w.r = {}
        return ev

    def events(s):
        return [e for e in s.last if e is not None]


def attn_blocks():
    blks = []
    for c in range(9):
        tlo, thi = max(0, 128 * (c - 1)), min(TOK, 128 * (c + 1))
        blks.append(dict(p=0, nk=128, kslice=slice(960 + 128 * c, 960 + 128 * c + 128, 1),
                         qslice=slice(tlo, thi, 1), q0=tlo - 128 * (c - 1), nq=thi - tlo,
                         v=("v1", c)))
    for r in range(4):
        for c in range(3):
            llo, lhi = max(256, 128 + 128 * c), min(512, 384 + 128 * c)
            k0 = 4 * (192 + 128 * c) + r
            blks.append(dict(p=1, nk=128, kslice=slice(k0, k0 + 512, 4),
                             qslice=slice(4 * (llo - 256) + r, 4 * (lhi - 256), 4),
                             q0=llo - (128 + 128 * c), nq=lhi - llo, v=("v2", r, c)))
    for r in range(16):
        blks.append(dict(p=2, nk=128, kslice=slice(r, 2048, 16), qslice=slice(r, TOK, 16),
                         q0=128, nq=64, v=("v3a", r)))
    for r in range(16):
        blks.append(dict(p=2, nk=64, kslice=slice(2048 + r, WIN, 16), qslice=slice(r, TOK, 16),
                         q0=0, nq=64, v=("v3b", r)))
    assert len(blks) == NBLK
    return blks


def key_window_index(b, kk):
    s = b["kslice"]
    return s.start + s.step * kk


def build_program(debug=False):
    nc = bass.Bass("TRN2", target_bir_lowering=False)
    dkind = "ExternalOutput" if debug else "Internal"
    xw = nc.dram_tensor("xw", [WIN, D], F32, kind="ExternalInput").ap()
    w_in = nc.dram_tensor("w_in", [D, INW], F32, kind="ExternalInput").ap()
    w_out = nc.dram_tensor("w_out", [D, D], F32, kind="ExternalInput").ap()
    params = nc.dram_tensor("params", [128, NP], F32, kind="ExternalInput").ap()
    biasT = nc.dram_tensor("biasT", [128, 48, 256], F32, kind="ExternalInput").ap()
    maskd = nc.dram_tensor("mask", [128, 256], F32, kind="ExternalInput").ap()
    y = nc.dram_tensor("y", [TOK, D], F32, kind="ExternalOutput").ap()
    kT_win = nc.dram_tensor("kT_win", [NHEAD, 128, WIN], BF16, kind=dkind).ap()
    v_win = nc.dram_tensor("v_win", [WIN, 2048], BF16, kind=dkind).ap()
    qT_d = nc.dram_tensor("qT_d", [NHEAD, 128, TOK], BF16, kind=dkind).ap()
    gT_d = nc.dram_tensor("gT_d", [NHEAD, 128, TOK], BF16, kind=dkind).ap()
    if debug:
        mix_d = nc.dram_tensor("mix_d", [32, 128, TOK], BF16, kind="ExternalOutput").ap()

    w_in_v = w_in.rearrange("(c p) n -> p c n", p=128)
    w_out_v = w_out.rearrange("(c p) n -> p c n", p=128)
    blks = attn_blocks()

    with ExitStack() as es:
        PE = Eng(nc, nc.tensor, "pe", es)
        ACT = Eng(nc, nc.scalar, "act", es)
        DVE = Eng(nc, nc.vector, "dve", es)
        POOL = Eng(nc, nc.gpsimd, "pool", es)
        SP = Eng(nc, nc.sync, "sp", es)
        engines = [PE, ACT, DVE, POOL, SP]
        d_w = DmaPool(nc, es, "w", 4)
        d_x = DmaPool(nc, es, "x", 4)
        d_s = DmaPool(nc, es, "s", 8)
        d_l = DmaPool(nc, es, "l", 8)
        d_o = DmaPool(nc, es, "o", 8)
        pools = [d_w, d_x, d_s, d_l, d_o]

        def barrier():
            evs = [(e.sem, e.sem.v) for e in engines]
            for p in pools:
                evs += p.events()
            for e in engines:
                for ev in evs:
                    if ev[0] is not e.sem:
                        e.wait(ev)

        uid = [0]

        def T(stack, name, shape, dt):
            uid[0] += 1
            return stack.enter_context(nc.sbuf_tensor(f"{name}_{uid[0]}", shape, dt))

        prm = T(es, "prm", [128, NP], F32)
        ident = T(es, "ident", [128, 128], BF16)
        ones1 = T(es, "ones1", [128, 128], BF16)
        onesq = T(es, "onesq", [128, 128], BF16)
        onesl = T(es, "onesl", [128, 128], BF16)
        mixC = T(es, "mixC", [128, 16, TOK], BF16)
        r_prm, r_const, r_mixC = Res(), Res(), Res()
        psb = [es.enter_context(nc.psum_tensor(f"psb{i}", [128, 512], F32)) for i in range(8)]
        r_ps = [Res() for _ in range(8)]

        d_s.dma(SP, prm[:], params[:, :], writes=[r_prm])
        POOL.op(lambda: nc.gpsimd.memset(ident[:], 0.0), writes=[r_const])
        POOL.op(lambda: nc.gpsimd.affine_select(out=ident[:], in_=ident[:], pattern=[[-1, 128]],
                                                compare_op=ALU.not_equal, fill=1.0, base=0,
                                                channel_multiplier=1), reads=[r_const], writes=[r_const])
        POOL.op(lambda: nc.gpsimd.memset(ones1[:], 1.0), writes=[r_const])
        POOL.op(lambda: nc.gpsimd.memset(onesq[:], 1.0 / 128), writes=[r_const])
        POOL.op(lambda: nc.gpsimd.memset(onesl[:], 1.0 / 2048), writes=[r_const])

        passes = [("L", 0), ("O", 1024), ("R", 2048)]
        for pname, w0 in passes:
            own = pname == "O"
            with ExitStack() as ps_:
                xnT = T(ps_, "xnT", [128, NCH, 1056], BF16)
                r_xnA, r_xnD = Res(), Res()
                wr = T(ps_, "wr", [128, 4, NCH, 128], BF16)
                r_w = [Res() for _ in range(4)]

                if own:
                    cols = []
                    for cb in range(16):
                        cols += [("glu", cb, 16 + cb), ("val", cb, cb)]
                    cols += [("gate", cb, 32 + cb) for cb in range(16)]
                    cols += [("ag", h, 96 + h) for h in range(16)]
                    cols += [("v", h, 80 + h) for h in range(16)]
                    cols += [("q", h, 48 + h) for h in range(16)]
                    cols += [("k", h, 64 + h) for h in range(16)]
                else:
                    cols = [("v", h, 80 + h) for h in range(16)] + [("k", h, 64 + h) for h in range(16)]

                def load_w(i):
                    if i < len(cols):
                        j = cols[i][2]
                        d_w.dma(POOL, wr[:, i % 4], w_in_v[:, :, 128 * j:128 * j + 128],
                                writes=[r_w[i % 4]])

                for i in range(3):
                    load_w(i)

                with ExitStack() as ns_:
                    xs = [T(ns_, f"xs{i}", [128, D], F32) for i in range(2)]
                    xb = [T(ns_, f"xb{i}", [128, D], BF16) for i in range(2)]
                    st = [T(ns_, f"st{i}", [128, 4], F32) for i in range(2)]
                    r_xs = [Res(), Res()]
                    r_xb = [Res(), Res()]
                    r_st = [Res(), Res()]
                    r_junk = Res()
                    tiles = [(w0 + 128 * i, 128, 128 * i) for i in range(8)]
                    if own:
                        tiles.append((None, 32, 1024))
                    bank_i = 0
                    for ti, (row, npart, col0) in enumerate(tiles):
                        sl = ti % 2
                        if row is not None:
                            d_x.dma(SP, xs[sl][:], xw[row:row + 128, :], writes=[r_xs[sl]])
                        else:
                            d_x.dma(SP, xs[sl][0:16, :], xw[1008:1024, :], writes=[r_xs[sl]])
                            d_x.dma(SP, xs[sl][16:32, :], xw[2048:2064, :], writes=[])
                            ev2 = d_x.last[(d_x.i - 1) % 4]
                            ACT.wait(ev2)
                            DVE.wait(ev2)
                        ACT.op(lambda sl=sl, n=npart: nc.scalar.activation(
                            out=xb[sl][0:n, :], in_=xs[sl][0:n, :], func=AF.Square,
                            accum_out=st[sl][0:n, 0:1]), reads=[r_xs[sl]], writes=[r_xb[sl], r_st[sl]])
                        ACT.op(lambda sl=sl, n=npart: nc.scalar.activation(
                            out=st[sl][0:n, 1:2], in_=st[sl][0:n, 0:1], func=AF.Sqrt,
                            scale=1.0 / D, bias=NORM_EPS), reads=[r_st[sl]], writes=[r_st[sl]])
                        DVE.op(lambda sl=sl, n=npart: nc.vector.reciprocal(
                            out=st[sl][0:n, 2:3], in_=st[sl][0:n, 1:2]), reads=[r_st[sl]], writes=[r_st[sl]])
                        DVE.op(lambda sl=sl, n=npart: nc.vector.tensor_scalar(
                            out=xb[sl][0:n, :], in0=xs[sl][0:n, :], scalar1=st[sl][0:n, 2:3],
                            scalar2=None, op0=ALU.mult), reads=[r_xs[sl], r_st[sl]], writes=[r_xb[sl]])
                        for grp in range(4):
                            bk = bank_i % 4
                            bank_i += 1
                            pt = psb[bk][:].bitcast(BF16)

                            def tr(sl=sl, n=npart, grp=grp, pt=pt):
                                ins = None
                                for k in range(8):
                                    c = grp * 8 + k
                                    ins = nc.tensor.transpose(pt[:, k * 128:k * 128 + n],
                                                              xb[sl][0:n, c * 128:(c + 1) * 128],
                                                              ident[0:n, 0:n])
                                return ins
                            PE.op(tr, reads=[r_xb[sl], r_const], writes=[r_ps[bk]])
                            useA = (grp % 2 == 0)
                            E = ACT if useA else DVE
                            rx = r_xnA if useA else r_xnD
                            for k in range(8):
                                c = grp * 8 + k
                                if useA:
                                    fn = (lambda c=c, k=k, n=npart, pt=pt, col0=col0: nc.scalar.mul(
                                        out=xnT[:, c, col0:col0 + n], in_=pt[:, k * 128:k * 128 + n],
                                        mul=prm[:, P_G + c:P_G + c + 1]))
                                else:
                                    fn = (lambda c=c, k=k, n=npart, pt=pt, col0=col0: nc.vector.tensor_scalar(
                                        out=xnT[:, c, col0:col0 + n], in0=pt[:, k * 128:k * 128 + n],
                                        scalar1=prm[:, P_G + c:P_G + c + 1], scalar2=None, op0=ALU.mult))
                                E.op(fn, reads=[r_prm], writes=[r_ps[bk], rx])
                    barrier()

                ntok_groups = [(0, 512), (512, 1024)]
                main_i = [0]
                main_nb = [3]

                def main_mm(i, halo_cols=None):
                    slot = i % 4
                    out = []
                    for (c0, c1) in ntok_groups:
                        bk = main_i[0] % main_nb[0]
                        main_i[0] += 1

                        def mm(bk=bk, c0=c0, c1=c1, slot=slot):
                            ins = None
                            for c in range(NCH):
                                ins = nc.tensor.matmul(psb[bk][:, 0:c1 - c0], lhsT=wr[:, slot, c, :],
                                                       rhs=xnT[:, c, c0:c1], start=(c == 0), stop=(c == NCH - 1))
                            return ins
                        PE.op(mm, reads=[r_w[slot], r_xnA, r_xnD], writes=[r_ps[bk]])
                        out.append(bk)
                    if halo_cols is not None:
                        def mmh(slot=slot, hc=halo_cols):
                            ins = None
                            for c in range(NCH):
                                ins = nc.tensor.matmul(psb[3][:, hc:hc + 32], lhsT=wr[:, slot, c, :],
                                                       rhs=xnT[:, c, 1024:1056], start=(c == 0), stop=(c == NCH - 1))
                            return ins
                        PE.op(mmh, reads=[r_w[slot], r_xnA, r_xnD], writes=[r_ps[3]])
                    return out

                if own:
                    with ExitStack() as cs_:
                        sig = T(cs_, "sig", [128, 1056], F32)
                        valsb = T(cs_, "valsb", [128, 1056], F32)
                        aext = [T(cs_, f"aext{i}", [128, 1056], F32) for i in range(2)]
                        accA = T(cs_, "accA", [128, TOK], F32)
                        accB = T(cs_, "accB", [128, TOK], F32)
                        accC = [T(cs_, f"accC{i}", [128, TOK], F32) for i in range(2)]
                        tmpP = T(cs_, "tmpP", [128, TOK], F32)
                        csq = [T(cs_, f"csq{i}", [128, TOK], BF16) for i in range(3)]
                        r_sig, r_valsb, r_accA, r_accB, r_tmpP = Res(), Res(), Res(), Res(), Res()
                        r_aext = [Res(), Res()]
                        r_accC = [Res(), Res()]
                        r_csq = [Res(), Res(), Res()]
                        NDV = 31
                        pend_stats = []

                        def emit_stats(cb):
                            ACT.op(lambda cb=cb: nc.scalar.activation(out=csq[cb % 3][:], in_=mixC[:, cb, :],
                                                                      func=AF.Square),
                                   reads=[r_mixC], writes=[r_csq[cb % 3]])
                            for hf in range(2):
                                def stat(hf=hf, cb=cb):
                                    nc.tensor.matmul(psb[4 + hf][:, :], lhsT=onesl[:], rhs=mixC[:, cb, 512 * hf:512 * hf + 512],
                                                     start=(cb == 0), stop=(cb == 15), skip_group_check=True)
                                    return nc.tensor.matmul(psb[6 + hf][:, :], lhsT=onesl[:], rhs=csq[cb % 3][:, 512 * hf:512 * hf + 512],
                                                            start=(cb == 0), stop=(cb == 15), skip_group_check=True)
                                PE.op(stat, reads=[r_mixC, r_csq[cb % 3], r_const], writes=[r_ps[4 + hf], r_ps[6 + hf]])

                        for i in range(32):
                            kind, cb, j = cols[i]
                            load_w(i + 3)
                            banks = main_mm(i, halo_cols=(0 if kind == "glu" else 32))
                            if kind == "glu":
                                while pend_stats and pend_stats[0] <= cb - 2:
                                    emit_stats(pend_stats.pop(0))
                                for hf, bk in enumerate(banks):
                                    ACT.op(lambda bk=bk, hf=hf: nc.scalar.activation(
                                        out=sig[:, 16 + 512 * hf:16 + 512 * hf + 512], in_=psb[bk][:, :],
                                        func=AF.Sigmoid), writes=[r_ps[bk], r_sig])
                                ACT.op(lambda: nc.scalar.activation(out=sig[:, 0:16], in_=psb[3][:, 0:16],
                                                                    func=AF.Sigmoid), writes=[r_ps[3], r_sig])
                                ACT.op(lambda: nc.scalar.activation(out=sig[:, 1040:1056], in_=psb[3][:, 16:32],
                                                                    func=AF.Sigmoid), writes=[r_ps[3], r_sig])
                            else:
                                ae, rae = aext[cb % 2], r_aext[cb % 2]
                                aC, raC = accC[cb % 2], r_accC[cb % 2]
                                for hf, bk in enumerate(banks):
                                    ACT.op(lambda bk=bk, hf=hf: nc.scalar.activation(
                                        out=valsb[:, 16 + 512 * hf:16 + 512 * hf + 512], in_=psb[bk][:, :], func=AF.Copy),
                                        writes=[r_ps[bk], r_valsb])
                                ACT.op(lambda: nc.scalar.activation(out=valsb[:, 0:16], in_=psb[3][:, 32:48], func=AF.Copy),
                                       writes=[r_ps[3], r_valsb])
                                ACT.op(lambda: nc.scalar.activation(out=valsb[:, 1040:1056], in_=psb[3][:, 48:64], func=AF.Copy),
                                       writes=[r_ps[3], r_valsb])
                                DVE.op(lambda ae=ae: nc.vector.tensor_tensor(out=ae[:], in0=valsb[:], in1=sig[:], op=ALU.mult),
                                       reads=[r_sig, r_valsb], writes=[rae])
                                cwc = P_CW + cb * 31
                                DVE.op(lambda ae=ae, cwc=cwc, cb=cb: nc.vector.tensor_scalar(
                                    out=accA[:], in0=ae[:, 1:1 + TOK], scalar1=prm[:, cwc:cwc + 1],
                                    scalar2=prm[:, P_CB + cb:P_CB + cb + 1], op0=ALU.mult, op1=ALU.add),
                                    reads=[rae, r_prm], writes=[r_accA])
                                DVE.op(lambda ae=ae, cwc=cwc: nc.vector.tensor_scalar(
                                    out=accB[:], in0=ae[:, 2:2 + TOK], scalar1=prm[:, cwc + 1:cwc + 2],
                                    scalar2=None, op0=ALU.mult), reads=[rae, r_prm], writes=[r_accB])
                                for jt in range(2, NDV):
                                    acc, racc = (accA, r_accA) if jt % 2 == 0 else (accB, r_accB)
                                    DVE.op(lambda ae=ae, jt=jt, acc=acc, cwc=cwc: nc.vector.scalar_tensor_tensor(
                                        out=acc[:], in0=ae[:, 1 + jt:1 + jt + TOK],
                                        scalar=prm[:, cwc + jt:cwc + jt + 1], in1=acc[:],
                                        op0=ALU.mult, op1=ALU.add), reads=[rae, racc], writes=[racc],
                                        own_ok=([racc] if jt >= 4 else []))
                                DVE.op(lambda cb=cb: nc.vector.tensor_tensor(
                                    out=mixC[:, cb, :], in0=accA[:], in1=accB[:], op=ALU.add),
                                    reads=[r_accA, r_accB], writes=[r_mixC])
                                pend_stats.append(cb)
                        while pend_stats:
                            emit_stats(pend_stats.pop(0))
                        barrier()
                    with ExitStack() as cs_:
                        musb = T(cs_, "musb", [128, TOK], F32)
                        rsb = T(cs_, "rsb", [128, TOK], F32)
                        tmp1 = T(cs_, "tmp1", [128, TOK], F32)
                        tmp2 = T(cs_, "tmp2", [128, TOK], F32)
                        gsb = T(cs_, "gsb", [128, TOK], BF16)
                        r_mu, r_rs, r_t1, r_t2, r_gs = Res(), Res(), Res(), Res(), Res()
                        for hf in range(2):
                            hs = slice(512 * hf, 512 * hf + 512)
                            DVE.op(lambda hf=hf, hs=hs: nc.vector.tensor_copy(out=musb[:, hs], in_=psb[4 + hf][:, :]),
                                   writes=[r_ps[4 + hf], r_mu])
                            DVE.op(lambda hs=hs: nc.vector.tensor_tensor(out=tmp1[:, hs], in0=musb[:, hs], in1=musb[:, hs],
                                                                         op=ALU.mult), reads=[r_mu], writes=[r_t1])
                            DVE.op(lambda hf=hf, hs=hs: nc.vector.tensor_tensor(out=tmp2[:, hs], in0=psb[6 + hf][:, :],
                                                                                in1=tmp1[:, hs], op=ALU.subtract),
                                   reads=[r_t1], writes=[r_ps[6 + hf], r_t2])
                            ACT.op(lambda hs=hs: nc.scalar.activation(out=tmp1[:, hs], in_=tmp2[:, hs], func=AF.Sqrt,
                                                                      bias=LN_EPS), reads=[r_t2], writes=[r_t1])
                            DVE.op(lambda hs=hs: nc.vector.reciprocal(out=rsb[:, hs], in_=tmp1[:, hs]),
                                   reads=[r_t1], writes=[r_rs])
                        for i in range(32, 48):
                            kind, cb, j = cols[i]
                            load_w(i + 3)
                            banks = main_mm(i)
                            for hf, bk in enumerate(banks):
                                ACT.op(lambda bk=bk, hf=hf: nc.scalar.activation(
                                    out=gsb[:, 512 * hf:512 * hf + 512], in_=psb[bk][:, :], func=AF.Silu),
                                    writes=[r_ps[bk], r_gs])
                            DVE.op(lambda cb=cb: nc.vector.tensor_tensor(out=tmp1[:], in0=mixC[:, cb, :], in1=musb[:],
                                                                         op=ALU.subtract),
                                   reads=[r_mixC, r_mu], writes=[r_t1])
                            DVE.op(lambda: nc.vector.tensor_tensor(out=tmp2[:], in0=tmp1[:], in1=rsb[:], op=ALU.mult),
                                   reads=[r_t1, r_rs], writes=[r_t2])
                            ACT.op(lambda cb=cb: nc.scalar.activation(
                                out=tmp1[:], in_=tmp2[:], func=AF.Silu, scale=prm[:, P_LG + cb:P_LG + cb + 1],
                                bias=prm[:, P_LB + cb:P_LB + cb + 1]), reads=[r_t2, r_prm], writes=[r_t1])
                            DVE.op(lambda cb=cb: nc.vector.tensor_tensor(out=mixC[:, cb, :], in0=tmp1[:], in1=gsb[:],
                                                                         op=ALU.mult),
                                   reads=[r_t1, r_gs], writes=[r_mixC])
                        barrier()
                    first_rest = 48
                else:
                    first_rest = 0

                with ExitStack() as qs_:
                    sq = [T(qs_, f"sq{i}", [128, 512], BF16) for i in range(2)]
                    sd = [T(qs_, f"sd{i}", [128, 512], F32) for i in range(2)]
                    rs = [T(qs_, f"rs{i}", [128, 512], F32) for i in range(2)]
                    kn = [T(qs_, f"kn{i}", [128, TOK], BF16) for i in range(2)]
                    raw = [[T(qs_, f"raw{i}{j}", [128, 512], F32) for j in range(2)] for i in range(2)]
                    r_raw = [[Res(), Res()], [Res(), Res()]]
                    raw_i = [0]
                    vT = T(qs_, "vT", [128, TOK], BF16)
                    vst = T(qs_, "vst", [128, 8, 512], BF16)
                    r_sq, r_sd, r_rs2 = [Res(), Res()], [Res(), Res()], [Res(), Res()]
                    r_kn = [Res(), Res()]
                    r_vT, r_vst = Res(), Res()
                    kn_i = [0]
                    pending = []
                    main_nb[0] = 4

                    def stage2(kind, h, banks, rw=0):
                        if kind in ("q", "k"):
                            gcol = P_QG if kind == "q" else P_KG
                            b = kn_i[0] % 2
                            kn_i[0] += 1
                            for hf, bk in enumerate(banks):
                                PE.op(lambda hf=hf: nc.tensor.matmul(psb[4 + hf][:, :], lhsT=onesq[:], rhs=sq[hf][:],
                                                                     start=True, stop=True),
                                      reads=[r_sq[hf], r_const], writes=[r_ps[4 + hf]])
                                ACT.op(lambda hf=hf: nc.scalar.activation(out=sd[hf][:], in_=psb[4 + hf][:, :],
                                                                          func=AF.Sqrt, bias=NORM_EPS),
                                       writes=[r_ps[4 + hf], r_sd[hf]])
                                DVE.op(lambda hf=hf: nc.vector.reciprocal(out=rs[hf][:], in_=sd[hf][:]),
                                       reads=[r_sd[hf]], writes=[r_rs2[hf]])
                                DVE.op(lambda hf=hf, rw=rw, b=b, gcol=gcol: nc.vector.scalar_tensor_tensor(
                                    out=kn[b][:, 512 * hf:512 * hf + 512], in0=raw[rw][hf][:],
                                    scalar=prm[:, gcol:gcol + 1], in1=rs[hf][:], op0=ALU.mult, op1=ALU.mult),
                                    reads=[r_rs2[hf], r_prm, r_raw[rw][hf]], writes=[r_kn[b]])
                            if kind == "k":
                                d_s.dma(SP, kT_win[h, :, w0:w0 + TOK], kn[b][:], reads=[r_kn[b]])
                            else:
                                d_s.dma(SP, qT_d[h, :, :], kn[b][:], reads=[r_kn[b]])
                        elif kind == "v":
                            pt = psb[6][:].bitcast(BF16)

                            def tr(pt=pt):
                                ins = None
                                for t in range(8):
                                    ins = nc.tensor.transpose(pt[:, t * 128:(t + 1) * 128], vT[:, t * 128:(t + 1) * 128],
                                                              ident[:])
                                return ins
                            PE.op(tr, reads=[r_vT, r_const], writes=[r_ps[6]])
                            hh = h % 4
                            DVE.op(lambda pt=pt, hh=hh: nc.vector.tensor_copy(
                                out=vst[:, :, hh * 128:(hh + 1) * 128],
                                in_=pt[:, 0:1024].rearrange("p (t d) -> p t d", d=128)),
                                writes=[r_ps[6], r_vst])
                            if hh == 3:
                                g = h // 4
                                d_s.dma(SP, v_win[w0:w0 + TOK, g * 512:(g + 1) * 512].rearrange("(t p) n -> p t n", p=128),
                                        vst[:], reads=[r_vst])

                    for i in range(first_rest, len(cols)):
                        kind, h, j = cols[i]
                        load_w(i + 3)
                        banks = main_mm(i)
                        for fn in pending:
                            fn()
                        pending = []
                        rw = raw_i[0] % 2
                        if kind in ("q", "k"):
                            raw_i[0] += 1
                            for hf, bk in enumerate(banks):
                                ACT.op(lambda bk=bk, hf=hf: nc.scalar.activation(out=sq[hf][:], in_=psb[bk][:, :],
                                                                                 func=AF.Square),
                                       writes=[r_ps[bk], r_sq[hf]])
                                ACT.op(lambda bk=bk, hf=hf, rw=rw: nc.scalar.activation(out=raw[rw][hf][:], in_=psb[bk][:, :],
                                                                                        func=AF.Copy),
                                       writes=[r_ps[bk], r_raw[rw][hf]])
                        elif kind == "v":
                            for hf, bk in enumerate(banks):
                                ACT.op(lambda bk=bk, hf=hf: nc.scalar.activation(out=vT[:, 512 * hf:512 * hf + 512],
                                                                                 in_=psb[bk][:, :], func=AF.Copy),
                                       writes=[r_ps[bk], r_vT])
                        elif kind == "ag":
                            b = kn_i[0] % 2
                            kn_i[0] += 1
                            for hf, bk in enumerate(banks):
                                ACT.op(lambda bk=bk, hf=hf, b=b: nc.scalar.activation(
                                    out=kn[b][:, 512 * hf:512 * hf + 512], in_=psb[bk][:, :], func=AF.Silu),
                                    writes=[r_ps[bk], r_kn[b]])
                            d_s.dma(SP, gT_d[h, :, :], kn[b][:], reads=[r_kn[b]])
                        if kind in ("q", "k", "v"):
                            pending.append(lambda kind=kind, h=h, banks=banks, rw=rw: stage2(kind, h, banks, rw))
                    for fn in pending:
                        fn()
                    barrier()

        mixA = T(es, "mixA", [128, 16, TOK], BF16)
        r_mixA = Res()
        with ExitStack() as as_:
            EB = T(as_, "EB", [128, 48, 256], BF16)
            msk = T(as_, "msk", [128, 256], F32)
            r_EB, r_msk = Res(), Res()
            d_l.dma(SP, msk[:], maskd[:, :], writes=[r_msk])
            with ExitStack() as bs_:
                bst = T(bs_, "bst", [128, 16, 256], F32)
                r_bst = Res()
                for p in range(3):
                    d_l.dma(SP, bst[:], biasT[:, 16 * p:16 * p + 16, :], writes=[r_bst])
                    ACT.op(lambda: nc.scalar.activation(out=bst[:], in_=bst[:], func=AF.Exp),
                           reads=[r_bst], writes=[r_bst])
                    for h in range(16):
                        DVE.op(lambda p=p, h=h: nc.vector.tensor_tensor(out=EB[:, 16 * p + h, :], in0=bst[:, h, :],
                                                                        in1=msk[:], op=ALU.mult),
                               reads=[r_bst, r_msk], writes=[r_EB])
                barrier()
            DEPTH = 3
            NR = 4
            kTs = [T(as_, f"kTs{i}", [128, WIN], BF16) for i in range(2)]
            qs = [T(as_, f"qs{i}", [128, TOK], BF16) for i in range(2)]
            gs = [T(as_, f"gs{i}", [128, TOK], BF16) for i in range(2)]
            v1 = [T(as_, f"v1{i}", [128, 9, 128], BF16) for i in range(2)]
            v2 = [T(as_, f"v2{i}", [128, 4, 3, 128], BF16) for i in range(2)]
            v3a = [T(as_, f"v3a{i}", [128, 16, 128], BF16) for i in range(2)]
            v3b = [T(as_, f"v3b{i}", [64, 16, 128], BF16) for i in range(2)]
            esb = [T(as_, f"esb{i}", [128, 512], BF16) for i in range(NR)]
            psb_ = [T(as_, f"pp{i}", [128, 512], BF16) for i in range(NR)]
            rz = T(as_, "rz", [128, TOK], F32)
            at = T(as_, "at", [128, TOK], F32)
            r_kTs, r_qs, r_gs2 = [Res(), Res()], [Res(), Res()], [Res(), Res()]
            r_v = [[Res() for _ in range(4)] for _ in range(2)]
            r_esb, r_pp = [Res() for _ in range(NR)], [Res() for _ in range(NR)]
            r_rz, r_at = Res(), Res()
            NB0, ZB0, SB0 = 0, 2, 4
            scale = 1.0 / np.sqrt(128.0)

            def load_head(h):
                if h >= NHEAD:
                    return
                sl = h % 2
                vc = slice(128 * h, 128 * h + 128)
                d_l.dma(SP, kTs[sl][:], kT_win[h, :, :], writes=[r_kTs[sl]])
                d_l.dma(SP, qs[sl][:], qT_d[h, :, :], writes=[r_qs[sl]])
                d_l.dma(SP, v1[sl][:], v_win[960:960 + 1152, vc].rearrange("(c p) n -> p c n", p=128),
                        writes=[r_v[sl][0]])
                d_l.dma(SP, v2[sl][:], v_win[768:2304, vc].rearrange("(c p r) n -> p r c n", p=128, r=4),
                        writes=[r_v[sl][1]])
                d_l.dma(SP, v3a[sl][:], v_win[0:2048, vc].rearrange("(p r) n -> p r n", r=16),
                        writes=[r_v[sl][2]])
                d_l.dma(SP, v3b[sl][:], v_win[2048:3072, vc].rearrange("(p r) n -> p r n", r=16),
                        writes=[r_v[sl][3]])
                d_l.dma(SP, gs[sl][:], gT_d[h, :, :], writes=[r_gs2[sl]])

            unit_defs = [[c] for c in range(9)]
            unit_defs += [[9 + r * 3 + 0 for r in range(4)]]
            unit_defs += [[9 + r * 3 + 1 for r in (0, 1)], [9 + r * 3 + 1 for r in (2, 3)]]
            unit_defs += [[9 + r * 3 + 2 for r in range(4)]]
            unit_defs += [[21 + r for r in range(8)], [21 + r for r in range(8, 16)]]
            unit_defs += [[37 + r for r in range(8)], [37 + r for r in range(8, 16)]]
            NU = len(unit_defs)
            units = [(h, ui) for h in range(NHEAD) for ui in range(NU)]
            started = {}

            def first_touch(key):
                st_ = not started.get(key, False)
                started[key] = True
                return st_

            def stage_ab(u):
                h, ui = units[u]
                members = [blks[bi] for bi in unit_defs[ui]]
                b0 = members[0]
                sl = h % 2
                nk, nq, q0, p = b0["nk"], b0["nq"], b0["q0"], b0["p"]
                R = len(members)
                sb = SB0 + u % NR
                eb = u % NR

                def st():
                    ins = None
                    for i, b in enumerate(members):
                        ins = nc.tensor.matmul(psb[sb][0:nk, i * nq:(i + 1) * nq], lhsT=kTs[sl][:, b["kslice"]],
                                               rhs=qs[sl][:, b["qslice"]], start=True, stop=True,
                                               skip_group_check=True)
                    return ins
                PE.op(st, reads=[r_kTs[sl], r_qs[sl]], writes=[r_ps[sb]])
                ACT.op(lambda: nc.scalar.activation(
                    out=esb[eb][0:nk, 0:R * nq], in_=psb[sb][0:nk, 0:R * nq], func=AF.Exp, scale=float(scale)),
                    writes=[r_ps[sb], r_esb[eb]])
                kvc = P_KV + unit_defs[ui][0]
                ebv = EB[0:nk, 16 * p + h, q0:q0 + nq]
                if R == 1:
                    o_ap, i_ap, e_ap = psb_[eb][0:nk, 0:nq], esb[eb][0:nk, 0:nq], ebv
                else:
                    o_ap = psb_[eb][0:nk, 0:R * nq].rearrange("k (r q) -> k r q", q=nq)
                    i_ap = esb[eb][0:nk, 0:R * nq].rearrange("k (r q) -> k r q", q=nq)
                    e_ap = ebv.unsqueeze(1).broadcast_to([nk, R, nq])
                DVE.op(lambda: nc.vector.scalar_tensor_tensor(
                    out=o_ap, in0=i_ap, scalar=prm[0:nk, kvc:kvc + 1], in1=e_ap,
                    op0=ALU.mult, op1=ALU.mult), reads=[r_esb[eb], r_EB, r_prm], writes=[r_pp[eb]])

            def stage_c(u):
                h, ui = units[u]
                members = [blks[bi] for bi in unit_defs[ui]]
                b0 = members[0]
                sl = h % 2
                nk, nq = b0["nk"], b0["nq"]
                R = len(members)
                eb = u % NR
                step = b0["qslice"].step
                jobs = []
                rvs = []
                for i, b in enumerate(members):
                    vk = b["v"]
                    if vk[0] == "v1":
                        vap, rv = v1[sl][:, vk[1], :], r_v[sl][0]
                    elif vk[0] == "v2":
                        vap, rv = v2[sl][:, vk[1], vk[2], :], r_v[sl][1]
                    elif vk[0] == "v3a":
                        vap, rv = v3a[sl][:, vk[1], :], r_v[sl][2]
                    else:
                        vap, rv = v3b[sl][:, vk[1], :], r_v[sl][3]
                    rvs.append(rv)
                    qsl = b["qslice"]
                    qi = 0
                    while qi < nq:
                        t0 = qsl.start + qsl.step * qi
                        bank = t0 // 512
                        n_in = min(nq - qi, (512 * (bank + 1) - t0 + qsl.step - 1) // qsl.step)
                        if b["p"] == 0:
                            n_in = min(n_in, 128)
                        c0 = t0 - 512 * bank
                        osl = slice(c0, c0 + qsl.step * (n_in - 1) + 1, qsl.step)
                        rhs = psb_[eb][0:nk, i * nq + qi:i * nq + qi + n_in]
                        jobs.append(("N", bank, psb[NB0 + bank][:, osl], vap, rhs))
                        if R == 1:
                            jobs.append(("Z", bank, psb[ZB0 + bank][:, osl], ones1[0:nk, :], rhs))
                        qi += n_in
                if R > 1:
                    tstart = b0["qslice"].start - (b0["qslice"].start % step)
                    r0 = b0["qslice"].start % step
                    qa = 0
                    while qa < nq:
                        bank = (tstart + step * qa) // 512
                        qb = min(nq, (512 * (bank + 1) - tstart) // step)
                        c0 = tstart + step * qa - 512 * bank
                        o_ap = psb[ZB0 + bank][:, :].rearrange("z (m r) -> z r m", r=step)[
                            :, r0:r0 + R, c0 // step:c0 // step + (qb - qa)]
                        rhs = psb_[eb][0:nk, 0:R * nq].rearrange("k (r q) -> k r q", q=nq)[:, :, qa:qb]
                        jobs.append(("Z", bank, o_ap, ones1[0:nk, :], rhs))
                        qa = qb

                def pv():
                    ins = None
                    for (kind, bank, o_ap, lhsT, rhs) in jobs:
                        st_ = first_touch((h, kind, bank))
                        ins = nc.tensor.matmul(o_ap, lhsT=lhsT, rhs=rhs, start=st_, stop=False, skip_group_check=True)
                    return ins
                banks = sorted(set(j[1] for j in jobs))
                PE.op(pv, reads=[r_pp[eb], r_const] + list(set(rvs)),
                      writes=[r_ps[NB0 + bk] for bk in banks] + [r_ps[ZB0 + bk] for bk in banks])
                if ui == NU - 1:
                    for bank in range(2):
                        hs = slice(512 * bank, 512 * bank + 512)
                        DVE.op(lambda bank=bank, hs=hs: nc.vector.reciprocal(out=rz[:, hs], in_=psb[ZB0 + bank][:, :]),
                               writes=[r_ps[ZB0 + bank], r_rz])
                        DVE.op(lambda bank=bank, hs=hs: nc.vector.tensor_tensor(out=at[:, hs], in0=psb[NB0 + bank][:, :],
                                                                                in1=rz[:, hs], op=ALU.mult),
                               reads=[r_rz], writes=[r_ps[NB0 + bank], r_at])
                        DVE.op(lambda hs=hs: nc.vector.tensor_tensor(out=mixA[:, h, hs], in0=at[:, hs],
                                                                     in1=gs[sl][:, hs], op=ALU.mult),
                               reads=[r_at, r_gs2[sl]], writes=[r_mixA])
                    load_head(h + 2)

            load_head(0)
            load_head(1)
            for u in range(len(units) + DEPTH):
                if u < len(units):
                    stage_ab(u)
                if u - DEPTH >= 0:
                    stage_c(u - DEPTH)
            barrier()

        if debug:
            d_s.dma(SP, mix_d[0:16, :, :].rearrange("c p t -> p c t"), mixC[:], reads=[r_mixC])
            d_s.dma(SP, mix_d[16:32, :, :].rearrange("c p t -> p c t"), mixA[:], reads=[r_mixA])

        with ExitStack() as os_:
            wo = T(os_, "wo", [128, 2, NCH, 512], BF16)
            r_wo = [Res(), Res()]
            xp = [T(os_, f"xp{i}", [128, 512], F32) for i in range(4)]
            yst = [T(os_, f"yst{i}", [128, 512], F32) for i in range(4)]
            r_xp = [Res() for _ in range(4)]
            r_y = [Res() for _ in range(4)]

            def load_wo(n):
                if n < 8:
                    d_w.dma(POOL, wo[:, n % 2], w_out_v[:, :, 512 * n:512 * n + 512], writes=[r_wo[n % 2]])
            load_wo(0)
            u = 0
            out_evs = []
            for n in range(8):
                load_wo(n + 1)
                for i in range(8):
                    bk = u % 4
                    sl = u % 4
                    u += 1
                    d_x.dma(SP, xp[sl][:], xw[1024 + 128 * i:1024 + 128 * i + 128, 512 * n:512 * n + 512],
                            writes=[r_xp[sl]])

                    def mm(bk=bk, i=i, n=n):
                        ins = None
                        for e in range(NCH):
                            src = mixC if e < 16 else mixA
                            ins = nc.tensor.matmul(psb[bk][:, :], lhsT=src[:, e % 16, 128 * i:128 * i + 128],
                                                   rhs=wo[:, n % 2, e, :], start=(e == 0), stop=(e == NCH - 1))
                        return ins
                    PE.op(mm, reads=[r_wo[n % 2], r_mixC, r_mixA], writes=[r_ps[bk]])
                    DVE.op(lambda bk=bk, sl=sl: nc.vector.tensor_tensor(out=yst[sl][:], in0=psb[bk][:, :], in1=xp[sl][:],
                                                                        op=ALU.add),
                           reads=[r_xp[sl]], writes=[r_ps[bk], r_y[sl]])
                    out_evs.append(d_o.dma(SP, y[128 * i:128 * i + 128, 512 * n:512 * n + 512], yst[sl][:],
                                           reads=[r_y[sl]]))
            for ev in d_o.events():
                SP.wait(ev)
            barrier()
    return nc


def _t5_bucket(rel):
    import math
    half, exact = 16, 8
    rel = np.asarray(rel, dtype=np.int32)
    n = np.abs(rel)
    nf = np.maximum(n, 1).astype(np.float32)
    large = exact + (np.log(nf / np.float32(exact)) / np.float32(math.log(1024 / exact))
                     * np.float32(half - exact)).astype(np.int32)
    large = np.minimum(large, half - 1)
    return np.where(rel > 0, half, 0) + np.where(n < exact, n, large)


_CACHE = {}


def _prep(inputs, debug=False):
    x = np.asarray(inputs["x"], dtype=np.float32)[0]
    f = lambda k: np.asarray(inputs[k], dtype=np.float32)
    xpad = np.zeros((S + 2048, D), np.float32)
    xpad[1024:1024 + S] = x
    w_in = np.ascontiguousarray(f("w_in")[0])
    w_out = np.ascontiguousarray(f("w_out")[0])
    prm = np.zeros((128, NP), np.float32)
    prm[:, P_G:P_G + 32] = f("norm_g")[0].reshape(32, 128).T
    prm[:, P_QG] = f("q_norm_g")[0]
    prm[:, P_KG] = f("k_norm_g")[0]
    cw = f("conv_w")[0]
    prm[:, P_CW:P_CW + 496] = cw.reshape(31, 16, 128).transpose(2, 1, 0).reshape(128, 496)
    prm[:, P_CB:P_CB + 16] = f("conv_b")[0].reshape(16, 128).T
    prm[:, P_LG:P_LG + 16] = f("conv_ln_g")[0].reshape(16, 128).T
    prm[:, P_LB:P_LB + 16] = f("conv_ln_b")[0].reshape(16, 128).T
    rb = f("rel_bias")
    kk = np.arange(128)[:, None]
    qq = np.arange(256)[None, :]
    rel = 64 + kk - qq
    biasT = np.zeros((128, 48, 256), np.float32)
    for p, (_, dil) in enumerate(PATTERNS):
        bucket = _t5_bucket(rel * dil)
        biasT[:, 16 * p:16 * p + 16, :] = rb[bucket].transpose(0, 2, 1)
    mask = ((qq - kk >= 0) & (qq - kk <= 128)).astype(np.float32)
    blks = attn_blocks()
    in_maps = []
    for c in range(NCORE):
        pc = prm.copy()
        for bi, b in enumerate(blks):
            wk = np.array([key_window_index(b, k) if k < b["nk"] else -10 ** 6 for k in range(128)])
            g = 1024 * c - 1024 + wk
            pc[:, P_KV + bi] = ((g >= 0) & (g < S) & (wk >= 0)).astype(np.float32)
        in_maps.append({"xw": np.ascontiguousarray(xpad[1024 * c:1024 * c + WIN]), "w_in": w_in, "w_out": w_out,
                        "params": pc, "biasT": biasT, "mask": mask})
    return in_maps


def kernel(**inputs):
    in_maps = _prep(inputs)
    if "nc" not in _CACHE:
        _CACHE["nc"] = build_program()
    res = run_bass_kernel_spmd(_CACHE["nc"], in_maps, core_ids=list(range(NCORE)))
    out = np.concatenate([np.asarray(r["y"], dtype=np.float32) for r in res.results], axis=0)
    return out[None]
```

```python
import numpy as np
import ml_dtypes
from contextlib import ExitStack

import concourse.bass as bass
import concourse.mybir as mybir
from concourse.bass_utils import run_bass_kernel_spmd

F32 = mybir.dt.float32
BF16 = mybir.dt.bfloat16
AF = mybir.ActivationFunctionType
ALU = mybir.AluOpType

D = 4096
S = 8192
NCORE = 8
TOK = 1024
WIN = 3072
NCH = 32
INW = 14336
NHEAD = 16
NBLK = 53
NP = 640
P_G, P_QG, P_KG, P_CW, P_CB, P_LG, P_LB, P_KV = 0, 32, 33, 34, 530, 546, 562, 578
NORM_EPS = 1e-6
LN_EPS = 1e-5
PATTERNS = ((128, 1), (512, 4), (2048, 16))


class Sem:
    def __init__(s, h):
        s.h = h
        s.v = 0


class Res:
    def __init__(s):
        s.w = None
        s.r = {}
        s.extra = []


class Eng:
    def __init__(s, nc, h, name, es):
        s.h = h
        s.sem = Sem(es.enter_context(nc.semaphore("e_" + name)))
        s.seen = {}

    def wait(s, ev):
        if ev is None:
            return
        sem, v = ev
        if v <= 0 or s.seen.get(sem, 0) >= v:
            return
        s.h.wait_ge(sem.h, v)
        s.seen[sem] = v

    def pre(s, reads=(), writes=(), own_ok=()):
        for r in reads:
            if r in own_ok and r.w is not None and r.w[0] is s.sem:
                continue
            s.wait(r.w)
            for ev in r.extra:
                s.wait(ev)
        for w in writes:
            if w.w is not None and w.w[0] is not s.sem:
                s.wait(w.w)
            for sem, v in list(w.r.items()):
                if sem is not s.sem:
                    s.wait((sem, v))

    def post(s, ins, reads=(), writes=()):
        s.sem.v += 1
        ins.then_inc(s.sem.h, 1)
        ev = (s.sem, s.sem.v)
        for r in reads:
            r.r[ev[0]] = ev[1]
        for w in writes:
            w.w = ev
            w.r = {}
            w.extra = []
        return ev

    def op(s, fn, reads=(), writes=(), own_ok=()):
        s.pre(reads, writes, own_ok)
        return s.post(fn(), reads, writes)


class DmaPool:
    def __init__(s, nc, es, name, k):
        s.sems = [Sem(es.enter_context(nc.semaphore(f"d_{name}{i}"))) for i in range(k)]
        s.last = [None] * k
        s.i = 0

    def dma(s, q, out, in_, reads=(), writes=()):
        k = s.i % len(s.sems)
        s.i += 1
        sem = s.sems[k]
        q.wait(s.last[k])
        q.pre(reads, writes)
        ins = q.h.dma_start(out=out, in_=in_)
        sem.v += 16
        ins.then_inc(sem.h, 16)
        ev = (sem, sem.v)
        s.last[k] = ev
        for r in reads:
            r.r[sem] = sem.v
        for w in writes:
            w.w = ev
            w.r = {}
            w.extra = []
        return ev

    def events(s):
        return [e for e in s.last if e is not None]


def attn_blocks():
    blks = []
    for c in range(9):
        tlo, thi = max(0, 128 * (c - 1)), min(TOK, 128 * (c + 1))
        blks.append(dict(p=0, nk=128, kslice=slice(960 + 128 * c, 960 + 128 * c + 128, 1),
                         qslice=slice(tlo, thi, 1), q0=tlo - 128 * (c - 1), nq=thi - tlo,
                         v=("v1", c)))
    for r in range(4):
        for c in range(3):
            llo, lhi = max(256, 128 + 128 * c), min(512, 384 + 128 * c)
            k0 = 4 * (192 + 128 * c) + r
            blks.append(dict(p=1, nk=128, kslice=slice(k0, k0 + 512, 4),
                             qslice=slice(4 * (llo - 256) + r, 4 * (lhi - 256), 4),
                             q0=llo - (128 + 128 * c), nq=lhi - llo, v=("v2", r, c)))
    for r in range(16):
        blks.append(dict(p=2, nk=128, kslice=slice(r, 2048, 16), qslice=slice(r, TOK, 16),
                         q0=128, nq=64, v=("v3a", r)))
    for r in range(16):
        blks.append(dict(p=2, nk=64, kslice=slice(2048 + r, WIN, 16), qslice=slice(r, TOK, 16),
                         q0=0, nq=64, v=("v3b", r)))
    assert len(blks) == NBLK
    return blks


def key_window_index(b, kk):
    s = b["kslice"]
    return s.start + s.step * kk


def build_program(debug=False):
    nc = bass.Bass("TRN2", target_bir_lowering=False)
    dkind = "ExternalOutput" if debug else "Internal"
    xw = nc.dram_tensor("xw", [WIN, D], F32, kind="ExternalInput").ap()
    w_in = nc.dram_tensor("w_in", [D, INW], F32, kind="ExternalInput").ap()
    w_out = nc.dram_tensor("w_out", [D, D], F32, kind="ExternalInput").ap()
    params = nc.dram_tensor("params", [128, NP], F32, kind="ExternalInput").ap()
    biasT = nc.dram_tensor("biasT", [128, 48, 256], F32, kind="ExternalInput").ap()
    maskd = nc.dram_tensor("mask", [128, 256], F32, kind="ExternalInput").ap()
    y = nc.dram_tensor("y", [TOK, D], F32, kind="ExternalOutput").ap()
    kT_win = nc.dram_tensor("kT_win", [NHEAD, 128, WIN], BF16, kind=dkind).ap()
    v_win = nc.dram_tensor("v_win", [WIN, 2048], BF16, kind=dkind).ap()
    qT_d = nc.dram_tensor("qT_d", [NHEAD, 128, TOK], BF16, kind=dkind).ap()
    gT_d = nc.dram_tensor("gT_d", [NHEAD, 128, TOK], BF16, kind=dkind).ap()
    if debug:
        mix_d = nc.dram_tensor("mix_d", [32, 128, TOK], BF16, kind="ExternalOutput").ap()

    w_in_v = w_in.rearrange("(c p) n -> p c n", p=128)
    w_out_v = w_out.rearrange("(c p) n -> p c n", p=128)
    blks = attn_blocks()

    with ExitStack() as es:
        PE = Eng(nc, nc.tensor, "pe", es)
        ACT = Eng(nc, nc.scalar, "act", es)
        DVE = Eng(nc, nc.vector, "dve", es)
        POOL = Eng(nc, nc.gpsimd, "pool", es)
        SP = Eng(nc, nc.sync, "sp", es)
        engines = [PE, ACT, DVE, POOL, SP]
        d_w = DmaPool(nc, es, "w", 16)
        d_x = DmaPool(nc, es, "x", 4)
        d_s = DmaPool(nc, es, "s", 8)
        d_l = DmaPool(nc, es, "l", 8)
        d_o = DmaPool(nc, es, "o", 8)
        pools = [d_w, d_x, d_s, d_l, d_o]

        def barrier():
            evs = [(e.sem, e.sem.v) for e in engines]
            for p in pools:
                evs += p.events()
            for e in engines:
                for ev in evs:
                    if ev[0] is not e.sem:
                        e.wait(ev)

        uid = [0]

        def T(stack, name, shape, dt):
            uid[0] += 1
            return stack.enter_context(nc.sbuf_tensor(f"{name}_{uid[0]}", shape, dt))

        prm = T(es, "prm", [128, NP], F32)
        ident = T(es, "ident", [128, 128], BF16)
        ones1 = T(es, "ones1", [128, 128], BF16)
        onesq = T(es, "onesq", [128, 128], BF16)
        onesl = T(es, "onesl", [128, 128], BF16)
        mixC = T(es, "mixC", [128, 16, TOK], BF16)
        r_prm, r_const, r_mixC = Res(), Res(), Res()
        psb = [es.enter_context(nc.psum_tensor(f"psb{i}", [128, 512], F32)) for i in range(8)]
        r_ps = [Res() for _ in range(8)]

        d_s.dma(SP, prm[:], params[:, :], writes=[r_prm])
        POOL.op(lambda: nc.gpsimd.memset(ident[:], 0.0), writes=[r_const])
        POOL.op(lambda: nc.gpsimd.affine_select(out=ident[:], in_=ident[:], pattern=[[-1, 128]],
                                                compare_op=ALU.not_equal, fill=1.0, base=0,
                                                channel_multiplier=1), reads=[r_const], writes=[r_const])
        POOL.op(lambda: nc.gpsimd.memset(ones1[:], 1.0), writes=[r_const])
        POOL.op(lambda: nc.gpsimd.memset(onesq[:], 1.0 / 128), writes=[r_const])
        POOL.op(lambda: nc.gpsimd.memset(onesl[:], 1.0 / 2048), writes=[r_const])

        passes = [("L", 0), ("O", 1024), ("R", 2048)]
        for pname, w0 in passes:
            own = pname == "O"
            with ExitStack() as ps_:
                xnT = T(ps_, "xnT", [128, NCH, 1056], BF16)
                r_xnA, r_xnD = Res(), Res()
                wr = T(ps_, "wr", [128, 4, NCH, 128], BF16)
                r_w = [Res() for _ in range(4)]

                if own:
                    cols = []
                    for cb in range(16):
                        cols += [("glu", cb, 16 + cb), ("val", cb, cb)]
                    cols += [("gate", cb, 32 + cb) for cb in range(16)]
                    cols += [("ag", h, 96 + h) for h in range(16)]
                    cols += [("v", h, 80 + h) for h in range(16)]
                    cols += [("q", h, 48 + h) for h in range(16)]
                    cols += [("k", h, 64 + h) for h in range(16)]
                else:
                    cols = [("v", h, 80 + h) for h in range(16)] + [("k", h, 64 + h) for h in range(16)]

                def load_w(i):
                    if i < len(cols):
                        j = cols[i][2]
                        for g in range(4):
                            d_w.dma(POOL, wr[:, i % 4, 8 * g:8 * g + 8, :], w_in_v[:, 8 * g:8 * g + 8, 128 * j:128 * j + 128],
                                    writes=([r_w[i % 4]] if g == 0 else []))
                            if g > 0:
                                r_w[i % 4].extra.append(d_w.last[(d_w.i - 1) % len(d_w.sems)])

                for i in range(3):
                    load_w(i)

                with ExitStack() as ns_:
                    xs = [T(ns_, f"xs{i}", [128, D], F32) for i in range(2)]
                    xb = [T(ns_, f"xb{i}", [128, D], BF16) for i in range(2)]
                    st = [T(ns_, f"st{i}", [128, 4], F32) for i in range(2)]
                    r_xs = [Res(), Res()]
                    r_xb = [Res(), Res()]
                    r_st = [Res(), Res()]
                    r_junk = Res()
                    tiles = [(w0 + 128 * i, 128, 128 * i) for i in range(8)]
                    if own:
                        tiles.append((None, 32, 1024))
                    bank_i = 0
                    for ti, (row, npart, col0) in enumerate(tiles):
                        sl = ti % 2
                        if row is not None:
                            d_x.dma(SP, xs[sl][:], xw[row:row + 128, :], writes=[r_xs[sl]])
                        else:
                            d_x.dma(SP, xs[sl][0:16, :], xw[1008:1024, :], writes=[r_xs[sl]])
                            d_x.dma(SP, xs[sl][16:32, :], xw[2048:2064, :], writes=[])
                            ev2 = d_x.last[(d_x.i - 1) % 4]
                            ACT.wait(ev2)
                            DVE.wait(ev2)
                        ACT.op(lambda sl=sl, n=npart: nc.scalar.activation(
                            out=xb[sl][0:n, :], in_=xs[sl][0:n, :], func=AF.Square,
                            accum_out=st[sl][0:n, 0:1]), reads=[r_xs[sl]], writes=[r_xb[sl], r_st[sl]])
                        ACT.op(lambda sl=sl, n=npart: nc.scalar.activation(
                            out=st[sl][0:n, 1:2], in_=st[sl][0:n, 0:1], func=AF.Sqrt,
                            scale=1.0 / D, bias=NORM_EPS), reads=[r_st[sl]], writes=[r_st[sl]])
                        DVE.op(lambda sl=sl, n=npart: nc.vector.reciprocal(
                            out=st[sl][0:n, 2:3], in_=st[sl][0:n, 1:2]), reads=[r_st[sl]], writes=[r_st[sl]])
                        DVE.op(lambda sl=sl, n=npart: nc.vector.tensor_scalar(
                            out=xb[sl][0:n, :], in0=xs[sl][0:n, :], scalar1=st[sl][0:n, 2:3],
                            scalar2=None, op0=ALU.mult), reads=[r_xs[sl], r_st[sl]], writes=[r_xb[sl]])
                        for grp in range(4):
                            bk = bank_i % 4
                            bank_i += 1
                            pt = psb[bk][:].bitcast(BF16)

                            def tr(sl=sl, n=npart, grp=grp, pt=pt):
                                ins = None
                                for k in range(8):
                                    c = grp * 8 + k
                                    ins = nc.tensor.transpose(pt[:, k * 128:k * 128 + n],
                                                              xb[sl][0:n, c * 128:(c + 1) * 128],
                                                              ident[0:n, 0:n])
                                return ins
                            PE.op(tr, reads=[r_xb[sl], r_const], writes=[r_ps[bk]])
                            useA = (grp % 2 == 0)
                            E = ACT if useA else DVE
                            rx = r_xnA if useA else r_xnD
                            for k in range(8):
                                c = grp * 8 + k
                                if useA:
                                    fn = (lambda c=c, k=k, n=npart, pt=pt, col0=col0: nc.scalar.mul(
                                        out=xnT[:, c, col0:col0 + n], in_=pt[:, k * 128:k * 128 + n],
                                        mul=prm[:, P_G + c:P_G + c + 1]))
                                else:
                                    fn = (lambda c=c, k=k, n=npart, pt=pt, col0=col0: nc.vector.tensor_scalar(
                                        out=xnT[:, c, col0:col0 + n], in0=pt[:, k * 128:k * 128 + n],
                                        scalar1=prm[:, P_G + c:P_G + c + 1], scalar2=None, op0=ALU.mult))
                                E.op(fn, reads=[r_prm], writes=[r_ps[bk], rx])
                    barrier()

                ntok_groups = [(0, 512), (512, 1024)]
                main_i = [0]
                main_nb = [3]

                def main_mm(i, halo_cols=None):
                    slot = i % 4
                    out = []
                    for (c0, c1) in ntok_groups:
                        bk = main_i[0] % main_nb[0]
                        main_i[0] += 1

                        def mm(bk=bk, c0=c0, c1=c1, slot=slot):
                            ins = None
                            for c in range(NCH):
                                ins = nc.tensor.matmul(psb[bk][:, 0:c1 - c0], lhsT=wr[:, slot, c, :],
                                                       rhs=xnT[:, c, c0:c1], start=(c == 0), stop=(c == NCH - 1))
                            return ins
                        PE.op(mm, reads=[r_w[slot], r_xnA, r_xnD], writes=[r_ps[bk]])
                        out.append(bk)
                    if halo_cols is not None:
                        def mmh(slot=slot, hc=halo_cols):
                            ins = None
                            for c in range(NCH):
                                ins = nc.tensor.matmul(psb[3][:, hc:hc + 32], lhsT=wr[:, slot, c, :],
                                                       rhs=xnT[:, c, 1024:1056], start=(c == 0), stop=(c == NCH - 1))
                            return ins
                        PE.op(mmh, reads=[r_w[slot], r_xnA, r_xnD], writes=[r_ps[3]])
                    return out

                if own:
                    r_mixCb = [Res() for _ in range(16)]
                    with ExitStack() as cs_:
                        sig = T(cs_, "sig", [128, 1056], F32)
                        valsb = T(cs_, "valsb", [128, 1056], F32)
                        aext = [T(cs_, f"aext{i}", [128, 1056], F32) for i in range(2)]
                        accA = T(cs_, "accA", [128, TOK], F32)
                        accB = T(cs_, "accB", [128, TOK], F32)
                        accC = [T(cs_, f"accC{i}", [128, TOK], F32) for i in range(2)]
                        tmpP = T(cs_, "tmpP", [128, TOK], F32)
                        csq = [T(cs_, f"csq{i}", [128, TOK], BF16) for i in range(3)]
                        r_sig, r_valsb, r_accA, r_accB, r_tmpP = Res(), Res(), Res(), Res(), Res()
                        r_aext = [Res(), Res()]
                        r_accC = [Res(), Res()]
                        r_csq = [Res(), Res(), Res()]
                        NDV = 31
                        pend_stats = []

                        def emit_stats(cb):
                            ACT.op(lambda cb=cb: nc.scalar.activation(out=csq[cb % 3][:], in_=mixC[:, cb, :],
                                                                      func=AF.Square),
                                   reads=[r_mixCb[cb]], writes=[r_csq[cb % 3]])
                            for hf in range(2):
                                def stat(hf=hf, cb=cb):
                                    nc.tensor.matmul(psb[4 + hf][:, :], lhsT=onesl[:], rhs=mixC[:, cb, 512 * hf:512 * hf + 512],
                                                     start=(cb == 0), stop=(cb == 15), skip_group_check=True)
                                    return nc.tensor.matmul(psb[6 + hf][:, :], lhsT=onesl[:], rhs=csq[cb % 3][:, 512 * hf:512 * hf + 512],
                                                            start=(cb == 0), stop=(cb == 15), skip_group_check=True)
                                PE.op(stat, reads=[r_mixCb[cb], r_csq[cb % 3], r_const], writes=[r_ps[4 + hf], r_ps[6 + hf]])

                        for i in range(32):
                            kind, cb, j = cols[i]
                            load_w(i + 3)
                            banks = main_mm(i, halo_cols=(0 if kind == "glu" else 32))
                            if kind == "glu":
                                while pend_stats and pend_stats[0] <= cb - 2:
                                    emit_stats(pend_stats.pop(0))
                                for hf, bk in enumerate(banks):
                                    ACT.op(lambda bk=bk, hf=hf: nc.scalar.activation(
                                        out=sig[:, 16 + 512 * hf:16 + 512 * hf + 512], in_=psb[bk][:, :],
                                        func=AF.Sigmoid), writes=[r_ps[bk], r_sig])
                                ACT.op(lambda: nc.scalar.activation(out=sig[:, 0:16], in_=psb[3][:, 0:16],
                                                                    func=AF.Sigmoid), writes=[r_ps[3], r_sig])
                                ACT.op(lambda: nc.scalar.activation(out=sig[:, 1040:1056], in_=psb[3][:, 16:32],
                                                                    func=AF.Sigmoid), writes=[r_ps[3], r_sig])
                            else:
                                ae, rae = aext[cb % 2], r_aext[cb % 2]
                                aC, raC = accC[cb % 2], r_accC[cb % 2]
                                for hf, bk in enumerate(banks):
                                    ACT.op(lambda bk=bk, hf=hf: nc.scalar.activation(
                                        out=valsb[:, 16 + 512 * hf:16 + 512 * hf + 512], in_=psb[bk][:, :], func=AF.Copy),
                                        writes=[r_ps[bk], r_valsb])
                                ACT.op(lambda: nc.scalar.activation(out=valsb[:, 0:16], in_=psb[3][:, 32:48], func=AF.Copy),
                                       writes=[r_ps[3], r_valsb])
                                ACT.op(lambda: nc.scalar.activation(out=valsb[:, 1040:1056], in_=psb[3][:, 48:64], func=AF.Copy),
                                       writes=[r_ps[3], r_valsb])
                                DVE.op(lambda ae=ae: nc.vector.tensor_tensor(out=ae[:], in0=valsb[:], in1=sig[:], op=ALU.mult),
                                       reads=[r_sig, r_valsb], writes=[rae])
                                cwc = P_CW + cb * 31
                                DVE.op(lambda ae=ae, cwc=cwc, cb=cb: nc.vector.tensor_scalar(
                                    out=accA[:], in0=ae[:, 1:1 + TOK], scalar1=prm[:, cwc:cwc + 1],
                                    scalar2=prm[:, P_CB + cb:P_CB + cb + 1], op0=ALU.mult, op1=ALU.add),
                                    reads=[rae, r_prm], writes=[r_accA])
                                DVE.op(lambda ae=ae, cwc=cwc: nc.vector.tensor_scalar(
                                    out=accB[:], in0=ae[:, 2:2 + TOK], scalar1=prm[:, cwc + 1:cwc + 2],
                                    scalar2=None, op0=ALU.mult), reads=[rae, r_prm], writes=[r_accB])
                                for jt in range(2, NDV):
                                    acc, racc = (accA, r_accA) if jt % 2 == 0 else (accB, r_accB)
                                    DVE.op(lambda ae=ae, jt=jt, acc=acc, cwc=cwc: nc.vector.scalar_tensor_tensor(
                                        out=acc[:], in0=ae[:, 1 + jt:1 + jt + TOK],
                                        scalar=prm[:, cwc + jt:cwc + jt + 1], in1=acc[:],
                                        op0=ALU.mult, op1=ALU.add), reads=[rae, racc], writes=[racc],
                                        own_ok=([racc] if jt >= 4 else []))
                                DVE.op(lambda cb=cb: nc.vector.tensor_tensor(
                                    out=mixC[:, cb, :], in0=accA[:], in1=accB[:], op=ALU.add),
                                    reads=[r_accA, r_accB], writes=[r_mixCb[cb]])
                                pend_stats.append(cb)
                        while pend_stats:
                            emit_stats(pend_stats.pop(0))
                        barrier()
                    with ExitStack() as cs_:
                        musb = T(cs_, "musb", [128, TOK], F32)
                        rsb = T(cs_, "rsb", [128, TOK], F32)
                        tmp1 = T(cs_, "tmp1", [128, TOK], F32)
                        tmp2 = T(cs_, "tmp2", [128, TOK], F32)
                        gsb = T(cs_, "gsb", [128, TOK], BF16)
                        r_mu, r_rs, r_t1, r_t2, r_gs = Res(), Res(), Res(), Res(), Res()
                        for hf in range(2):
                            hs = slice(512 * hf, 512 * hf + 512)
                            DVE.op(lambda hf=hf, hs=hs: nc.vector.tensor_copy(out=musb[:, hs], in_=psb[4 + hf][:, :]),
                                   writes=[r_ps[4 + hf], r_mu])
                            DVE.op(lambda hs=hs: nc.vector.tensor_tensor(out=tmp1[:, hs], in0=musb[:, hs], in1=musb[:, hs],
                                                                         op=ALU.mult), reads=[r_mu], writes=[r_t1])
                            DVE.op(lambda hf=hf, hs=hs: nc.vector.tensor_tensor(out=tmp2[:, hs], in0=psb[6 + hf][:, :],
                                                                                in1=tmp1[:, hs], op=ALU.subtract),
                                   reads=[r_t1], writes=[r_ps[6 + hf], r_t2])
                            ACT.op(lambda hs=hs: nc.scalar.activation(out=tmp1[:, hs], in_=tmp2[:, hs], func=AF.Sqrt,
                                                                      bias=LN_EPS), reads=[r_t2], writes=[r_t1])
                            DVE.op(lambda hs=hs: nc.vector.reciprocal(out=rsb[:, hs], in_=tmp1[:, hs]),
                                   reads=[r_t1], writes=[r_rs])
                        for i in range(32, 48):
                            kind, cb, j = cols[i]
                            load_w(i + 3)
                            banks = main_mm(i)
                            for hf, bk in enumerate(banks):
                                ACT.op(lambda bk=bk, hf=hf: nc.scalar.activation(
                                    out=gsb[:, 512 * hf:512 * hf + 512], in_=psb[bk][:, :], func=AF.Silu),
                                    writes=[r_ps[bk], r_gs])
                            DVE.op(lambda cb=cb: nc.vector.tensor_tensor(out=tmp1[:], in0=mixC[:, cb, :], in1=musb[:],
                                                                         op=ALU.subtract),
                                   reads=[r_mixCb[cb], r_mu], writes=[r_t1])
                            DVE.op(lambda: nc.vector.tensor_tensor(out=tmp2[:], in0=tmp1[:], in1=rsb[:], op=ALU.mult),
                                   reads=[r_t1, r_rs], writes=[r_t2])
                            ACT.op(lambda cb=cb: nc.scalar.activation(
                                out=tmp1[:], in_=tmp2[:], func=AF.Silu, scale=prm[:, P_LG + cb:P_LG + cb + 1],
                                bias=prm[:, P_LB + cb:P_LB + cb + 1]), reads=[r_t2, r_prm], writes=[r_t1])
                            DVE.op(lambda cb=cb: nc.vector.tensor_tensor(out=mixC[:, cb, :], in0=tmp1[:], in1=gsb[:],
                                                                         op=ALU.mult),
                                   reads=[r_t1, r_gs], writes=[r_mixCb[cb]])
                        barrier()
                    first_rest = 48
                else:
                    first_rest = 0

                with ExitStack() as qs_:
                    sq = [T(qs_, f"sq{i}", [128, 512], BF16) for i in range(2)]
                    sd = [T(qs_, f"sd{i}", [128, 512], F32) for i in range(2)]
                    rs = [T(qs_, f"rs{i}", [128, 512], F32) for i in range(2)]
                    kn = [T(qs_, f"kn{i}", [128, TOK], BF16) for i in range(2)]
                    raw = [[T(qs_, f"raw{i}{j}", [128, 512], F32) for j in range(2)] for i in range(2)]
                    r_raw = [[Res(), Res()], [Res(), Res()]]
                    raw_i = [0]
                    vT = T(qs_, "vT", [128, TOK], BF16)
                    vst = T(qs_, "vst", [128, 8, 512], BF16)
                    r_sq, r_sd, r_rs2 = [Res(), Res()], [Res(), Res()], [Res(), Res()]
                    r_kn = [Res(), Res()]
                    r_vT, r_vst = Res(), Res()
                    kn_i = [0]
                    pending = []
                    main_nb[0] = 4

                    def stage2(kind, h, banks, rw=0):
                        if kind in ("q", "k"):
                            gcol = P_QG if kind == "q" else P_KG
                            b = kn_i[0] % 2
                            kn_i[0] += 1
                            for hf, bk in enumerate(banks):
                                PE.op(lambda hf=hf: nc.tensor.matmul(psb[4 + hf][:, :], lhsT=onesq[:], rhs=sq[hf][:],
                                                                     start=True, stop=True),
                                      reads=[r_sq[hf], r_const], writes=[r_ps[4 + hf]])
                                ACT.op(lambda hf=hf: nc.scalar.activation(out=sd[hf][:], in_=psb[4 + hf][:, :],
                                                                          func=AF.Sqrt, bias=NORM_EPS),
                                       writes=[r_ps[4 + hf], r_sd[hf]])
                                DVE.op(lambda hf=hf: nc.vector.reciprocal(out=rs[hf][:], in_=sd[hf][:]),
                                       reads=[r_sd[hf]], writes=[r_rs2[hf]])
                                DVE.op(lambda hf=hf, rw=rw, b=b, gcol=gcol: nc.vector.scalar_tensor_tensor(
                                    out=kn[b][:, 512 * hf:512 * hf + 512], in0=raw[rw][hf][:],
                                    scalar=prm[:, gcol:gcol + 1], in1=rs[hf][:], op0=ALU.mult, op1=ALU.mult),
                                    reads=[r_rs2[hf], r_prm, r_raw[rw][hf]], writes=[r_kn[b]])
                            if kind == "k":
                                d_s.dma(SP, kT_win[h, :, w0:w0 + TOK], kn[b][:], reads=[r_kn[b]])
                            else:
                                d_s.dma(SP, qT_d[h, :, :], kn[b][:], reads=[r_kn[b]])
                        elif kind == "v":
                            pt = psb[6][:].bitcast(BF16)

                            def tr(pt=pt):
                                ins = None
                                for t in range(8):
                                    ins = nc.tensor.transpose(pt[:, t * 128:(t + 1) * 128], vT[:, t * 128:(t + 1) * 128],
                                                              ident[:])
                                return ins
                            PE.op(tr, reads=[r_vT, r_const], writes=[r_ps[6]])
                            hh = h % 4
                            DVE.op(lambda pt=pt, hh=hh: nc.vector.tensor_copy(
                                out=vst[:, :, hh * 128:(hh + 1) * 128],
                                in_=pt[:, 0:1024].rearrange("p (t d) -> p t d", d=128)),
                                writes=[r_ps[6], r_vst])
                            if hh == 3:
                                g = h // 4
                                d_s.dma(SP, v_win[w0:w0 + TOK, g * 512:(g + 1) * 512].rearrange("(t p) n -> p t n", p=128),
                                        vst[:], reads=[r_vst])

                    for i in range(first_rest, len(cols)):
                        kind, h, j = cols[i]
                        load_w(i + 3)
                        banks = main_mm(i)
                        for fn in pending:
                            fn()
                        pending = []
                        rw = raw_i[0] % 2
                        if kind in ("q", "k"):
                            raw_i[0] += 1
                            for hf, bk in enumerate(banks):
                                ACT.op(lambda bk=bk, hf=hf: nc.scalar.activation(out=sq[hf][:], in_=psb[bk][:, :],
                                                                                 func=AF.Square),
                                       writes=[r_ps[bk], r_sq[hf]])
                                ACT.op(lambda bk=bk, hf=hf, rw=rw: nc.scalar.activation(out=raw[rw][hf][:], in_=psb[bk][:, :],
                                                                                        func=AF.Copy),
                                       writes=[r_ps[bk], r_raw[rw][hf]])
                        elif kind == "v":
                            for hf, bk in enumerate(banks):
                                ACT.op(lambda bk=bk, hf=hf: nc.scalar.activation(out=vT[:, 512 * hf:512 * hf + 512],
                                                                                 in_=psb[bk][:, :], func=AF.Copy),
                                       writes=[r_ps[bk], r_vT])
                        elif kind == "ag":
                            b = kn_i[0] % 2
                            kn_i[0] += 1
                            for hf, bk in enumerate(banks):
                                ACT.op(lambda bk=bk, hf=hf, b=b: nc.scalar.activation(
                                    out=kn[b][:, 512 * hf:512 * hf + 512], in_=psb[bk][:, :], func=AF.Silu),
                                    writes=[r_ps[bk], r_kn[b]])
                            d_s.dma(SP, gT_d[h, :, :], kn[b][:], reads=[r_kn[b]])
                        if kind in ("q", "k", "v"):
                            pending.append(lambda kind=kind, h=h, banks=banks, rw=rw: stage2(kind, h, banks, rw))
                    for fn in pending:
                        fn()
                    barrier()

        mixA = T(es, "mixA", [128, 16, TOK], BF16)
        r_mixA = Res()
        with ExitStack() as as_:
            EB = T(as_, "EB", [128, 48, 256], BF16)
            msk = T(as_, "msk", [128, 256], F32)
            r_EB, r_msk = Res(), Res()
            d_l.dma(SP, msk[:], maskd[:, :], writes=[r_msk])
            with ExitStack() as bs_:
                bst = T(bs_, "bst", [128, 16, 256], F32)
                r_bst = Res()
                for p in range(3):
                    d_l.dma(SP, bst[:], biasT[:, 16 * p:16 * p + 16, :], writes=[r_bst])
                    ACT.op(lambda: nc.scalar.activation(out=bst[:], in_=bst[:], func=AF.Exp),
                           reads=[r_bst], writes=[r_bst])
                    for h in range(16):
                        DVE.op(lambda p=p, h=h: nc.vector.tensor_tensor(out=EB[:, 16 * p + h, :], in0=bst[:, h, :],
                                                                        in1=msk[:], op=ALU.mult),
                               reads=[r_bst, r_msk], writes=[r_EB])
                barrier()
            DEPTH = 3
            NR = 4
            kTs = [T(as_, f"kTs{i}", [128, WIN], BF16) for i in range(2)]
            qs = [T(as_, f"qs{i}", [128, TOK], BF16) for i in range(2)]
            gs = [T(as_, f"gs{i}", [128, TOK], BF16) for i in range(2)]
            v1 = [T(as_, f"v1{i}", [128, 9, 128], BF16) for i in range(2)]
            v2 = [T(as_, f"v2{i}", [128, 4, 3, 128], BF16) for i in range(2)]
            v3a = [T(as_, f"v3a{i}", [128, 16, 128], BF16) for i in range(2)]
            v3b = [T(as_, f"v3b{i}", [64, 16, 128], BF16) for i in range(2)]
            esb = [T(as_, f"esb{i}", [128, 512], BF16) for i in range(NR)]
            psb_ = [T(as_, f"pp{i}", [128, 512], BF16) for i in range(NR)]
            rz = T(as_, "rz", [128, TOK], F32)
            at = T(as_, "at", [128, TOK], F32)
            r_kTs, r_qs, r_gs2 = [Res(), Res()], [Res(), Res()], [Res(), Res()]
            r_v = [[Res() for _ in range(4)] for _ in range(2)]
            r_esb, r_pp = [Res() for _ in range(NR)], [Res() for _ in range(NR)]
            r_rz, r_at = Res(), Res()
            NB0, ZB0, SB0 = 0, 2, 4
            scale = 1.0 / np.sqrt(128.0)

            def load_head(h):
                if h >= NHEAD:
                    return
                sl = h % 2
                vc = slice(128 * h, 128 * h + 128)
                d_l.dma(SP, kTs[sl][:], kT_win[h, :, :], writes=[r_kTs[sl]])
                d_l.dma(SP, qs[sl][:], qT_d[h, :, :], writes=[r_qs[sl]])
                d_l.dma(SP, v1[sl][:], v_win[960:960 + 1152, vc].rearrange("(c p) n -> p c n", p=128),
                        writes=[r_v[sl][0]])
                d_l.dma(SP, v2[sl][:], v_win[768:2304, vc].rearrange("(c p r) n -> p r c n", p=128, r=4),
                        writes=[r_v[sl][1]])
                d_l.dma(SP, v3a[sl][:], v_win[0:2048, vc].rearrange("(p r) n -> p r n", r=16),
                        writes=[r_v[sl][2]])
                d_l.dma(SP, v3b[sl][:], v_win[2048:3072, vc].rearrange("(p r) n -> p r n", r=16),
                        writes=[r_v[sl][3]])
                d_l.dma(SP, gs[sl][:], gT_d[h, :, :], writes=[r_gs2[sl]])

            unit_defs = [[c] for c in range(9)]
            unit_defs += [[9 + r * 3 + 0 for r in range(4)]]
            unit_defs += [[9 + r * 3 + 1 for r in (0, 1)], [9 + r * 3 + 1 for r in (2, 3)]]
            unit_defs += [[9 + r * 3 + 2 for r in range(4)]]
            unit_defs += [[21 + r for r in range(8)], [21 + r for r in range(8, 16)]]
            unit_defs += [[37 + r for r in range(8)], [37 + r for r in range(8, 16)]]
            NU = len(unit_defs)
            units = [(h, ui) for h in range(NHEAD) for ui in range(NU)]
            started = {}

            def first_touch(key):
                st_ = not started.get(key, False)
                started[key] = True
                return st_

            def stage_ab(u):
                h, ui = units[u]
                members = [blks[bi] for bi in unit_defs[ui]]
                b0 = members[0]
                sl = h % 2
                nk, nq, q0, p = b0["nk"], b0["nq"], b0["q0"], b0["p"]
                R = len(members)
                sb = SB0 + u % NR
                eb = u % NR

                def st():
                    ins = None
                    for i, b in enumerate(members):
                        ins = nc.tensor.matmul(psb[sb][0:nk, i * nq:(i + 1) * nq], lhsT=kTs[sl][:, b["kslice"]],
                                               rhs=qs[sl][:, b["qslice"]], start=True, stop=True,
                                               skip_group_check=True)
                    return ins
                PE.op(st, reads=[r_kTs[sl], r_qs[sl]], writes=[r_ps[sb]])
                ACT.op(lambda: nc.scalar.activation(
                    out=esb[eb][0:nk, 0:R * nq], in_=psb[sb][0:nk, 0:R * nq], func=AF.Exp, scale=float(scale)),
                    writes=[r_ps[sb], r_esb[eb]])
                kvc = P_KV + unit_defs[ui][0]
                ebv = EB[0:nk, 16 * p + h, q0:q0 + nq]
                if R == 1:
                    o_ap, i_ap, e_ap = psb_[eb][0:nk, 0:nq], esb[eb][0:nk, 0:nq], ebv
                else:
                    o_ap = psb_[eb][0:nk, 0:R * nq].rearrange("k (r q) -> k r q", q=nq)
                    i_ap = esb[eb][0:nk, 0:R * nq].rearrange("k (r q) -> k r q", q=nq)
                    e_ap = ebv.unsqueeze(1).broadcast_to([nk, R, nq])
                DVE.op(lambda: nc.vector.scalar_tensor_tensor(
                    out=o_ap, in0=i_ap, scalar=prm[0:nk, kvc:kvc + 1], in1=e_ap,
                    op0=ALU.mult, op1=ALU.mult), reads=[r_esb[eb], r_EB, r_prm], writes=[r_pp[eb]])

            def stage_c(u):
                h, ui = units[u]
                members = [blks[bi] for bi in unit_defs[ui]]
                b0 = members[0]
                sl = h % 2
                nk, nq = b0["nk"], b0["nq"]
                R = len(members)
                eb = u % NR
                step = b0["qslice"].step
                jobs = []
                rvs = []
                for i, b in enumerate(members):
                    vk = b["v"]
                    if vk[0] == "v1":
                        vap, rv = v1[sl][:, vk[1], :], r_v[sl][0]
                    elif vk[0] == "v2":
                        vap, rv = v2[sl][:, vk[1], vk[2], :], r_v[sl][1]
                    elif vk[0] == "v3a":
                        vap, rv = v3a[sl][:, vk[1], :], r_v[sl][2]
                    else:
                        vap, rv = v3b[sl][:, vk[1], :], r_v[sl][3]
                    rvs.append(rv)
                    qsl = b["qslice"]
                    qi = 0
                    while qi < nq:
                        t0 = qsl.start + qsl.step * qi
                        bank = t0 // 512
                        n_in = min(nq - qi, (512 * (bank + 1) - t0 + qsl.step - 1) // qsl.step)
                        if b["p"] == 0:
                            n_in = min(n_in, 128)
                        c0 = t0 - 512 * bank
                        osl = slice(c0, c0 + qsl.step * (n_in - 1) + 1, qsl.step)
                        rhs = psb_[eb][0:nk, i * nq + qi:i * nq + qi + n_in]
                        jobs.append(("N", bank, psb[NB0 + bank][:, osl], vap, rhs))
                        if R == 1:
                            jobs.append(("Z", bank, psb[ZB0 + bank][:, osl], ones1[0:nk, :], rhs))
                        qi += n_in
                if R > 1:
                    tstart = b0["qslice"].start - (b0["qslice"].start % step)
                    r0 = b0["qslice"].start % step
                    qa = 0
                    while qa < nq:
                        bank = (tstart + step * qa) // 512
                        qb = min(nq, (512 * (bank + 1) - tstart) // step)
                        c0 = tstart + step * qa - 512 * bank
                        o_ap = psb[ZB0 + bank][:, :].rearrange("z (m r) -> z r m", r=step)[
                            :, r0:r0 + R, c0 // step:c0 // step + (qb - qa)]
                        rhs = psb_[eb][0:nk, 0:R * nq].rearrange("k (r q) -> k r q", q=nq)[:, :, qa:qb]
                        jobs.append(("Z", bank, o_ap, ones1[0:nk, :], rhs))
                        qa = qb

                def pv():
                    ins = None
                    for (kind, bank, o_ap, lhsT, rhs) in jobs:
                        st_ = first_touch((h, kind, bank))
                        ins = nc.tensor.matmul(o_ap, lhsT=lhsT, rhs=rhs, start=st_, stop=False, skip_group_check=True)
                    return ins
                banks = sorted(set(j[1] for j in jobs))
                PE.op(pv, reads=[r_pp[eb], r_const] + list(set(rvs)),
                      writes=[r_ps[NB0 + bk] for bk in banks] + [r_ps[ZB0 + bk] for bk in banks])
                if ui == NU - 1:
                    for bank in range(2):
                        hs = slice(512 * bank, 512 * bank + 512)
                        DVE.op(lambda bank=bank, hs=hs: nc.vector.reciprocal(out=rz[:, hs], in_=psb[ZB0 + bank][:, :]),
                               writes=[r_ps[ZB0 + bank], r_rz])
                        DVE.op(lambda bank=bank, hs=hs: nc.vector.tensor_tensor(out=at[:, hs], in0=psb[NB0 + bank][:, :],
                                                                                in1=rz[:, hs], op=ALU.mult),
                               reads=[r_rz], writes=[r_ps[NB0 + bank], r_at])
                        DVE.op(lambda hs=hs: nc.vector.tensor_tensor(out=mixA[:, h, hs], in0=at[:, hs],
                                                                     in1=gs[sl][:, hs], op=ALU.mult),
                               reads=[r_at, r_gs2[sl]], writes=[r_mixA])
                    load_head(h + 2)

            load_head(0)
            load_head(1)
            for u in range(len(units) + DEPTH):
                if u < len(units):
                    stage_ab(u)
                if u - DEPTH >= 0:
                    stage_c(u - DEPTH)
            barrier()

        if debug:
            d_s.dma(SP, mix_d[0:16, :, :].rearrange("c p t -> p c t"), mixC[:], reads=[r_mixC])
            d_s.dma(SP, mix_d[16:32, :, :].rearrange("c p t -> p c t"), mixA[:], reads=[r_mixA])

        with ExitStack() as os_:
            wo = T(os_, "wo", [128, 2, NCH, 512], BF16)
            r_wo = [Res(), Res()]
            xp = [T(os_, f"xp{i}", [128, 512], F32) for i in range(4)]
            yst = [T(os_, f"yst{i}", [128, 512], F32) for i in range(4)]
            r_xp = [Res() for _ in range(4)]
            r_y = [Res() for _ in range(4)]

            def load_wo(n):
                if n < 8:
                    for g in range(4):
                        d_w.dma(POOL, wo[:, n % 2, 8 * g:8 * g + 8, :], w_out_v[:, 8 * g:8 * g + 8, 512 * n:512 * n + 512],
                                writes=([r_wo[n % 2]] if g == 0 else []))
                        if g > 0:
                            r_wo[n % 2].extra.append(d_w.last[(d_w.i - 1) % len(d_w.sems)])
            load_wo(0)
            u = 0
            out_evs = []
            for n in range(8):
                load_wo(n + 1)
                for i in range(8):
                    bk = u % 4
                    sl = u % 4
                    u += 1
                    d_x.dma(SP, xp[sl][:], xw[1024 + 128 * i:1024 + 128 * i + 128, 512 * n:512 * n + 512],
                            writes=[r_xp[sl]])

                    def mm(bk=bk, i=i, n=n):
                        ins = None
                        for e in range(NCH):
                            src = mixC if e < 16 else mixA
                            ins = nc.tensor.matmul(psb[bk][:, :], lhsT=src[:, e % 16, 128 * i:128 * i + 128],
                                                   rhs=wo[:, n % 2, e, :], start=(e == 0), stop=(e == NCH - 1))
                        return ins
                    PE.op(mm, reads=[r_wo[n % 2], r_mixC, r_mixA], writes=[r_ps[bk]])
                    DVE.op(lambda bk=bk, sl=sl: nc.vector.tensor_tensor(out=yst[sl][:], in0=psb[bk][:, :], in1=xp[sl][:],
                                                                        op=ALU.add),
                           reads=[r_xp[sl]], writes=[r_ps[bk], r_y[sl]])
                    out_evs.append(d_o.dma(SP, y[128 * i:128 * i + 128, 512 * n:512 * n + 512], yst[sl][:],
                                           reads=[r_y[sl]]))
            for ev in d_o.events():
                SP.wait(ev)
            barrier()
    return nc


def _t5_bucket(rel):
    import math
    half, exact = 16, 8
    rel = np.asarray(rel, dtype=np.int32)
    n = np.abs(rel)
    nf = np.maximum(n, 1).astype(np.float32)
    large = exact + (np.log(nf / np.float32(exact)) / np.float32(math.log(1024 / exact))
                     * np.float32(half - exact)).astype(np.int32)
    large = np.minimum(large, half - 1)
    return np.where(rel > 0, half, 0) + np.where(n < exact, n, large)


_CACHE = {}


def _prep(inputs, debug=False):
    x = np.asarray(inputs["x"], dtype=np.float32)[0]
    f = lambda k: np.asarray(inputs[k], dtype=np.float32)
    xpad = np.zeros((S + 2048, D), np.float32)
    xpad[1024:1024 + S] = x
    w_in = np.ascontiguousarray(f("w_in")[0])
    w_out = np.ascontiguousarray(f("w_out")[0])
    prm = np.zeros((128, NP), np.float32)
    prm[:, P_G:P_G + 32] = f("norm_g")[0].reshape(32, 128).T
    prm[:, P_QG] = f("q_norm_g")[0]
    prm[:, P_KG] = f("k_norm_g")[0]
    cw = f("conv_w")[0]
    prm[:, P_CW:P_CW + 496] = cw.reshape(31, 16, 128).transpose(2, 1, 0).reshape(128, 496)
    prm[:, P_CB:P_CB + 16] = f("conv_b")[0].reshape(16, 128).T
    prm[:, P_LG:P_LG + 16] = f("conv_ln_g")[0].reshape(16, 128).T
    prm[:, P_LB:P_LB + 16] = f("conv_ln_b")[0].reshape(16, 128).T
    rb = f("rel_bias")
    kk = np.arange(128)[:, None]
    qq = np.arange(256)[None, :]
    rel = 64 + kk - qq
    biasT = np.zeros((128, 48, 256), np.float32)
    for p, (_, dil) in enumerate(PATTERNS):
        bucket = _t5_bucket(rel * dil)
        biasT[:, 16 * p:16 * p + 16, :] = rb[bucket].transpose(0, 2, 1)
    mask = ((qq - kk >= 0) & (qq - kk <= 128)).astype(np.float32)
    blks = attn_blocks()
    in_maps = []
    for c in range(NCORE):
        pc = prm.copy()
        for bi, b in enumerate(blks):
            wk = np.array([key_window_index(b, k) if k < b["nk"] else -10 ** 6 for k in range(128)])
            g = 1024 * c - 1024 + wk
            pc[:, P_KV + bi] = ((g >= 0) & (g < S) & (wk >= 0)).astype(np.float32)
        in_maps.append({"xw": np.ascontiguousarray(xpad[1024 * c:1024 * c + WIN]), "w_in": w_in, "w_out": w_out,
                        "params": pc, "biasT": biasT, "mask": mask})
    return in_maps


def kernel(**inputs):
    in_maps = _prep(inputs)
    if "nc" not in _CACHE:
        _CACHE["nc"] = build_program()
    res = run_bass_kernel_spmd(_CACHE["nc"], in_maps, core_ids=list(range(NCORE)))
    out = np.concatenate([np.asarray(r["y"], dtype=np.float32) for r in res.results], axis=0)
    return out[None]
```

```python
import numpy as np
import ml_dtypes
from contextlib import ExitStack

import concourse.bass as bass
import concourse.mybir as mybir
from concourse.bass_utils import run_bass_kernel_spmd

F32 = mybir.dt.float32
BF16 = mybir.dt.bfloat16
AF = mybir.ActivationFunctionType
ALU = mybir.AluOpType

D = 4096
S = 8192
NCORE = 8
TOK = 1024
WIN = 3072
NCH = 32
INW = 14336
NHEAD = 16
NBLK = 53
NP = 640
P_G, P_QG, P_KG, P_CW, P_CB, P_LG, P_LB, P_KV = 0, 32, 33, 34, 530, 546, 562, 578
NORM_EPS = 1e-6
LN_EPS = 1e-5
PATTERNS = ((128, 1), (512, 4), (2048, 16))


class Sem:
    def __init__(s, h):
        s.h = h
        s.v = 0


class Res:
    def __init__(s):
        s.w = None
        s.r = {}
        s.extra = []


class Eng:
    def __init__(s, nc, h, name, es):
        s.h = h
        s.sem = Sem(es.enter_context(nc.semaphore("e_" + name)))
        s.seen = {}

    def wait(s, ev):
        if ev is None:
            return
        sem, v = ev
        if v <= 0 or s.seen.get(sem, 0) >= v:
            return
        s.h.wait_ge(sem.h, v)
        s.seen[sem] = v

    def pre(s, reads=(), writes=(), own_ok=()):
        for r in reads:
            if r in own_ok and r.w is not None and r.w[0] is s.sem:
                continue
            s.wait(r.w)
            for ev in r.extra:
                s.wait(ev)
        for w in writes:
            if w.w is not None and w.w[0] is not s.sem:
                s.wait(w.w)
            for sem, v in list(w.r.items()):
                if sem is not s.sem:
                    s.wait((sem, v))

    def post(s, ins, reads=(), writes=()):
        s.sem.v += 1
        ins.then_inc(s.sem.h, 1)
        ev = (s.sem, s.sem.v)
        for r in reads:
            r.r[ev[0]] = ev[1]
        for w in writes:
            w.w = ev
            w.r = {}
            w.extra = []
        return ev

    def op(s, fn, reads=(), writes=(), own_ok=()):
        s.pre(reads, writes, own_ok)
        return s.post(fn(), reads, writes)


class DmaPool:
    def __init__(s, nc, es, name, k):
        s.sems = [Sem(es.enter_context(nc.semaphore(f"d_{name}{i}"))) for i in range(k)]
        s.last = [None] * k
        s.i = 0

    def dma(s, q, out, in_, reads=(), writes=()):
        k = s.i % len(s.sems)
        s.i += 1
        sem = s.sems[k]
        q.wait(s.last[k])
        q.pre(reads, writes)
        ins = q.h.dma_start(out=out, in_=in_)
        sem.v += 16
        ins.then_inc(sem.h, 16)
        ev = (sem, sem.v)
        s.last[k] = ev
        for r in reads:
            r.r[sem] = sem.v
        for w in writes:
            w.w = ev
            w.r = {}
            w.extra = []
        return ev

    def events(s):
        return [e for e in s.last if e is not None]


def attn_blocks():
    blks = []
    for c in range(9):
        tlo, thi = max(0, 128 * (c - 1)), min(TOK, 128 * (c + 1))
        blks.append(dict(p=0, nk=128, kslice=slice(960 + 128 * c, 960 + 128 * c + 128, 1),
                         qslice=slice(tlo, thi, 1), q0=tlo - 128 * (c - 1), nq=thi - tlo,
                         v=("v1", c)))
    for r in range(4):
        for c in range(3):
            llo, lhi = max(256, 128 + 128 * c), min(512, 384 + 128 * c)
            k0 = 4 * (192 + 128 * c) + r
            blks.append(dict(p=1, nk=128, kslice=slice(k0, k0 + 512, 4),
                             qslice=slice(4 * (llo - 256) + r, 4 * (lhi - 256), 4),
                             q0=llo - (128 + 128 * c), nq=lhi - llo, v=("v2", r, c)))
    for r in range(16):
        blks.append(dict(p=2, nk=128, kslice=slice(r, 2048, 16), qslice=slice(r, TOK, 16),
                         q0=128, nq=64, v=("v3a", r)))
    for r in range(16):
        blks.append(dict(p=2, nk=64, kslice=slice(2048 + r, WIN, 16), qslice=slice(r, TOK, 16),
                         q0=0, nq=64, v=("v3b", r)))
    assert len(blks) == NBLK
    return blks


def key_window_index(b, kk):
    s = b["kslice"]
    return s.start + s.step * kk


def build_program(debug=False):
    nc = bass.Bass("TRN2", target_bir_lowering=False)
    dkind = "ExternalOutput" if debug else "Internal"
    xw = nc.dram_tensor("xw", [WIN, D], F32, kind="ExternalInput").ap()
    w_in = nc.dram_tensor("w_in", [D, INW], F32, kind="ExternalInput").ap()
    w_out = nc.dram_tensor("w_out", [D, D], F32, kind="ExternalInput").ap()
    params = nc.dram_tensor("params", [128, NP], F32, kind="ExternalInput").ap()
    biasT = nc.dram_tensor("biasT", [128, 48, 256], F32, kind="ExternalInput").ap()
    maskd = nc.dram_tensor("mask", [128, 256], F32, kind="ExternalInput").ap()
    y = nc.dram_tensor("y", [TOK, D], F32, kind="ExternalOutput").ap()
    kT_win = nc.dram_tensor("kT_win", [NHEAD, 128, WIN], BF16, kind=dkind).ap()
    v_win = nc.dram_tensor("v_win", [WIN, 2048], BF16, kind=dkind).ap()
    qT_d = nc.dram_tensor("qT_d", [NHEAD, 128, TOK], BF16, kind=dkind).ap()
    gT_d = nc.dram_tensor("gT_d", [NHEAD, 128, TOK], BF16, kind=dkind).ap()
    if debug:
        mix_d = nc.dram_tensor("mix_d", [32, 128, TOK], BF16, kind="ExternalOutput").ap()

    w_in_v = w_in.rearrange("(c p) n -> p c n", p=128)
    w_out_v = w_out.rearrange("(c p) n -> p c n", p=128)
    blks = attn_blocks()

    with ExitStack() as es:
        PE = Eng(nc, nc.tensor, "pe", es)
        ACT = Eng(nc, nc.scalar, "act", es)
        DVE = Eng(nc, nc.vector, "dve", es)
        POOL = Eng(nc, nc.gpsimd, "pool", es)
        SP = Eng(nc, nc.sync, "sp", es)
        engines = [PE, ACT, DVE, POOL, SP]
        d_w = DmaPool(nc, es, "w", 16)
        d_x = DmaPool(nc, es, "x", 4)
        d_s = DmaPool(nc, es, "s", 8)
        d_l = DmaPool(nc, es, "l", 8)
        d_o = DmaPool(nc, es, "o", 8)
        pools = [d_w, d_x, d_s, d_l, d_o]

        def barrier():
            evs = [(e.sem, e.sem.v) for e in engines]
            for p in pools:
                evs += p.events()
            for e in engines:
                for ev in evs:
                    if ev[0] is not e.sem:
                        e.wait(ev)

        uid = [0]

        def T(stack, name, shape, dt):
            uid[0] += 1
            return stack.enter_context(nc.sbuf_tensor(f"{name}_{uid[0]}", shape, dt))

        prm = T(es, "prm", [128, NP], F32)
        ident = T(es, "ident", [128, 128], BF16)
        ones1 = T(es, "ones1", [128, 128], BF16)
        onesq = T(es, "onesq", [128, 128], BF16)
        onesl = T(es, "onesl", [128, 128], BF16)
        mixC = T(es, "mixC", [128, 16, TOK], BF16)
        r_prm, r_const, r_mixC = Res(), Res(), Res()
        psb = [es.enter_context(nc.psum_tensor(f"psb{i}", [128, 512], F32)) for i in range(8)]
        r_ps = [Res() for _ in range(8)]

        d_s.dma(SP, prm[:], params[:, :], writes=[r_prm])
        POOL.op(lambda: nc.gpsimd.memset(ident[:], 0.0), writes=[r_const])
        POOL.op(lambda: nc.gpsimd.affine_select(out=ident[:], in_=ident[:], pattern=[[-1, 128]],
                                                compare_op=ALU.not_equal, fill=1.0, base=0,
                                                channel_multiplier=1), reads=[r_const], writes=[r_const])
        POOL.op(lambda: nc.gpsimd.memset(ones1[:], 1.0), writes=[r_const])
        POOL.op(lambda: nc.gpsimd.memset(onesq[:], 1.0 / 128), writes=[r_const])
        POOL.op(lambda: nc.gpsimd.memset(onesl[:], 1.0 / 2048), writes=[r_const])

        passes = [("L", 0), ("O", 1024), ("R", 2048)]
        for pname, w0 in passes:
            own = pname == "O"
            with ExitStack() as ps_:
                xnT = T(ps_, "xnT", [128, NCH, 1056], BF16)
                r_xnA, r_xnD = Res(), Res()
                wr = T(ps_, "wr", [128, 4, NCH, 128], BF16)
                r_w = [Res() for _ in range(4)]

                if own:
                    cols = []
                    for cb in range(16):
                        cols += [("glu", cb, 16 + cb), ("val", cb, cb), ("ag", cb, 96 + cb)]
                    cols += [("gate", cb, 32 + cb) for cb in range(16)]
                    cols += [("v", h, 80 + h) for h in range(16)]
                    cols += [("q", h, 48 + h) for h in range(16)]
                    cols += [("k", h, 64 + h) for h in range(16)]
                else:
                    cols = [("v", h, 80 + h) for h in range(16)] + [("k", h, 64 + h) for h in range(16)]

                def load_w(i):
                    if i < len(cols):
                        j = cols[i][2]
                        for g in range(4):
                            d_w.dma(POOL, wr[:, i % 4, 8 * g:8 * g + 8, :], w_in_v[:, 8 * g:8 * g + 8, 128 * j:128 * j + 128],
                                    writes=([r_w[i % 4]] if g == 0 else []))
                            if g > 0:
                                r_w[i % 4].extra.append(d_w.last[(d_w.i - 1) % len(d_w.sems)])

                for i in range(3):
                    load_w(i)

                with ExitStack() as ns_:
                    xs = [T(ns_, f"xs{i}", [128, D], F32) for i in range(2)]
                    xb = [T(ns_, f"xb{i}", [128, D], BF16) for i in range(2)]
                    st = [T(ns_, f"st{i}", [128, 4], F32) for i in range(2)]
                    r_xs = [Res(), Res()]
                    r_xb = [Res(), Res()]
                    r_st = [Res(), Res()]
                    r_junk = Res()
                    tiles = [(w0 + 128 * i, 128, 128 * i) for i in range(8)]
                    if own:
                        tiles.append((None, 32, 1024))
                    bank_i = 0
                    for ti, (row, npart, col0) in enumerate(tiles):
                        sl = ti % 2
                        if row is not None:
                            d_x.dma(SP, xs[sl][:], xw[row:row + 128, :], writes=[r_xs[sl]])
                        else:
                            d_x.dma(SP, xs[sl][0:16, :], xw[1008:1024, :], writes=[r_xs[sl]])
                            d_x.dma(SP, xs[sl][16:32, :], xw[2048:2064, :], writes=[])
                            ev2 = d_x.last[(d_x.i - 1) % 4]
                            ACT.wait(ev2)
                            DVE.wait(ev2)
                        ACT.op(lambda sl=sl, n=npart: nc.scalar.activation(
                            out=xb[sl][0:n, :], in_=xs[sl][0:n, :], func=AF.Square,
                            accum_out=st[sl][0:n, 0:1]), reads=[r_xs[sl]], writes=[r_xb[sl], r_st[sl]])
                        ACT.op(lambda sl=sl, n=npart: nc.scalar.activation(
                            out=st[sl][0:n, 1:2], in_=st[sl][0:n, 0:1], func=AF.Sqrt,
                            scale=1.0 / D, bias=NORM_EPS), reads=[r_st[sl]], writes=[r_st[sl]])
                        DVE.op(lambda sl=sl, n=npart: nc.vector.reciprocal(
                            out=st[sl][0:n, 2:3], in_=st[sl][0:n, 1:2]), reads=[r_st[sl]], writes=[r_st[sl]])
                        DVE.op(lambda sl=sl, n=npart: nc.vector.tensor_scalar(
                            out=xb[sl][0:n, :], in0=xs[sl][0:n, :], scalar1=st[sl][0:n, 2:3],
                            scalar2=None, op0=ALU.mult), reads=[r_xs[sl], r_st[sl]], writes=[r_xb[sl]])
                        for grp in range(4):
                            bk = bank_i % 4
                            bank_i += 1
                            pt = psb[bk][:].bitcast(BF16)

                            def tr(sl=sl, n=npart, grp=grp, pt=pt):
                                ins = None
                                for k in range(8):
                                    c = grp * 8 + k
                                    ins = nc.tensor.transpose(pt[:, k * 128:k * 128 + n],
                                                              xb[sl][0:n, c * 128:(c + 1) * 128],
                                                              ident[0:n, 0:n])
                                return ins
                            PE.op(tr, reads=[r_xb[sl], r_const], writes=[r_ps[bk]])
                            useA = (grp % 2 == 0)
                            E = ACT if useA else DVE
                            rx = r_xnA if useA else r_xnD
                            for k in range(8):
                                c = grp * 8 + k
                                if useA:
                                    fn = (lambda c=c, k=k, n=npart, pt=pt, col0=col0: nc.scalar.mul(
                                        out=xnT[:, c, col0:col0 + n], in_=pt[:, k * 128:k * 128 + n],
                                        mul=prm[:, P_G + c:P_G + c + 1]))
                                else:
                                    fn = (lambda c=c, k=k, n=npart, pt=pt, col0=col0: nc.vector.tensor_scalar(
                                        out=xnT[:, c, col0:col0 + n], in0=pt[:, k * 128:k * 128 + n],
                                        scalar1=prm[:, P_G + c:P_G + c + 1], scalar2=None, op0=ALU.mult))
                                E.op(fn, reads=[r_prm], writes=[r_ps[bk], rx])
                    barrier()

                ntok_groups = [(0, 512), (512, 1024)]
                main_i = [0]
                main_nb = [3]

                def main_mm(i, halo_cols=None):
                    slot = i % 4
                    out = []
                    for (c0, c1) in ntok_groups:
                        bk = main_i[0] % main_nb[0]
                        main_i[0] += 1

                        def mm(bk=bk, c0=c0, c1=c1, slot=slot):
                            ins = None
                            for c in range(NCH):
                                ins = nc.tensor.matmul(psb[bk][:, 0:c1 - c0], lhsT=wr[:, slot, c, :],
                                                       rhs=xnT[:, c, c0:c1], start=(c == 0), stop=(c == NCH - 1))
                            return ins
                        PE.op(mm, reads=[r_w[slot], r_xnA, r_xnD], writes=[r_ps[bk]])
                        out.append(bk)
                    if halo_cols is not None:
                        def mmh(slot=slot, hc=halo_cols):
                            ins = None
                            for c in range(NCH):
                                ins = nc.tensor.matmul(psb[3][:, hc:hc + 32], lhsT=wr[:, slot, c, :],
                                                       rhs=xnT[:, c, 1024:1056], start=(c == 0), stop=(c == NCH - 1))
                            return ins
                        PE.op(mmh, reads=[r_w[slot], r_xnA, r_xnD], writes=[r_ps[3]])
                    return out

                if own:
                    r_mixCb = [Res() for _ in range(16)]
                    with ExitStack() as cs_:
                        sig = T(cs_, "sig", [128, 1056], F32)
                        valsb = T(cs_, "valsb", [128, 1056], F32)
                        aext = [T(cs_, f"aext{i}", [128, 1056], F32) for i in range(2)]
                        accA = T(cs_, "accA", [128, TOK], F32)
                        accB = T(cs_, "accB", [128, TOK], F32)
                        accC = [T(cs_, f"accC{i}", [128, TOK], F32) for i in range(2)]
                        tmpP = T(cs_, "tmpP", [128, TOK], F32)
                        csq = [T(cs_, f"csq{i}", [128, TOK], BF16) for i in range(3)]
                        r_sig, r_valsb, r_accA, r_accB, r_tmpP = Res(), Res(), Res(), Res(), Res()
                        r_aext = [Res(), Res()]
                        r_accC = [Res(), Res()]
                        r_csq = [Res(), Res(), Res()]
                        NDV = 31
                        pend_stats = []

                        def emit_stats(cb):
                            ACT.op(lambda cb=cb: nc.scalar.activation(out=csq[cb % 3][:], in_=mixC[:, cb, :],
                                                                      func=AF.Square),
                                   reads=[r_mixCb[cb]], writes=[r_csq[cb % 3]])
                            for hf in range(2):
                                def stat(hf=hf, cb=cb):
                                    nc.tensor.matmul(psb[4 + hf][:, :], lhsT=onesl[:], rhs=mixC[:, cb, 512 * hf:512 * hf + 512],
                                                     start=(cb == 0), stop=(cb == 15), skip_group_check=True)
                                    return nc.tensor.matmul(psb[6 + hf][:, :], lhsT=onesl[:], rhs=csq[cb % 3][:, 512 * hf:512 * hf + 512],
                                                            start=(cb == 0), stop=(cb == 15), skip_group_check=True)
                                PE.op(stat, reads=[r_mixCb[cb], r_csq[cb % 3], r_const], writes=[r_ps[4 + hf], r_ps[6 + hf]])

                        agb = [T(cs_, f"agb{i}", [128, TOK], BF16) for i in range(2)]
                        r_agb = [Res(), Res()]
                        for i in range(48):
                            kind, cb, j = cols[i]
                            load_w(i + 3)
                            banks = main_mm(i, halo_cols=(0 if kind == "glu" else (32 if kind == "val" else None)))
                            if kind == "ag":
                                ab = cb % 2
                                for hf, bk in enumerate(banks):
                                    ACT.op(lambda bk=bk, hf=hf, ab=ab: nc.scalar.activation(
                                        out=agb[ab][:, 512 * hf:512 * hf + 512], in_=psb[bk][:, :], func=AF.Silu),
                                        writes=[r_ps[bk], r_agb[ab]])
                                d_s.dma(SP, gT_d[cb, :, :], agb[ab][:], reads=[r_agb[ab]])
                            elif kind == "glu":
                                while pend_stats and pend_stats[0] <= cb - 2:
                                    emit_stats(pend_stats.pop(0))
                                for hf, bk in enumerate(banks):
                                    ACT.op(lambda bk=bk, hf=hf: nc.scalar.activation(
                                        out=sig[:, 16 + 512 * hf:16 + 512 * hf + 512], in_=psb[bk][:, :],
                                        func=AF.Sigmoid), writes=[r_ps[bk], r_sig])
                                ACT.op(lambda: nc.scalar.activation(out=sig[:, 0:16], in_=psb[3][:, 0:16],
                                                                    func=AF.Sigmoid), writes=[r_ps[3], r_sig])
                                ACT.op(lambda: nc.scalar.activation(out=sig[:, 1040:1056], in_=psb[3][:, 16:32],
                                                                    func=AF.Sigmoid), writes=[r_ps[3], r_sig])
                            else:
                                ae, rae = aext[cb % 2], r_aext[cb % 2]
                                aC, raC = accC[cb % 2], r_accC[cb % 2]
                                for hf, bk in enumerate(banks):
                                    ACT.op(lambda bk=bk, hf=hf: nc.scalar.activation(
                                        out=valsb[:, 16 + 512 * hf:16 + 512 * hf + 512], in_=psb[bk][:, :], func=AF.Copy),
                                        writes=[r_ps[bk], r_valsb])
                                ACT.op(lambda: nc.scalar.activation(out=valsb[:, 0:16], in_=psb[3][:, 32:48], func=AF.Copy),
                                       writes=[r_ps[3], r_valsb])
                                ACT.op(lambda: nc.scalar.activation(out=valsb[:, 1040:1056], in_=psb[3][:, 48:64], func=AF.Copy),
                                       writes=[r_ps[3], r_valsb])
                                DVE.op(lambda ae=ae: nc.vector.tensor_tensor(out=ae[:], in0=valsb[:], in1=sig[:], op=ALU.mult),
                                       reads=[r_sig, r_valsb], writes=[rae])
                                cwc = P_CW + cb * 31
                                DVE.op(lambda ae=ae, cwc=cwc, cb=cb: nc.vector.tensor_scalar(
                                    out=accA[:], in0=ae[:, 1:1 + TOK], scalar1=prm[:, cwc:cwc + 1],
                                    scalar2=prm[:, P_CB + cb:P_CB + cb + 1], op0=ALU.mult, op1=ALU.add),
                                    reads=[rae, r_prm], writes=[r_accA])
                                DVE.op(lambda ae=ae, cwc=cwc: nc.vector.tensor_scalar(
                                    out=accB[:], in0=ae[:, 2:2 + TOK], scalar1=prm[:, cwc + 1:cwc + 2],
                                    scalar2=None, op0=ALU.mult), reads=[rae, r_prm], writes=[r_accB])
                                for jt in range(2, NDV):
                                    acc, racc = (accA, r_accA) if jt % 2 == 0 else (accB, r_accB)
                                    DVE.op(lambda ae=ae, jt=jt, acc=acc, cwc=cwc: nc.vector.scalar_tensor_tensor(
                                        out=acc[:], in0=ae[:, 1 + jt:1 + jt + TOK],
                                        scalar=prm[:, cwc + jt:cwc + jt + 1], in1=acc[:],
                                        op0=ALU.mult, op1=ALU.add), reads=[rae, racc], writes=[racc],
                                        own_ok=([racc] if jt >= 4 else []))
                                DVE.op(lambda cb=cb: nc.vector.tensor_tensor(
                                    out=mixC[:, cb, :], in0=accA[:], in1=accB[:], op=ALU.add),
                                    reads=[r_accA, r_accB], writes=[r_mixCb[cb]])
                                pend_stats.append(cb)
                        while pend_stats:
                            emit_stats(pend_stats.pop(0))
                        barrier()
                    with ExitStack() as cs_:
                        musb = T(cs_, "musb", [128, TOK], F32)
                        rsb = T(cs_, "rsb", [128, TOK], F32)
                        tmp1 = T(cs_, "tmp1", [128, TOK], F32)
                        tmp2 = T(cs_, "tmp2", [128, TOK], F32)
                        gsb = T(cs_, "gsb", [128, TOK], BF16)
                        r_mu, r_rs, r_t1, r_t2, r_gs = Res(), Res(), Res(), Res(), Res()
                        for hf in range(2):
                            hs = slice(512 * hf, 512 * hf + 512)
                            DVE.op(lambda hf=hf, hs=hs: nc.vector.tensor_copy(out=musb[:, hs], in_=psb[4 + hf][:, :]),
                                   writes=[r_ps[4 + hf], r_mu])
                            DVE.op(lambda hs=hs: nc.vector.tensor_tensor(out=tmp1[:, hs], in0=musb[:, hs], in1=musb[:, hs],
                                                                         op=ALU.mult), reads=[r_mu], writes=[r_t1])
                            DVE.op(lambda hf=hf, hs=hs: nc.vector.tensor_tensor(out=tmp2[:, hs], in0=psb[6 + hf][:, :],
                                                                                in1=tmp1[:, hs], op=ALU.subtract),
                                   reads=[r_t1], writes=[r_ps[6 + hf], r_t2])
                            ACT.op(lambda hs=hs: nc.scalar.activation(out=tmp1[:, hs], in_=tmp2[:, hs], func=AF.Sqrt,
                                                                      bias=LN_EPS), reads=[r_t2], writes=[r_t1])
                            DVE.op(lambda hs=hs: nc.vector.reciprocal(out=rsb[:, hs], in_=tmp1[:, hs]),
                                   reads=[r_t1], writes=[r_rs])
                        for i in range(48, 64):
                            kind, cb, j = cols[i]
                            load_w(i + 3)
                            banks = main_mm(i)
                            for hf, bk in enumerate(banks):
                                ACT.op(lambda bk=bk, hf=hf: nc.scalar.activation(
                                    out=gsb[:, 512 * hf:512 * hf + 512], in_=psb[bk][:, :], func=AF.Silu),
                                    writes=[r_ps[bk], r_gs])
                            DVE.op(lambda cb=cb: nc.vector.tensor_tensor(out=tmp1[:], in0=mixC[:, cb, :], in1=musb[:],
                                                                         op=ALU.subtract),
                                   reads=[r_mixCb[cb], r_mu], writes=[r_t1])
                            DVE.op(lambda: nc.vector.tensor_tensor(out=tmp2[:], in0=tmp1[:], in1=rsb[:], op=ALU.mult),
                                   reads=[r_t1, r_rs], writes=[r_t2])
                            ACT.op(lambda cb=cb: nc.scalar.activation(
                                out=tmp1[:], in_=tmp2[:], func=AF.Silu, scale=prm[:, P_LG + cb:P_LG + cb + 1],
                                bias=prm[:, P_LB + cb:P_LB + cb + 1]), reads=[r_t2, r_prm], writes=[r_t1])
                            DVE.op(lambda cb=cb: nc.vector.tensor_tensor(out=mixC[:, cb, :], in0=tmp1[:], in1=gsb[:],
                                                                         op=ALU.mult),
                                   reads=[r_t1, r_gs], writes=[r_mixCb[cb]])
                        barrier()
                    first_rest = 64
                else:
                    first_rest = 0

                with ExitStack() as qs_:
                    sq = [T(qs_, f"sq{i}", [128, 512], BF16) for i in range(2)]
                    sd = [T(qs_, f"sd{i}", [128, 512], F32) for i in range(2)]
                    rs = [T(qs_, f"rs{i}", [128, 512], F32) for i in range(2)]
                    kn = [T(qs_, f"kn{i}", [128, TOK], BF16) for i in range(2)]
                    raw = [[T(qs_, f"raw{i}{j}", [128, 512], F32) for j in range(2)] for i in range(2)]
                    r_raw = [[Res(), Res()], [Res(), Res()]]
                    raw_i = [0]
                    vT = T(qs_, "vT", [128, TOK], BF16)
                    vst = T(qs_, "vst", [128, 8, 512], BF16)
                    r_sq, r_sd, r_rs2 = [Res(), Res()], [Res(), Res()], [Res(), Res()]
                    r_kn = [Res(), Res()]
                    r_vT, r_vst = Res(), Res()
                    kn_i = [0]
                    pending = []
                    main_nb[0] = 4

                    def stage2(kind, h, banks, rw=0):
                        if kind in ("q", "k"):
                            gcol = P_QG if kind == "q" else P_KG
                            b = kn_i[0] % 2
                            kn_i[0] += 1
                            for hf, bk in enumerate(banks):
                                PE.op(lambda hf=hf: nc.tensor.matmul(psb[4 + hf][:, :], lhsT=onesq[:], rhs=sq[hf][:],
                                                                     start=True, stop=True),
                                      reads=[r_sq[hf], r_const], writes=[r_ps[4 + hf]])
                                ACT.op(lambda hf=hf: nc.scalar.activation(out=sd[hf][:], in_=psb[4 + hf][:, :],
                                                                          func=AF.Sqrt, bias=NORM_EPS),
                                       writes=[r_ps[4 + hf], r_sd[hf]])
                                DVE.op(lambda hf=hf: nc.vector.reciprocal(out=rs[hf][:], in_=sd[hf][:]),
                                       reads=[r_sd[hf]], writes=[r_rs2[hf]])
                                DVE.op(lambda hf=hf, rw=rw, b=b, gcol=gcol: nc.vector.scalar_tensor_tensor(
                                    out=kn[b][:, 512 * hf:512 * hf + 512], in0=raw[rw][hf][:],
                                    scalar=prm[:, gcol:gcol + 1], in1=rs[hf][:], op0=ALU.mult, op1=ALU.mult),
                                    reads=[r_rs2[hf], r_prm, r_raw[rw][hf]], writes=[r_kn[b]])
                            if kind == "k":
                                d_s.dma(SP, kT_win[h, :, w0:w0 + TOK], kn[b][:], reads=[r_kn[b]])
                            else:
                                d_s.dma(SP, qT_d[h, :, :], kn[b][:], reads=[r_kn[b]])
                        elif kind == "v":
                            pt = psb[6][:].bitcast(BF16)

                            def tr(pt=pt):
                                ins = None
                                for t in range(8):
                                    ins = nc.tensor.transpose(pt[:, t * 128:(t + 1) * 128], vT[:, t * 128:(t + 1) * 128],
                                                              ident[:])
                                return ins
                            PE.op(tr, reads=[r_vT, r_const], writes=[r_ps[6]])
                            hh = h % 4
                            DVE.op(lambda pt=pt, hh=hh: nc.vector.tensor_copy(
                                out=vst[:, :, hh * 128:(hh + 1) * 128],
                                in_=pt[:, 0:1024].rearrange("p (t d) -> p t d", d=128)),
                                writes=[r_ps[6], r_vst])
                            if hh == 3:
                                g = h // 4
                                d_s.dma(SP, v_win[w0:w0 + TOK, g * 512:(g + 1) * 512].rearrange("(t p) n -> p t n", p=128),
                                        vst[:], reads=[r_vst])

                    for i in range(first_rest, len(cols)):
                        kind, h, j = cols[i]
                        load_w(i + 3)
                        banks = main_mm(i)
                        for fn in pending:
                            fn()
                        pending = []
                        rw = raw_i[0] % 2
                        if kind in ("q", "k"):
                            raw_i[0] += 1
                            for hf, bk in enumerate(banks):
                                ACT.op(lambda bk=bk, hf=hf: nc.scalar.activation(out=sq[hf][:], in_=psb[bk][:, :],
                                                                                 func=AF.Square),
                                       writes=[r_ps[bk], r_sq[hf]])
                                ACT.op(lambda bk=bk, hf=hf, rw=rw: nc.scalar.activation(out=raw[rw][hf][:], in_=psb[bk][:, :],
                                                                                        func=AF.Copy),
                                       writes=[r_ps[bk], r_raw[rw][hf]])
                        elif kind == "v":
                            for hf, bk in enumerate(banks):
                                ACT.op(lambda bk=bk, hf=hf: nc.scalar.activation(out=vT[:, 512 * hf:512 * hf + 512],
                                                                                 in_=psb[bk][:, :], func=AF.Copy),
                                       writes=[r_ps[bk], r_vT])
                        elif kind == "ag":
                            b = kn_i[0] % 2
                            kn_i[0] += 1
                            for hf, bk in enumerate(banks):
                                ACT.op(lambda bk=bk, hf=hf, b=b: nc.scalar.activation(
                                    out=kn[b][:, 512 * hf:512 * hf + 512], in_=psb[bk][:, :], func=AF.Silu),
                                    writes=[r_ps[bk], r_kn[b]])
                            d_s.dma(SP, gT_d[h, :, :], kn[b][:], reads=[r_kn[b]])
                        if kind in ("q", "k", "v"):
                            pending.append(lambda kind=kind, h=h, banks=banks, rw=rw: stage2(kind, h, banks, rw))
                    for fn in pending:
                        fn()
                    barrier()

        mixA = T(es, "mixA", [128, 16, TOK], BF16)
        r_mixA = Res()
        with ExitStack() as as_:
            EB = T(as_, "EB", [128, 48, 256], BF16)
            msk = T(as_, "msk", [128, 256], F32)
            r_EB, r_msk = Res(), Res()
            d_l.dma(SP, msk[:], maskd[:, :], writes=[r_msk])
            with ExitStack() as bs_:
                bst = T(bs_, "bst", [128, 16, 256], F32)
                r_bst = Res()
                for p in range(3):
                    d_l.dma(SP, bst[:], biasT[:, 16 * p:16 * p + 16, :], writes=[r_bst])
                    ACT.op(lambda: nc.scalar.activation(out=bst[:], in_=bst[:], func=AF.Exp),
                           reads=[r_bst], writes=[r_bst])
                    for h in range(16):
                        DVE.op(lambda p=p, h=h: nc.vector.tensor_tensor(out=EB[:, 16 * p + h, :], in0=bst[:, h, :],
                                                                        in1=msk[:], op=ALU.mult),
                               reads=[r_bst, r_msk], writes=[r_EB])
                barrier()
            DEPTH = 3
            NR = 4
            kTs = [T(as_, f"kTs{i}", [128, WIN], BF16) for i in range(2)]
            qs = [T(as_, f"qs{i}", [128, TOK], BF16) for i in range(2)]
            gs = [T(as_, f"gs{i}", [128, TOK], BF16) for i in range(2)]
            v1 = [T(as_, f"v1{i}", [128, 9, 128], BF16) for i in range(2)]
            v2 = [T(as_, f"v2{i}", [128, 4, 3, 128], BF16) for i in range(2)]
            v3a = [T(as_, f"v3a{i}", [128, 16, 128], BF16) for i in range(2)]
            v3b = [T(as_, f"v3b{i}", [64, 16, 128], BF16) for i in range(2)]
            esb = [T(as_, f"esb{i}", [128, 512], BF16) for i in range(NR)]
            psb_ = [T(as_, f"pp{i}", [128, 512], BF16) for i in range(NR)]
            rz = T(as_, "rz", [128, TOK], F32)
            at = T(as_, "at", [128, TOK], F32)
            r_kTs, r_qs, r_gs2 = [Res(), Res()], [Res(), Res()], [Res(), Res()]
            r_v = [[Res() for _ in range(4)] for _ in range(2)]
            r_esb, r_pp = [Res() for _ in range(NR)], [Res() for _ in range(NR)]
            r_rz, r_at = Res(), Res()
            NB0, ZB0, SB0 = 0, 2, 4
            scale = 1.0 / np.sqrt(128.0)

            def load_head(h):
                if h >= NHEAD:
                    return
                sl = h % 2
                vc = slice(128 * h, 128 * h + 128)
                d_l.dma(SP, kTs[sl][:], kT_win[h, :, :], writes=[r_kTs[sl]])
                d_l.dma(SP, qs[sl][:], qT_d[h, :, :], writes=[r_qs[sl]])
                d_l.dma(SP, v1[sl][:], v_win[960:960 + 1152, vc].rearrange("(c p) n -> p c n", p=128),
                        writes=[r_v[sl][0]])
                d_l.dma(SP, v2[sl][:], v_win[768:2304, vc].rearrange("(c p r) n -> p r c n", p=128, r=4),
                        writes=[r_v[sl][1]])
                d_l.dma(SP, v3a[sl][:], v_win[0:2048, vc].rearrange("(p r) n -> p r n", r=16),
                        writes=[r_v[sl][2]])
                d_l.dma(SP, v3b[sl][:], v_win[2048:3072, vc].rearrange("(p r) n -> p r n", r=16),
                        writes=[r_v[sl][3]])
                d_l.dma(SP, gs[sl][:], gT_d[h, :, :], writes=[r_gs2[sl]])

            unit_defs = [[c] for c in range(9)]
            unit_defs += [[9 + r * 3 + 0 for r in range(4)]]
            unit_defs += [[9 + r * 3 + 1 for r in (0, 1)], [9 + r * 3 + 1 for r in (2, 3)]]
            unit_defs += [[9 + r * 3 + 2 for r in range(4)]]
            unit_defs += [[21 + r for r in range(8)], [21 + r for r in range(8, 16)]]
            unit_defs += [[37 + r for r in range(8)], [37 + r for r in range(8, 16)]]
            NU = len(unit_defs)
            units = [(h, ui) for h in range(NHEAD) for ui in range(NU)]
            started = {}

            def first_touch(key):
                st_ = not started.get(key, False)
                started[key] = True
                return st_

            def stage_ab(u):
                h, ui = units[u]
                members = [blks[bi] for bi in unit_defs[ui]]
                b0 = members[0]
                sl = h % 2
                nk, nq, q0, p = b0["nk"], b0["nq"], b0["q0"], b0["p"]
                R = len(members)
                sb = SB0 + u % NR
                eb = u % NR

                def st():
                    ins = None
                    for i, b in enumerate(members):
                        ins = nc.tensor.matmul(psb[sb][0:nk, i * nq:(i + 1) * nq], lhsT=kTs[sl][:, b["kslice"]],
                                               rhs=qs[sl][:, b["qslice"]], start=True, stop=True,
                                               skip_group_check=True)
                    return ins
                PE.op(st, reads=[r_kTs[sl], r_qs[sl]], writes=[r_ps[sb]])
                ACT.op(lambda: nc.scalar.activation(
                    out=esb[eb][0:nk, 0:R * nq], in_=psb[sb][0:nk, 0:R * nq], func=AF.Exp, scale=float(scale)),
                    writes=[r_ps[sb], r_esb[eb]])
                kvc = P_KV + unit_defs[ui][0]
                ebv = EB[0:nk, 16 * p + h, q0:q0 + nq]
                if R == 1:
                    o_ap, i_ap, e_ap = psb_[eb][0:nk, 0:nq], esb[eb][0:nk, 0:nq], ebv
                else:
                    o_ap = psb_[eb][0:nk, 0:R * nq].rearrange("k (r q) -> k r q", q=nq)
                    i_ap = esb[eb][0:nk, 0:R * nq].rearrange("k (r q) -> k r q", q=nq)
                    e_ap = ebv.unsqueeze(1).broadcast_to([nk, R, nq])
                DVE.op(lambda: nc.vector.scalar_tensor_tensor(
                    out=o_ap, in0=i_ap, scalar=prm[0:nk, kvc:kvc + 1], in1=e_ap,
                    op0=ALU.mult, op1=ALU.mult), reads=[r_esb[eb], r_EB, r_prm], writes=[r_pp[eb]])

            def stage_c(u):
                h, ui = units[u]
                members = [blks[bi] for bi in unit_defs[ui]]
                b0 = members[0]
                sl = h % 2
                nk, nq = b0["nk"], b0["nq"]
                R = len(members)
                eb = u % NR
                step = b0["qslice"].step
                jobs = []
                rvs = []
                for i, b in enumerate(members):
                    vk = b["v"]
                    if vk[0] == "v1":
                        vap, rv = v1[sl][:, vk[1], :], r_v[sl][0]
                    elif vk[0] == "v2":
                        vap, rv = v2[sl][:, vk[1], vk[2], :], r_v[sl][1]
                    elif vk[0] == "v3a":
                        vap, rv = v3a[sl][:, vk[1], :], r_v[sl][2]
                    else:
                        vap, rv = v3b[sl][:, vk[1], :], r_v[sl][3]
                    rvs.append(rv)
                    qsl = b["qslice"]
                    qi = 0
                    while qi < nq:
                        t0 = qsl.start + qsl.step * qi
                        bank = t0 // 512
                        n_in = min(nq - qi, (512 * (bank + 1) - t0 + qsl.step - 1) // qsl.step)
                        if b["p"] == 0:
                            n_in = min(n_in, 128)
                        c0 = t0 - 512 * bank
                        osl = slice(c0, c0 + qsl.step * (n_in - 1) + 1, qsl.step)
                        rhs = psb_[eb][0:nk, i * nq + qi:i * nq + qi + n_in]
                        jobs.append(("N", bank, psb[NB0 + bank][:, osl], vap, rhs))
                        if R == 1:
                            jobs.append(("Z", bank, psb[ZB0 + bank][:, osl], ones1[0:nk, :], rhs))
                        qi += n_in
                if R > 1:
                    tstart = b0["qslice"].start - (b0["qslice"].start % step)
                    r0 = b0["qslice"].start % step
                    qa = 0
                    while qa < nq:
                        bank = (tstart + step * qa) // 512
                        qb = min(nq, (512 * (bank + 1) - tstart) // step)
                        c0 = tstart + step * qa - 512 * bank
                        o_ap = psb[ZB0 + bank][:, :].rearrange("z (m r) -> z r m", r=step)[
                            :, r0:r0 + R, c0 // step:c0 // step + (qb - qa)]
                        rhs = psb_[eb][0:nk, 0:R * nq].rearrange("k (r q) -> k r q", q=nq)[:, :, qa:qb]
                        jobs.append(("Z", bank, o_ap, ones1[0:nk, :], rhs))
                        qa = qb

                def pv():
                    ins = None
                    for (kind, bank, o_ap, lhsT, rhs) in jobs:
                        st_ = first_touch((h, kind, bank))
                        ins = nc.tensor.matmul(o_ap, lhsT=lhsT, rhs=rhs, start=st_, stop=False, skip_group_check=True)
                    return ins
                banks = sorted(set(j[1] for j in jobs))
                PE.op(pv, reads=[r_pp[eb], r_const] + list(set(rvs)),
                      writes=[r_ps[NB0 + bk] for bk in banks] + [r_ps[ZB0 + bk] for bk in banks])
                if ui == NU - 1:
                    for bank in range(2):
                        hs = slice(512 * bank, 512 * bank + 512)
                        DVE.op(lambda bank=bank, hs=hs: nc.vector.reciprocal(out=rz[:, hs], in_=psb[ZB0 + bank][:, :]),
                               writes=[r_ps[ZB0 + bank], r_rz])
                        DVE.op(lambda bank=bank, hs=hs: nc.vector.tensor_tensor(out=at[:, hs], in0=psb[NB0 + bank][:, :],
                                                                                in1=rz[:, hs], op=ALU.mult),
                               reads=[r_rz], writes=[r_ps[NB0 + bank], r_at])
                        DVE.op(lambda hs=hs: nc.vector.tensor_tensor(out=mixA[:, h, hs], in0=at[:, hs],
                                                                     in1=gs[sl][:, hs], op=ALU.mult),
                               reads=[r_at, r_gs2[sl]], writes=[r_mixA])
                    load_head(h + 2)

            load_head(0)
            load_head(1)
            for u in range(len(units) + DEPTH):
                if u < len(units):
                    stage_ab(u)
                if u - DEPTH >= 0:
                    stage_c(u - DEPTH)
            barrier()

        if debug:
            d_s.dma(SP, mix_d[0:16, :, :].rearrange("c p t -> p c t"), mixC[:], reads=[r_mixC])
            d_s.dma(SP, mix_d[16:32, :, :].rearrange("c p t -> p c t"), mixA[:], reads=[r_mixA])

        with ExitStack() as os_:
            wo = T(os_, "wo", [128, 2, NCH, 512], BF16)
            r_wo = [Res(), Res()]
            xp = [T(os_, f"xp{i}", [128, 512], F32) for i in range(4)]
            yst = [T(os_, f"yst{i}", [128, 512], F32) for i in range(4)]
            r_xp = [Res() for _ in range(4)]
            r_y = [Res() for _ in range(4)]

            def load_wo(n):
                if n < 8:
                    for g in range(4):
                        d_w.dma(POOL, wo[:, n % 2, 8 * g:8 * g + 8, :], w_out_v[:, 8 * g:8 * g + 8, 512 * n:512 * n + 512],
                                writes=([r_wo[n % 2]] if g == 0 else []))
                        if g > 0:
                            r_wo[n % 2].extra.append(d_w.last[(d_w.i - 1) % len(d_w.sems)])
            load_wo(0)
            u = 0
            out_evs = []
            for n in range(8):
                load_wo(n + 1)
                for i in range(8):
                    bk = u % 4
                    sl = u % 4
                    u += 1
                    d_x.dma(SP, xp[sl][:], xw[1024 + 128 * i:1024 + 128 * i + 128, 512 * n:512 * n + 512],
                            writes=[r_xp[sl]])

                    def mm(bk=bk, i=i, n=n):
                        ins = None
                        for e in range(NCH):
                            src = mixC if e < 16 else mixA
                            ins = nc.tensor.matmul(psb[bk][:, :], lhsT=src[:, e % 16, 128 * i:128 * i + 128],
                                                   rhs=wo[:, n % 2, e, :], start=(e == 0), stop=(e == NCH - 1))
                        return ins
                    PE.op(mm, reads=[r_wo[n % 2], r_mixC, r_mixA], writes=[r_ps[bk]])
                    DVE.op(lambda bk=bk, sl=sl: nc.vector.tensor_tensor(out=yst[sl][:], in0=psb[bk][:, :], in1=xp[sl][:],
                                                                        op=ALU.add),
                           reads=[r_xp[sl]], writes=[r_ps[bk], r_y[sl]])
                    out_evs.append(d_o.dma(SP, y[128 * i:128 * i + 128, 512 * n:512 * n + 512], yst[sl][:],
                                           reads=[r_y[sl]]))
            for ev in d_o.events():
                SP.wait(ev)
            barrier()
    return nc


def _t5_bucket(rel):
    import math
    half, exact = 16, 8
    rel = np.asarray(rel, dtype=np.int32)
    n = np.abs(rel)
    nf = np.maximum(n, 1).astype(np.float32)
    large = exact + (np.log(nf / np.float32(exact)) / np.float32(math.log(1024 / exact))
                     * np.float32(half - exact)).astype(np.int32)
    large = np.minimum(large, half - 1)
    return np.where(rel > 0, half, 0) + np.where(n < exact, n, large)


_CACHE = {}


def _prep(inputs, debug=False):
    x = np.asarray(inputs["x"], dtype=np.float32)[0]
    f = lambda k: np.asarray(inputs[k], dtype=np.float32)
    xpad = np.zeros((S + 2048, D), np.float32)
    xpad[1024:1024 + S] = x
    w_in = np.ascontiguousarray(f("w_in")[0])
    w_out = np.ascontiguousarray(f("w_out")[0])
    prm = np.zeros((128, NP), np.float32)
    prm[:, P_G:P_G + 32] = f("norm_g")[0].reshape(32, 128).T
    prm[:, P_QG] = f("q_norm_g")[0]
    prm[:, P_KG] = f("k_norm_g")[0]
    cw = f("conv_w")[0]
    prm[:, P_CW:P_CW + 496] = cw.reshape(31, 16, 128).transpose(2, 1, 0).reshape(128, 496)
    prm[:, P_CB:P_CB + 16] = f("conv_b")[0].reshape(16, 128).T
    prm[:, P_LG:P_LG + 16] = f("conv_ln_g")[0].reshape(16, 128).T
    prm[:, P_LB:P_LB + 16] = f("conv_ln_b")[0].reshape(16, 128).T
    rb = f("rel_bias")
    kk = np.arange(128)[:, None]
    qq = np.arange(256)[None, :]
    rel = 64 + kk - qq
    biasT = np.zeros((128, 48, 256), np.float32)
    for p, (_, dil) in enumerate(PATTERNS):
        bucket = _t5_bucket(rel * dil)
        biasT[:, 16 * p:16 * p + 16, :] = rb[bucket].transpose(0, 2, 1)
    mask = ((qq - kk >= 0) & (qq - kk <= 128)).astype(np.float32)
    blks = attn_blocks()
    in_maps = []
    for c in range(NCORE):
        pc = prm.copy()
        for bi, b in enumerate(blks):
            wk = np.array([key_window_index(b, k) if k < b["nk"] else -10 ** 6 for k in range(128)])
            g = 1024 * c - 1024 + wk
            pc[:, P_KV + bi] = ((g >= 0) & (g < S) & (wk >= 0)).astype(np.float32)
        in_maps.append({"xw": np.ascontiguousarray(xpad[1024 * c:1024 * c + WIN]), "w_in": w_in, "w_out": w_out,
                        "params": pc, "biasT": biasT, "mask": mask})
    return in_maps


def kernel(**inputs):
    in_maps = _prep(inputs)
    if "nc" not in _CACHE:
        _CACHE["nc"] = build_program()
    res = run_bass_kernel_spmd(_CACHE["nc"], in_maps, core_ids=list(range(NCORE)))
    out = np.concatenate([np.asarray(r["y"], dtype=np.float32) for r in res.results], axis=0)
    return out[None]
```
